# Optimizing a Trainium2 kernel written in Bass

```python
import jax, jax.numpy as jnp
from jax import lax
import numpy as np

D_MODEL = 1024
BATCH = 8
SEQ = 2048
DEPTH = 4
DEC_BATCH = 32
DEC_SEQ = 16
PAST_LEN = 4096

CHUNK = 64
N_META = 16
D_MIX = D_MODEL
D_RWKV = D_MIX // 2
D_CONV = D_MIX - D_RWKV
HEAD_DIM = 64
N_HEADS = D_RWKV // HEAD_DIM
LORA_W = 64
LORA_A = 64
LORA_G = 160
CONV_WIDTH = 31
D_FF = 4 * D_MODEL
P_RWKV = 3 * D_RWKV + LORA_W + LORA_A + LORA_G
P_IN = P_RWKV + 2 * D_CONV
NORM_EPS = 1e-6
LN_EPS = 1e-5
GN_EPS = 64e-5
DECAY_SCALE = 0.606531

kernel_name = "hymba_rwkv7_conformer_stream_step"


def rms_norm(x, g):
    xf = x.astype(jnp.float32)
    y = xf * lax.rsqrt(jnp.mean(xf * xf, axis=-1, keepdims=True) + NORM_EPS)
    return (y * g.astype(jnp.float32)).astype(x.dtype)


def layer_norm_f32(x, g, b, eps):
    xf = x.astype(jnp.float32)
    mu = jnp.mean(xf, axis=-1, keepdims=True)
    var = jnp.mean(jnp.square(xf - mu), axis=-1, keepdims=True)
    return (xf - mu) * lax.rsqrt(var + eps) * g.astype(jnp.float32) + b.astype(jnp.float32)


def wkv7_scan(S0, r, w, k, v, kk, b):
    def step(S, inp):
        r_t, w_t, k_t, v_t, kk_t, b_t = inp
        sa = jnp.einsum('bhvk,bhk->bhv', S, kk_t)
        S = S * w_t[:, :, None, :] - sa[..., None] * b_t[:, :, None, :] + v_t[..., None] * k_t[:, :, None, :]
        return S, jnp.einsum('bhvk,bhk->bhv', S, r_t)
    xs = (jnp.moveaxis(r, 1, 0), jnp.moveaxis(w, 1, 0), jnp.moveaxis(k, 1, 0),
          jnp.moveaxis(v, 1, 0), jnp.moveaxis(kk, 1, 0), jnp.moveaxis(b, 1, 0))
    S, o = lax.scan(step, S0, xs)
    return S, jnp.moveaxis(o, 0, 1)


def trunk_layer(h, wkv0, shift0, conv0, norm_mix, w_in, mu_shift, w0, lora_w, a0, lora_a, lora_g,
                k_k, k_a, r_k, gn_g, gn_b, conv_w, conv_b, cln_g, cln_b, w_out, norm_ffn, w_up, w_down):
    B, T, _ = h.shape
    xn = rms_norm(h, norm_mix)
    proj = xn @ w_in
    p_r = proj[..., :P_RWKV]
    p_c = proj[..., P_RWKV:]

    prev = jnp.concatenate([shift0[:, None].astype(p_r.dtype), p_r[:, :-1]], axis=1)
    xs = (p_r + (prev - p_r) * mu_shift).astype(jnp.float32)
    o1 = D_RWKV
    o2 = 2 * D_RWKV
    o3 = 3 * D_RWKV
    o4 = o3 + LORA_W
    o5 = o4 + LORA_A
    r = xs[..., :o1]
    k = xs[..., o1:o2]
    v = xs[..., o2:o3]
    xw = xs[..., o3:o4]
    xa = xs[..., o4:o5]
    xg = xs[..., o5:]
    w = jnp.exp(-DECAY_SCALE * jax.nn.sigmoid(w0 + jnp.tanh(xw) @ lora_w))
    a = jax.nn.sigmoid(a0 + xa @ lora_a)
    g = jax.nn.sigmoid(xg) @ lora_g

    def heads(t):
        return t.reshape(B, T, N_HEADS, HEAD_DIM)

    kk = heads(k * k_k)
    kk = kk / jnp.maximum(jnp.sqrt(jnp.sum(kk * kk, axis=-1, keepdims=True)), 1e-12)
    k = k * (1.0 + (a - 1.0) * k_a)
    rh, kh, vh = heads(r), heads(k), heads(v)
    S, o = wkv7_scan(wkv0.astype(jnp.float32), rh, heads(w), kh, vh, kk, kk * heads(a))
    o = layer_norm_f32(o, gn_g.reshape(N_HEADS, HEAD_DIM), gn_b.reshape(N_HEADS, HEAD_DIM), GN_EPS)
    o = o + jnp.sum(rh * kh * r_k.astype(jnp.float32), axis=-1, keepdims=True) * vh
    y_r = o.reshape(B, T, D_RWKV) * g

    u = p_c[..., :D_CONV] * jax.nn.sigmoid(p_c[..., D_CONV:])
    full = jnp.concatenate([conv0.astype(u.dtype), u], axis=1)
    c = lax.conv_general_dilated(full, conv_w[:, None, :].astype(u.dtype), (1,), 'VALID',
                                 dimension_numbers=('NWC', 'WIO', 'NWC'),
                                 feature_group_count=D_CONV) + conv_b
    y_c = jax.nn.silu(layer_norm_f32(c, cln_g, cln_b, LN_EPS))

    mix = jnp.concatenate([y_r, y_c], axis=-1).astype(h.dtype)
    h = h + mix @ w_out
    hn = rms_norm(h, norm_ffn)
    h = h + jnp.square(jax.nn.relu(hn @ w_up)) @ w_down
    return h, S.astype(wkv0.dtype), p_r[:, -1], full[:, -(CONV_WIDTH - 1):]


def setup_inputs(seed: int = 0) -> dict:
    key = jax.random.key(seed)
    ks = jax.random.split(key, 32)
    nrm = jax.random.normal
    f32 = jnp.float32
    return {
        "x_prompt": nrm(ks[0], (BATCH, SEQ, D_MODEL), f32),
        "x_sample": nrm(ks[1], (DEC_BATCH, DEC_SEQ, D_MODEL), f32),
        "state_wkv": 0.3 * nrm(ks[2], (DEPTH, DEC_BATCH, N_HEADS, HEAD_DIM, HEAD_DIM), f32),
        "state_shift": nrm(ks[3], (DEPTH, DEC_BATCH, P_RWKV), f32),
        "cache_conv": 0.5 * nrm(ks[4], (DEPTH, DEC_BATCH, CONV_WIDTH - 1, D_CONV), f32),
        "meta_tokens": nrm(ks[5], (N_META, D_MODEL), f32),
        "norm_mix": 1.0 + 0.02 * nrm(ks[6], (DEPTH, D_MODEL), f32),
        "w_in": nrm(ks[7], (DEPTH, D_MODEL, P_IN), f32) * D_MODEL ** -0.5,
        "mu_shift": jax.random.uniform(ks[8], (DEPTH, P_RWKV), f32),
        "w0": nrm(ks[9], (DEPTH, D_RWKV), f32),
        "lora_w": 0.5 * nrm(ks[10], (DEPTH, LORA_W, D_RWKV), f32) * LORA_W ** -0.5,
        "a0": 0.5 * nrm(ks[11], (DEPTH, D_RWKV), f32),
        "lora_a": 0.5 * nrm(ks[12], (DEPTH, LORA_A, D_RWKV), f32) * LORA_A ** -0.5,
        "lora_g": nrm(ks[13], (DEPTH, LORA_G, D_RWKV), f32) * LORA_G ** -0.5,
        "k_k": 0.85 + 0.05 * nrm(ks[14], (DEPTH, D_RWKV), f32),
        "k_a": 1.0 + 0.05 * nrm(ks[15], (DEPTH, D_RWKV), f32),
        "r_k": 0.1 * nrm(ks[16], (DEPTH, N_HEADS, HEAD_DIM), f32),
        "gn_g": 1.0 + 0.02 * nrm(ks[17], (DEPTH, D_RWKV), f32),
        "gn_b": 0.01 * nrm(ks[18], (DEPTH, D_RWKV), f32),
        "conv_w": nrm(ks[19], (DEPTH, CONV_WIDTH, D_CONV), f32) * CONV_WIDTH ** -0.5,
        "conv_b": 0.01 * nrm(ks[20], (DEPTH, D_CONV), f32),
        "cln_g": 1.0 + 0.02 * nrm(ks[21], (DEPTH, D_CONV), f32),
        "cln_b": 0.01 * nrm(ks[22], (DEPTH, D_CONV), f32),
        "w_out": nrm(ks[23], (DEPTH, D_MIX, D_MODEL), f32) * D_MIX ** -0.5,
        "norm_ffn": 1.0 + 0.02 * nrm(ks[24], (DEPTH, D_MODEL), f32),
        "w_up": nrm(ks[25], (DEPTH, D_MODEL, D_FF), f32) * D_MODEL ** -0.5,
        "w_down": nrm(ks[26], (DEPTH, D_FF, D_MODEL), f32) * D_FF ** -0.5,
        "norm_final": 1.0 + 0.02 * nrm(ks[27], (D_MODEL,), f32),
    }


def reference(x_prompt, x_sample, state_wkv, state_shift, cache_conv, meta_tokens, norm_mix, w_in,
              mu_shift, w0, lora_w, a0, lora_a, lora_g, k_k, k_a, r_k, gn_g, gn_b, conv_w, conv_b,
              cln_g, cln_b, w_out, norm_ffn, w_up, w_down, norm_final):
    bp = x_prompt.shape[0]
    dt = x_prompt.dtype
    hp = jnp.concatenate([jnp.broadcast_to(meta_tokens[None].astype(dt), (bp, N_META, D_MODEL)), x_prompt], axis=1)
    wkv_p0 = jnp.zeros((bp, N_HEADS, HEAD_DIM, HEAD_DIM), dt)
    shift_p0 = jnp.zeros((bp, P_RWKV), dt)
    conv_p0 = jnp.zeros((bp, CONV_WIDTH - 1, D_CONV), dt)
    hs = x_sample
    wkv_p, shift_p, conv_p, wkv_s, shift_s, conv_s = [], [], [], [], [], []
    for l in range(DEPTH):
        params = (norm_mix[l], w_in[l], mu_shift[l], w0[l], lora_w[l], a0[l], lora_a[l], lora_g[l],
                  k_k[l], k_a[l], r_k[l], gn_g[l], gn_b[l], conv_w[l], conv_b[l], cln_g[l], cln_b[l],
                  w_out[l], norm_ffn[l], w_up[l], w_down[l])
        hp, s1, s2, s3 = trunk_layer(hp, wkv_p0, shift_p0, conv_p0, *params)
        wkv_p.append(s1)
        shift_p.append(s2)
        conv_p.append(s3)
        hs, s1, s2, s3 = trunk_layer(hs, state_wkv[l], state_shift[l], cache_conv[l], *params)
        wkv_s.append(s1)
        shift_s.append(s2)
        conv_s.append(s3)
    y_prompt = rms_norm(hp, norm_final)[:, N_META:]
    y_sample = rms_norm(hs, norm_final)
    return (y_prompt, y_sample,
            jnp.stack(wkv_p), jnp.stack(shift_p), jnp.stack(conv_p),
            jnp.stack(wkv_s), jnp.stack(shift_s), jnp.stack(conv_s))
```

```python
import numpy as np
from contextlib import ExitStack
import concourse.bass as bass
import concourse.mybir as mybir
from concourse.bass_utils import run_bass_kernel_spmd

F32 = mybir.dt.float32
BF16 = mybir.dt.bfloat16
AF = mybir.ActivationFunctionType
ALU = mybir.AluOpType
AX = mybir.AxisListType

DEPTH = 4
D = 1024
NPT = 2048
NMETA = 16
NS = 4
TS = 16
TT = NMETA + NPT + NS * TS
PR = 1824
PIN = 2848
DFF = 4096
CW = 31
DECAY = 0.606531
NV = 187
V_NM, V_MU, V_W0, V_A0, V_KK, V_KA, V_RK, V_CW, V_CB, V_CG, V_CBB, V_NF = 0, 8, 23, 27, 31, 35, 39, 43, 167, 171, 175, 179
EPOCH = 30000
NDSEM = 24
import os
STAGE = float(os.environ.get("KSTAGE", "999"))


class Tk:
    __slots__ = ("w", "r", "excl")

    def __init__(self, excl=False):
        self.w = None
        self.r = []
        self.excl = excl


class Sched:
    def __init__(self, nc, es):
        self.nc = nc
        self.es = es
        self.eng = {"pe": nc.tensor, "dve": nc.vector, "act": nc.scalar, "pool": nc.gpsimd, "sp": nc.sync}
        self.sems = {}
        self.cnt = {k: 0 for k in self.eng}
        self.seen = {k: {} for k in self.eng}
        self.dsem = [es.enter_context(nc.semaphore(f"dq{i}")) for i in range(NDSEM)]
        self.dcnt = [0] * NDSEM
        self.dnext = 0
        self.ninst = 0
        self.rec = None

    def _sem(self, e, ep):
        key = (e, ep)
        if key not in self.sems:
            self.sems[key] = self.es.enter_context(self.nc.semaphore(f"s_{e}_{ep}"))
        return self.sems[key]

    def _wait(self, e, tok):
        key, val = tok
        if self.seen[e].get(key, 0) >= val:
            return
        if key[0] == "pe" and e != "pe" and STAGE > 900:
            assert key[1] * EPOCH + val <= self.cnt["pe"], "wait on unsignalled PE op"
        if key[0] == "d":
            sem = self.dsem[key[1]]
        else:
            sem = self._sem(key[0], key[1])
        self.eng[e].wait_ge(sem, val)
        self.seen[e][key] = val

    def _deps(self, e, r, w):
        deps = []
        for t in r:
            if t.w is not None:
                deps.append(t.w)
            if t.excl:
                deps.extend(tok for tok in t.r if tok[0][0] != e)
        for t in w:
            if t.w is not None:
                deps.append(t.w)
            deps.extend(t.r)
        for tok in deps:
            if tok[0][0] == e and e == "pe":
                continue
            self._wait(e, tok)

    def op(self, e, fn, r=(), w=(), sig=True):
        if self.rec is not None:
            self.rec.append((e, fn, list(r), list(w), sig))
            return None
        self._deps(e, r, w)
        inst = fn()
        if sig or e != "pe":
            self.cnt[e] += 1
            c = self.cnt[e]
            ep, v = divmod(c - 1, EPOCH)
            v += 1
            inst.then_inc(self._sem(e, ep), 1)
        else:
            c = self.cnt[e] + 1
            ep, v = divmod(c - 1, EPOCH)
            v += 1
        tok = ((e, ep), v)
        self.seen[e][(e, ep)] = max(self.seen[e].get((e, ep), 0), 0)
        for t in r:
            t.r.append(tok)
        for t in w:
            t.w = tok
            t.r = []
        self.ninst += 1
        return tok

    def dma(self, q, fn, r=(), w=()):
        i = self.dnext
        self.dnext = (self.dnext + 1) % NDSEM
        if self.dcnt[i] > 0:
            self._wait(q, (("d", i), self.dcnt[i]))
        self._deps(q, r, w)
        inst = fn(self.eng[q])
        self.dcnt[i] += 16
        inst.then_inc(self.dsem[i], 16)
        tok = (("d", i), self.dcnt[i])
        for t in r:
            t.r.append(tok)
        for t in w:
            t.w = tok
            t.r = []
        self.ninst += 1
        return tok

    def final_wait(self, toks):
        for tok in toks:
            self._wait("sp", tok)


def build_program():
    nc = bass.Bass("TRN2", target_bir_lowering=False, dynamic_dma_scratch_size=4096)

    def din(name, shape):
        return nc.dram_tensor(name, list(shape), F32, kind="ExternalInput").ap()

    def dout(name, shape):
        return nc.dram_tensor(name, list(shape), F32, kind="ExternalOutput").ap()

    xp = din("xp", [NPT, D])
    meta = din("meta", [NMETA, D])
    xs_in = din("xs", [NS * TS, D])
    swkv = din("swkv", [DEPTH, NS, 8, 64, 64])
    sshift = din("sshift", [DEPTH, NS, PR])
    sconv = din("sconv", [DEPTH, NS, 30, 512])
    vecs = din("vecs", [DEPTH, 128, NV])
    nfin = din("nfin", [128, 8])
    w_in = din("w_in", [DEPTH, D, PIN])
    lora_w = din("lora_w", [DEPTH, 64, 512])
    lora_a = din("lora_a", [DEPTH, 64, 512])
    lora_g = din("lora_g", [DEPTH, 160, 512])
    gn_g = din("gn_g", [DEPTH, 512])
    gn_b = din("gn_b", [DEPTH, 512])
    w_out = din("w_out", [DEPTH, D, D])
    w_up = din("w_up", [DEPTH, D, DFF])
    w_down = din("w_down", [DEPTH, DFF, D])

    yp = dout("yp", [NPT, D])
    ys = dout("ys", [NS * TS, D])
    wkvp = dout("wkvp", [DEPTH, 8, 64, 64])
    shp = dout("shp", [DEPTH, PR])
    cvp = dout("cvp", [DEPTH, 30, 512])
    wkvs = dout("wkvs", [DEPTH, NS, 8, 64, 64])
    shs = dout("shs", [DEPTH, NS, PR])
    cvs = dout("cvs", [DEPTH, NS, 30, 512])

    es = ExitStack()
    with es:
        S = Sched(nc, es)

        def sb(name, shape, dt=F32):
            return es.enter_context(nc.sbuf_tensor(name, list(shape), dt))

        H = sb("H", [128, 8, TT])
        Htk = [Tk() for _ in range(TT // 16)]

        def htks(c0, n):
            return Htk[c0 // 16:(c0 + n + 15) // 16]

        REG_E = 8 * PIN + 8 * D
        REG = sb("REG", [128, REG_E], BF16)
        WIN = REG[:, 0:8 * PIN].rearrange("p (k n) -> p k n", k=8)
        WOUT = REG[:, 8 * PIN:8 * PIN + 8 * D].rearrange("p (k n) -> p k n", k=8)
        HALF = TT // 2
        tWIN, tWOUT, tHN, tA2B = Tk(), Tk(), Tk(), Tk()
        tWU = [Tk(), Tk()]
        tWD = [Tk(), Tk()]

        VEC = sb("VEC", [128, NV]); tVEC = Tk()
        NFIN = sb("NFIN", [128, 8]); tNFIN = Tk()
        LW = sb("LW", [128, 512], BF16); LA = sb("LA", [128, 512], BF16); LG = sb("LG", [128, 2, 512], BF16)
        tLW, tLA, tLG = Tk(), Tk(), Tk()
        GNG = sb("GNG", [128, 512]); GNB = sb("GNB", [128, 512]); tGN = Tk()
        IDF = sb("IDF", [128, 128]); IDB = sb("IDB", [128, 128], BF16)
        ONESC = sb("ONESC", [128, 128], BF16)
        ONF = sb("ONF", [128, 128])
        BONES = sb("BONES", [128, 128], BF16)
        HIND = sb("HIND", [128, 2], BF16)
        MKUR = sb("MKUR", [128, 2, 128], BF16)
        MKSL = sb("MKSL", [128, 128], BF16)
        ONESF = sb("ONESF", [128, 128])
        CEPS = sb("CEPS", [128, 4])
        tC = Tk()
        WORK_E = 28774
        WORK = sb("WORK", [128, WORK_E], BF16)
        wo = [0]

        def wk(name, shape, dt=F32):
            ne = 1
            for d in shape[1:]:
                ne *= d
            nb = ne * (2 if dt == BF16 else 4)
            nb = (nb + 3) // 4 * 4
            a = wo[0]
            wo[0] += nb // 2
            assert wo[0] <= WORK_E, (name, wo[0])
            v = WORK[:, a:a + nb // 2]
            if dt != BF16:
                v = v.bitcast(F32)
            else:
                v = v[:, 0:ne]
            if len(shape) == 2:
                return v
            if len(shape) == 3:
                return v.rearrange("p (a b) -> p a b", a=shape[1])
            return v.rearrange("p (a b c) -> p a b c", a=shape[1], b=shape[2])
        RS = sb("RS", [128, 128]); tRS = Tk()
        tXN = Tk()
        P = wk("P", [128, 15, 129]); tP = Tk()
        XS = sb("XS", [128, 15, 128]); tXS = Tk()
        TWX = wk("TWX", [128, 128], BF16); tTWX = Tk()
        SG = wk("SG", [128, 2, 128], BF16); tSG = Tk()
        A_ = wk("A_", [128, 4, 128]); tA = Tk()
        SW = wk("SW", [128, 4, 128]); tSW = Tk()
        LR = wk("LR", [128, 4, 128]); tLR = Tk()
        EN = wk("EN", [128, 4, 128]); tEN = Tk()
        EP = wk("EP", [128, 4, 128]); tEP = Tk()
        KK = wk("KK", [128, 4, 128]); tKK = Tk()
        T1 = wk("T1", [128, 4, 128]); tT1 = Tk()
        KK2 = wk("KK2", [128, 4, 128], BF16); tKK2 = Tk()
        T2B = wk("T2B", [128, 4, 128], BF16); tT2B = Tk()
        AR = wk("AR", [128, 4, 2, 128], BF16); tAR = Tk()
        ATZ = wk("ATZ", [128, 4, 2, 128], BF16); tATZ = Tk()
        BTZ = wk("BTZ", [128, 4, 2, 128], BF16); tBTZ = Tk()
        KTZ = wk("KTZ", [128, 4, 2, 128], BF16); tKTZ = Tk()
        BTU = wk("BTU", [128, 4, 128], BF16); tBTU = Tk()
        KTU = wk("KTU", [128, 4, 128], BF16); tKTU = Tk()
        VT = wk("VT", [128, 512], BF16); tVT = Tk()
        BTT = wk("BTT", [128, 2, 512], BF16); tBTT = Tk()
        AMA = wk("AMA", [128, 4, 2, 128], BF16); tAMA = tXN; XN = AMA[:, :, :, :].rearrange("p a b c -> p (a b) c")
        AMK = wk("AMK", [128, 4, 2, 128], BF16); tAMK = Tk()
        NQ = [wk(f"NQ{i}", [128, 4, 128], BF16) for i in range(2)]; tNQ = [Tk(), Tk()]
        QT = [wk(f"QT{i}", [128, 4, 128], BF16) for i in range(2)]; tQT = [Tk(), Tk()]
        MM = wk("MM", [128, 4, 128], BF16); tMM = Tk()
        ACC = [wk(f"ACC{i}", [128, 4, 128], BF16) for i in range(2)]; tACC = [Tk(), Tk()]
        XB = wk("XB", [128, 256], BF16); tXB = Tk()
        UB = wk("UB", [128, 256], BF16); tUB = Tk()
        ST = wk("ST", [128, 4, 128]); tST = Tk()
        STB = wk("STB", [128, 4, 128], BF16); tSTB = Tk()
        OTM = EN[:, :, :].rearrange("p a b -> p (a b)"); tOTM = tEN
        OSQ = A_[:, :, :].rearrange("p a b -> p (a b)"); tOSQ = tA
        STAT = wk("STAT", [128, 6, 8]); tSTAT = Tk()
        YRT = KK2[:, :, :].rearrange("p a b -> p (a b)"); tYRT = tKK2
        MIX = sb("MIX", [128, 8, 128], BF16); tMIX = Tk(); SQ_DEF = MIX; tSQ_DEF = tMIX
        SQA = BTT[:, :, :].rearrange("p a (b c) -> p (a b) c", b=4)
        UH = wk("UH", [128, 4, 30 + 128]); tUH = Tk()
        NPE = 14
        DG = sb("DG", [128, 4 * NPE, 128], BF16); tDG = Tk()
        UHB = wk("UHB", [128, 4, 30 + 128], BF16); tUHB = Tk()
        tCVa = [Tk() for _ in range(4)]; tCVb = [Tk() for _ in range(4)]
        TMPH = RS[:, 0:120].rearrange("p (a b) -> p a b", a=4); tTMPH = tRS
        CV = KK; tCV = tKK
        CSQ = T1; tCSQ = tT1
        LNS = LR[:, 0:3, :]; tLNS = tLR
        XSF = XS[:, :, :].rearrange("p a b -> p (a b)"); IOT = XSF[:, 0:1024]; tIOT = tXS
        R32 = XSF[:, 1024:1536]; tR32 = tXS
        PS = [es.enter_context(nc.psum_tensor(f"ps{i}", [128, 512], F32)) for i in range(7)]
        tPS = [Tk(True) for _ in range(7)]
        PS7 = es.enter_context(nc.psum_tensor("ps7", [128, 512], F32)); tPSB = Tk(True)
        PSB = PS7[:, :].bitcast(BF16)

        def b3(ap, a):
            return ap.rearrange("p (a b) -> p a b", a=a)

        def b4(ap, a, b):
            return ap.rearrange("p (a b c) -> p a b c", a=a, b=b)

        V = nc.vector
        ACT = nc.scalar
        PE = nc.tensor
        POOL = nc.gpsimd

        o = 0
        HN = WORK[:, o:o + 8 * TT].rearrange("p (k n) -> p k n", k=8); o += 8 * TT
        WU = [None, None]
        WDf = [None, None]
        WU[0] = WORK[:, o:o + 8 * 512].rearrange("p (k n) -> p k n", k=8); o += 8 * 512
        WDf[0] = [WORK[:, o + fc * 1024:o + (fc + 1) * 1024] for fc in range(4)]; o += 4 * 1024
        DGF = DG[:, :, :].rearrange("p a b -> p (a b)")
        WU[1] = DGF[:, 0:8 * 512].rearrange("p (k n) -> p k n", k=8)
        WDf[1] = [DGF[:, 4096 + fc * 1024:4096 + (fc + 1) * 1024] for fc in range(3)] + [WORK[:, o:o + 1024]]; o += 1024
        assert o <= WORK_E, o
        assert 4096 + 3 * 1024 <= 4 * NPE * 128
        A2B = XS[:, :, :].rearrange("p a b -> p (a b)").bitcast(BF16)[:, 0:2048].rearrange("p (k n) -> p k n", k=4)
        def mk(fn, w):
            S.op("pool", fn, w=w)

        mk(lambda: POOL.memset(IDF[:], 0.0), [tC])
        mk(lambda: POOL.memset(ONESF[:], 1.0), [tC])
        mk(lambda: POOL.affine_select(out=IDF[:], in_=ONESF[:], pattern=[[-1, 128]], compare_op=ALU.is_equal, fill=0.0, base=0, channel_multiplier=1), [tC])
        mk(lambda: POOL.tensor_copy(out=IDB[:], in_=IDF[:]), [tC])
        mk(lambda: POOL.memset(ONESC[:], 1.0 / 1024.0), [tC])
        mk(lambda: POOL.memset(ONF[:], 1.0 / 512.0), [tC])
        mk(lambda: POOL.memset(BONES[:], 0.0), [tC])
        mk(lambda: POOL.memset(BONES[0:64, 0:64], 1.0), [tC])
        mk(lambda: POOL.memset(BONES[64:128, 64:128], 1.0), [tC])
        mk(lambda: POOL.memset(HIND[:], 0.0), [tC])
        mk(lambda: POOL.memset(HIND[0:64, 0:1], 1.0), [tC])
        mk(lambda: POOL.memset(HIND[64:128, 1:2], 1.0), [tC])
        mk(lambda: POOL.affine_select(out=MKUR[:, 0, :], in_=ONESF[:], pattern=[[1, 128]], compare_op=ALU.is_gt, fill=0.0, base=0, channel_multiplier=-1), [tC])
        mk(lambda: POOL.affine_select(out=MKUR[:, 1, :], in_=ONESF[:], pattern=[[1, 128]], compare_op=ALU.is_ge, fill=0.0, base=0, channel_multiplier=-1), [tC])
        mk(lambda: POOL.affine_select(out=MKSL[:], in_=ONESF[:], pattern=[[-1, 128]], compare_op=ALU.is_gt, fill=0.0, base=0, channel_multiplier=1), [tC])
        mk(lambda: POOL.memset(CEPS[:, 0:1], 1e-6), [tC])
        mk(lambda: POOL.memset(CEPS[:, 1:2], 64e-5), [tC])
        mk(lambda: POOL.memset(CEPS[:, 2:3], 1e-5), [tC])
        mk(lambda: POOL.memset(CEPS[:, 3:4], 1e-19), [tC])
        for buf, tk_ in ((ATZ, tATZ), (BTZ, tBTZ), (KTZ, tKTZ), (LW, tLW), (LA, tLA), (LG, tLG),
                         (ST, tST), (UH, tUH)):
            mk((lambda b=buf: POOL.memset(b[:], 0.0)), [tk_])

        LST = [WORK[:, k * 2048:(k + 1) * 2048].bitcast(F32) for k in range(3)]
        tLST = [Tk(), Tk(), Tk()]
        lcount = [0]

        def load_tokens(src_ap, n, c0):
            k_ = lcount[0] % 3
            lcount[0] += 1
            IOT, tIOT = LST[k_], tLST[k_]
            S.dma("sp", lambda e: e.dma_start(out=IOT[0:n, :], in_=src_ap), w=[tIOT])
            for half in range(2):
                for c in range(4):
                    cc = half * 4 + c
                    S.op("pe", lambda cc=cc, c=c: PE.transpose(out=PS[half][:, c * 128:c * 128 + n], in_=IOT[0:n, cc * 128:(cc + 1) * 128], identity=IDF[0:n, 0:n]),
                         r=[tIOT, tC], w=[tPS[half]], sig=(c == 3))
                S.op("act", lambda half=half: ACT.copy(out=H[:, half * 4:half * 4 + 4, c0:c0 + n], in_=b3(PS[half][:, :], 4)[:, :, 0:n]),
                     r=[tPS[half]], w=htks(c0, n))

        RING0 = [(WORK[:, 6144 + k * 1024:6144 + (k + 1) * 1024].bitcast(F32), Tk()) for k in range(4)]
        RINGB = [(GNG, Tk()), (GNB, Tk()), (LG[:, :, :].rearrange("p a b -> p (a b)").bitcast(F32), Tk())]
        RINGA = [(XSF[:, k * 512:(k + 1) * 512], tXS) for k in range(3)]
        ring_pos = [0]

        def stream_cast(ring, dst, src, dst_tk, P0=0, P1=128):
            W = src.shape[-1]
            c = 0
            while c < W:
                w_ = min(512, W - c)
                stg, stk = ring[ring_pos[0] % len(ring)]
                ring_pos[0] += 1
                S.dma("sp", lambda e, stg=stg, c=c, w_=w_: e.dma_start(out=stg[P0:P1, 0:w_], in_=src[:, c:c + w_]), w=[stk])
                S.op("pool", lambda stg=stg, c=c, w_=w_: POOL.tensor_copy(out=dst[:, c:c + w_], in_=stg[P0:P1, 0:w_]), r=[stk], w=[dst_tk])
                c += w_

        cur_ring = [RING0]

        def layer_weight_pieces(l):
            ps_ = []
            for kc in range(8):
                ps_.append(lambda kc=kc: stream_cast(cur_ring[0], WIN[:, kc, :], w_in[l, kc * 128:(kc + 1) * 128, :], tWIN))
            for kc in range(8):
                ps_.append(lambda kc=kc: stream_cast(cur_ring[0], WOUT[:, kc, :], w_out[l, kc * 128:(kc + 1) * 128, :], tWOUT))
            return ps_

        def load_layer_weights(l):
            for p_ in layer_weight_pieces(l):
                p_()

        w0_pieces = layer_weight_pieces(0)
        w0_pieces.reverse()

        def w0_step():
            if w0_pieces:
                w0_pieces.pop()()

        load_tokens(meta[:, :], 16, 0)
        w0_step()
        for i in range(16):
            load_tokens(xp[i * 128:(i + 1) * 128, :], 128, 16 + i * 128)
            w0_step()
        for j in range(NS):
            load_tokens(xs_in[j * 16:(j + 1) * 16, :], 16, NMETA + NPT + j * 16)
        S.dma("sp", lambda e: e.dma_start(out=NFIN[:], in_=nfin[:, :]), w=[tNFIN])

        def rmsnorm_tile(c0, n, gcol_ap, out_fn, out_tk, SQ=None, tSQ=None, RSb=None, tRSb=None, PSn=None, tPSn=None):
            if SQ is None:
                SQ, tSQ = SQ_DEF, tSQ_DEF
            if RSb is None:
                RSb, tRSb = RS, tRS
            if PSn is None:
                PSn, tPSn = PS[6], tPS[6]
            hT = H[:, :, c0:c0 + n]
            S.op("act", lambda: ACT.activation(out=SQ[:, :, 0:n], in_=hT, func=AF.Square), r=htks(c0, n), w=[tSQ])
            for c in range(8):
                S.op("pe", lambda c=c: PE.matmul(PSn[:, 0:n], lhsT=ONESC[:], rhs=SQ[:, c, 0:n], start=(c == 0), stop=(c == 7)),
                     r=[tSQ, tC], w=[tPSn], sig=(c == 7))
            S.op("act", lambda: ACT.activation(out=RSb[:, 0:n], in_=PSn[:, 0:n], func=AF.Ln, bias=CEPS[:, 0:1], scale=1.0),
                 r=[tPSn, tC], w=[tRSb])
            S.op("act", lambda: ACT.activation(out=RSb[:, 0:n], in_=RSb[:, 0:n], func=AF.Exp, scale=-0.5), r=[tRSb], w=[tRSb])
            for c in range(8):
                S.op("dve", lambda c=c: V.scalar_tensor_tensor(out=out_fn(c), in0=H[:, c, c0:c0 + n], scalar=gcol_ap(c), in1=RSb[:, 0:n],
                                                                op0=ALU.mult, op1=ALU.mult),
                     r=htks(c0, n) + [tRSb, tVEC, tNFIN], w=[out_tk])

        def nrounds_for(n):
            r = 0
            while (1 << r) < n:
                r += 1
            return r

        def front(l, c0, n):
            rmsnorm_tile(c0, n, lambda c: VEC[:, V_NM + c:V_NM + c + 1], lambda c: XN[:, c, 0:n], tXN, SQA, tBTT)
            for m in range(15):
                M = 128 if m < 14 else 32
                bank = m // 4
                sl = m % 4
                for kc in range(8):
                    S.op("pe", lambda m=m, M=M, bank=bank, sl=sl, kc=kc: PE.matmul(PS[bank][0:M, sl * 128:sl * 128 + n], lhsT=WIN[:, kc, m * 128:m * 128 + M],
                                                                                 rhs=XN[:, kc, 0:n], start=(kc == 0), stop=(kc == 7)),
                         r=[tWIN, tXN], w=[tPS[bank]], sig=(kc == 7 and (m % 4 == 3 or m == 14)))
            for bank in range(3):
                S.op("act", lambda bank=bank: ACT.copy(out=P[:, bank * 4:bank * 4 + 4, 1:n + 1], in_=b3(PS[bank][:, :], 4)[:, :, 0:n]),
                     r=[tPS[bank]], w=[tP])
            S.op("act", lambda: ACT.copy(out=P[:, 12:14, 1:n + 1], in_=b3(PS[3][:, :], 4)[:, 0:2, 0:n]), r=[tPS[3]], w=[tP])
            S.op("act", lambda: ACT.copy(out=P[0:32, 14, 1:n + 1], in_=PS[3][0:32, 256:256 + n]), r=[tPS[3]], w=[tP])
            for m in range(8):
                bank = 4 if m < 4 else 0
                sl = m % 4
                col = PR + m * 128
                for kc in range(8):
                    S.op("pe", lambda bank=bank, sl=sl, col=col, kc=kc: PE.matmul(PS[bank][:, sl * 128:sl * 128 + n], lhsT=WIN[:, kc, col:col + 128],
                                                                                rhs=XN[:, kc, 0:n], start=(kc == 0), stop=(kc == 7)),
                         r=[tWIN, tXN], w=[tPS[bank]], sig=(kc == 7 and m % 4 == 3))
            S.op("act", lambda: ACT.activation(out=CSQ[:, :, 0:n], in_=b3(PS[0][:, :], 4)[:, :, 0:n], func=AF.Sigmoid), r=[tPS[0]], w=[tCSQ])
            S.op("dve", lambda: V.tensor_tensor(out=UH[:, :, 30:30 + n], in0=CSQ[:, :, 0:n], in1=b3(PS[4][:, :], 4)[:, :, 0:n], op=ALU.mult),
                 r=[tCSQ, tPS[4]], w=[tUH])
            S.op("dve", lambda: V.tensor_tensor(out=XS[:, :, 0:n], in0=P[:, :, 0:n], in1=P[:, :, 1:n + 1], op=ALU.subtract), r=[tP], w=[tXS])
            S.op("dve", lambda: V.tensor_tensor(out=XS[:, :, 0:n], in0=XS[:, :, 0:n], in1=VEC[:, V_MU:V_MU + 15].unsqueeze(2).broadcast_to([128, 15, n]), op=ALU.mult),
                 r=[tXS, tVEC], w=[tXS])
            S.op("dve", lambda: V.tensor_tensor(out=XS[:, :, 0:n], in0=XS[:, :, 0:n], in1=P[:, :, 1:n + 1], op=ALU.add), r=[tXS, tP], w=[tXS])
            S.op("act", lambda: ACT.copy(out=P[:, :, 0:1], in_=P[:, :, n:n + 1]), r=[tP], w=[tP])

        def front_units(l, c0, n):
            XNa = LR[:, :, :].rearrange("p a b -> p (a b)").bitcast(BF16).rearrange("p (a b) -> p a b", a=8)
            SQe = EN[:, :, :].rearrange("p a b -> p (a b)").bitcast(BF16).rearrange("p (a b) -> p a b", a=8)
            units = []
            u0 = []
            hT = H[:, :, c0:c0 + n]
            u0.append(("act", lambda: ACT.activation(out=SQe[:, :, 0:n], in_=hT, func=AF.Square), htks(c0, n), [tEN], True))
            for c in range(8):
                u0.append(("pe", lambda c=c: PE.matmul(PS[6][:, 0:n], lhsT=ONESC[:], rhs=SQe[:, c, 0:n], start=(c == 0), stop=(c == 7)), [tEN, tC], [tPS[6]], c == 7))
            u0.append(("act", lambda: ACT.activation(out=RS[:, 0:n], in_=PS[6][:, 0:n], func=AF.Ln, bias=CEPS[:, 0:1], scale=1.0), [tPS[6], tC], [tRS], True))
            u0.append(("act", lambda: ACT.activation(out=RS[:, 0:n], in_=RS[:, 0:n], func=AF.Exp, scale=-0.5), [tRS], [tRS], True))
            for c in range(8):
                u0.append(("dve", lambda c=c: V.scalar_tensor_tensor(out=XNa[:, c, 0:n], in0=H[:, c, c0:c0 + n], scalar=VEC[:, V_NM + c:V_NM + c + 1], in1=RS[:, 0:n],
                                                                      op0=ALU.mult, op1=ALU.mult), htks(c0, n) + [tRS, tVEC], [tLR], True))
            units.append(u0)
            fills = [([0, 1, 2, 3], 3), ([4, 5, 6, 7], 4), ([8, 9, 10, 11], 3), ([12, 13, 14], 3), ("ca", 4), ("cg", 3)]
            for blocks, bank in fills:
                if blocks == "ca" or blocks == "cg":
                    cols = [(PR + (0 if blocks == "ca" else 512) + q * 128, 128) for q in range(4)]
                else:
                    cols = [(m * 128, 128 if m < 14 else 32) for m in blocks]
                halves = [cols[0:2], cols[2:]]
                for hi, hcols in enumerate(halves):
                    u = []
                    for qi, (col, M) in enumerate(hcols):
                        sl = hi * 2 + qi
                        for kc in range(8):
                            lastmm = (hi == 1 and qi == len(hcols) - 1 and kc == 7)
                            u.append(("pe", lambda col=col, M=M, sl=sl, kc=kc, bank=bank: PE.matmul(PS[bank][0:M, sl * 128:sl * 128 + n], lhsT=WIN[:, kc, col:col + M], rhs=XNa[:, kc, 0:n],
                                                                                                   start=(kc == 0), stop=(kc == 7)), [tWIN, tLR], [tPS[bank]], lastmm))
                    if hi == 1:
                        if blocks == "ca":
                            u.append(("act", lambda bank=bank: ACT.copy(out=A_[:, :, 0:n], in_=b3(PS[bank][:, :], 4)[:, :, 0:n]), [tPS[bank]], [tA], True))
                        elif blocks == "cg":
                            u.append(("act", lambda bank=bank: ACT.activation(out=CSQ[:, :, 0:n], in_=b3(PS[bank][:, :], 4)[:, :, 0:n], func=AF.Sigmoid), [tPS[bank]], [tCSQ], True))
                        elif len(blocks) == 4:
                            b0 = blocks[0]
                            u.append(("act", lambda bank=bank, b0=b0: ACT.copy(out=P[:, b0:b0 + 4, 1:n + 1], in_=b3(PS[bank][:, :], 4)[:, :, 0:n]), [tPS[bank]], [tP], True))
                        else:
                            u.append(("act", lambda bank=bank: ACT.copy(out=P[:, 12:14, 1:n + 1], in_=b3(PS[bank][:, :], 4)[:, 0:2, 0:n]), [tPS[bank]], [tP], True))
                            u.append(("act", lambda bank=bank: ACT.copy(out=P[0:32, 14, 1:n + 1], in_=PS[bank][0:32, 256:256 + n]), [tPS[bank]], [tP], True))
                    units.append(u)
            def shift_ops(b0, b1):
                nb = b1 - b0
                return [
                    ("dve", lambda: V.tensor_tensor(out=XS[:, b0:b1, 0:n], in0=P[:, b0:b1, 0:n], in1=P[:, b0:b1, 1:n + 1], op=ALU.subtract), [tP], [tXS], True),
                    ("dve", lambda: V.tensor_tensor(out=XS[:, b0:b1, 0:n], in0=XS[:, b0:b1, 0:n], in1=VEC[:, V_MU + b0:V_MU + b1].unsqueeze(2).broadcast_to([128, nb, n]), op=ALU.mult),
                     [tXS, tVEC], [tXS], True),
                    ("dve", lambda: V.tensor_tensor(out=XS[:, b0:b1, 0:n], in0=XS[:, b0:b1, 0:n], in1=P[:, b0:b1, 1:n + 1], op=ALU.add), [tXS, tP], [tXS], True),
                ]
            units[7].extend(shift_ops(0, 12))
            tail = []
            tail.append(("dve", lambda: V.tensor_tensor(out=UH[:, :, 30:30 + n], in0=CSQ[:, :, 0:n], in1=A_[:, :, 0:n], op=ALU.mult), [tCSQ, tA], [tUH], True))
            tail.extend(shift_ops(12, 15))
            tail.append(("act", lambda: ACT.copy(out=P[:, :, 0:1], in_=P[:, :, n:n + 1]), [tP], [tP], True))
            return units, tail

        def emit_unit(u):
            for e_, fn_, r_, w_, sig_ in u:
                S.op(e_, fn_, r=r_, w=w_, sig=sig_)

        def mid(l, c0, n, has_state, side=None):
            R = nrounds_for(n)
            rr = XS[:, 0:4, 0:n]
            kx = XS[:, 4:8, 0:n]
            S.op("act", lambda: ACT.activation(out=TWX[0:64, 0:n], in_=XS[0:64, 12, 0:n], func=AF.Tanh), r=[tXS], w=[tTWX])
            S.op("act", lambda: ACT.copy(out=TWX[64:128, 0:n], in_=XS[64:128, 12, 0:n]), r=[tXS], w=[tTWX])
            S.op("act", lambda: ACT.activation(out=SG[:, 0, 0:n], in_=XS[:, 13, 0:n], func=AF.Sigmoid), r=[tXS], w=[tSG])
            S.op("act", lambda: ACT.activation(out=SG[0:32, 1, 0:n], in_=XS[0:32, 14, 0:n], func=AF.Sigmoid), r=[tXS], w=[tSG])
            for j in range(4):
                S.op("pe", lambda j=j: PE.matmul(PS[0][:, j * 128:j * 128 + n], lhsT=LW[:, j * 128:(j + 1) * 128], rhs=TWX[:, 0:n], start=True, stop=True),
                     r=[tLW, tTWX], w=[tPS[0]], sig=(j == 3))
            for j in range(4):
                S.op("pe", lambda j=j: PE.matmul(PS[1][:, j * 128:j * 128 + n], lhsT=LA[:, j * 128:(j + 1) * 128], rhs=TWX[:, 0:n], start=True, stop=True),
                     r=[tLA, tTWX], w=[tPS[1]], sig=(j == 3))
            for j in range(4):
                S.op("act", lambda j=j: ACT.activation(out=SW[:, j, 0:n], in_=PS[0][:, j * 128:j * 128 + n], func=AF.Sigmoid, bias=VEC[:, V_W0 + j:V_W0 + j + 1], scale=1.0),
                     r=[tPS[0], tVEC], w=[tSW])
            for j in range(4):
                S.op("act", lambda j=j: ACT.activation(out=A_[:, j, 0:n], in_=PS[1][:, j * 128:j * 128 + n], func=AF.Sigmoid, bias=VEC[:, V_A0 + j:V_A0 + j + 1], scale=1.0),
                     r=[tPS[1], tVEC], w=[tA])
            for j in range(4):
                S.op("dve", lambda j=j: V.tensor_tensor_scan(out=LR[:, j, 0:n], data0=ONESF[:, 0:n], data1=SW[:, j, 0:n], initial=0.0, op0=ALU.mult, op1=ALU.add),
                     r=[tSW, tC], w=[tLR])
            S.op("act", lambda: ACT.activation(out=EN[:, :, 0:n], in_=LR[:, :, 0:n], func=AF.Exp, scale=DECAY), r=[tLR], w=[tEN])
            S.op("act", lambda: ACT.activation(out=EP[:, :, 0:n], in_=LR[:, :, 0:n], func=AF.Exp, scale=-DECAY), r=[tLR], w=[tEP])
            S.op("dve", lambda: V.tensor_tensor(out=SW[:, :, 0:n], in0=LR[:, :, 0:n], in1=SW[:, :, 0:n], op=ALU.subtract), r=[tLR, tSW], w=[tSW])
            S.op("act", lambda: ACT.activation(out=SW[:, :, 0:n], in_=SW[:, :, 0:n], func=AF.Exp, scale=-DECAY), r=[tSW], w=[tSW])
            S.op("dve", lambda: V.tensor_tensor(out=KK[:, :, 0:n], in0=kx, in1=VEC[:, V_KK:V_KK + 4].unsqueeze(2).broadcast_to([128, 4, n]), op=ALU.mult),
                 r=[tXS, tVEC], w=[tKK])
            S.op("act", lambda: ACT.activation(out=KK2[:, :, 0:n], in_=KK[:, :, 0:n], func=AF.Square), r=[tKK], w=[tKK2])
            for j in range(4):
                S.op("pe", lambda j=j: PE.matmul(PS[3][:, j * 128:j * 128 + n], lhsT=BONES[:], rhs=KK2[:, j, 0:n], start=True, stop=True),
                     r=[tKK2, tC], w=[tPS[3]], sig=(j == 3))
            S.op("act", lambda: ACT.activation(out=T1[:, :, 0:n], in_=b3(PS[3][:, :], 4)[:, :, 0:n], func=AF.Ln, bias=CEPS[:, 3:4], scale=1.0), r=[tPS[3], tC], w=[tT1])
            S.op("act", lambda: ACT.activation(out=T1[:, :, 0:n], in_=T1[:, :, 0:n], func=AF.Exp, scale=-0.5), r=[tT1], w=[tT1])
            S.op("dve", lambda: V.tensor_tensor(out=KK[:, :, 0:n], in0=KK[:, :, 0:n], in1=T1[:, :, 0:n], op=ALU.mult), r=[tKK, tT1], w=[tKK])
            S.op("dve", lambda: V.scalar_tensor_tensor(out=T1[:, :, 0:n], in0=A_[:, :, 0:n], scalar=-1.0, in1=VEC[:, V_KA:V_KA + 4].unsqueeze(2).broadcast_to([128, 4, n]),
                                                       op0=ALU.add, op1=ALU.mult), r=[tA, tVEC], w=[tT1])
            S.op("dve", lambda: V.scalar_tensor_tensor(out=kx, in0=T1[:, :, 0:n], scalar=1.0, in1=kx, op0=ALU.add, op1=ALU.mult), r=[tT1, tXS], w=[tXS])
            S.op("dve", lambda: V.tensor_tensor(out=T1[:, :, 0:n], in0=KK[:, :, 0:n], in1=A_[:, :, 0:n], op=ALU.mult), r=[tKK, tA], w=[tT1])
            S.op("dve", lambda: V.scalar_tensor_tensor(out=AR[:, :, 0, 0:n], in0=KK[:, :, 0:n], scalar=-1.0, in1=SW[:, :, 0:n], op0=ALU.mult, op1=ALU.mult),
                 r=[tKK, tSW], w=[tAR])
            S.op("dve", lambda: V.tensor_tensor(out=BTU[:, :, 0:n], in0=T1[:, :, 0:n], in1=EN[:, :, 0:n], op=ALU.mult), r=[tT1, tEN], w=[tBTU])
            S.op("dve", lambda: V.tensor_tensor(out=KTU[:, :, 0:n], in0=kx, in1=EN[:, :, 0:n], op=ALU.mult), r=[tXS, tEN], w=[tKTU])
            S.op("dve", lambda: V.tensor_tensor(out=AR[:, :, 1, 0:n], in0=rr, in1=EP[:, :, 0:n], op=ALU.mult), r=[tXS, tEP], w=[tAR])
            for i in range(2):
                ps_ = slice(64 * i, 64 * i + 64)
                S.op("dve", lambda i=i, ps_=ps_: V.tensor_copy(out=ATZ[ps_, :, i, 0:n], in_=AR[ps_, :, 0, 0:n]), r=[tAR], w=[tATZ])
                S.op("dve", lambda i=i, ps_=ps_: V.tensor_copy(out=BTZ[ps_, :, i, 0:n], in_=BTU[ps_, :, 0:n]), r=[tBTU], w=[tBTZ])
                S.op("dve", lambda i=i, ps_=ps_: V.tensor_copy(out=KTZ[ps_, :, i, 0:n], in_=KTU[ps_, :, 0:n]), r=[tKTU], w=[tKTZ])
            S.op("dve", lambda: V.tensor_tensor(out=T1[:, :, 0:n], in0=kx, in1=VEC[:, V_RK:V_RK + 4].unsqueeze(2).broadcast_to([128, 4, n]), op=ALU.mult),
                 r=[tXS, tVEC], w=[tT1])
            S.op("dve", lambda: V.tensor_tensor(out=T2B[:, :, 0:n], in0=T1[:, :, 0:n], in1=rr, op=ALU.mult), r=[tT1, tXS], w=[tT2B])
            for j in range(4):
                S.op("pe", lambda j=j: PE.transpose(out=PS[4][0:n, j * 128:(j + 1) * 128], in_=XS[:, 8 + j, 0:n], identity=IDF[:]), r=[tXS, tC], w=[tPS[4]], sig=(j == 3))
            S.op("act", lambda: ACT.copy(out=VT[0:n, :], in_=PS[4][0:n, :]), r=[tPS[4]], w=[tVT])
            for j in range(4):
                S.op("pe", lambda j=j: PE.transpose(out=PSB[0:n, j * 128:(j + 1) * 128], in_=BTU[:, j, 0:n], identity=IDB[:]), r=[tBTU, tC], w=[tPSB], sig=(False))
            for j in range(4):
                S.op("pe", lambda j=j: PE.transpose(out=PSB[0:n, 512 + j * 128:512 + (j + 1) * 128], in_=KTU[:, j, 0:n], identity=IDB[:]), r=[tKTU, tC], w=[tPSB], sig=(j == 3))
            S.op("act", lambda: ACT.copy(out=BTT[0:n, :, :], in_=PSB[0:n, :].rearrange("p (a b) -> p a b", a=2)), r=[tPSB], w=[tBTT])
            CVA = KK
            CVBf = SW
            S.op("act", lambda: ACT.copy(out=UHB[:, :, 0:30 + n], in_=UH[:, :, 0:30 + n]), r=[tUH], w=[tUHB])
            for j in range(4):
                for ti in range(NPE):
                    S.op("pe", lambda j=j, ti=ti: PE.matmul(PS[6][:, j * 128:j * 128 + n], lhsT=DG[:, j * NPE + ti, :], rhs=UHB[:, j, ti:ti + n], start=(ti == 0), stop=(ti == NPE - 1)),
                         r=[tDG, tUHB], w=[tPS[6]], sig=(j == 3 and ti == NPE - 1))
            S.op("act", lambda: ACT.copy(out=CVBf[:, :, 0:n], in_=b3(PS[6][:, :], 4)[:, :, 0:n]), r=[tPS[6]], w=[tSW] + tCVb)
            conv_ops = []
            for k, tap in enumerate(range(NPE, CW)):
                for j in range(4):
                    cwb = V_CW + j * CW
                    if k == 0:
                        conv_ops.append((lambda j=j, cwb=cwb, tap=tap: V.tensor_scalar(out=CVA[:, j, 0:n], in0=UH[:, j, tap:tap + n], scalar1=VEC[:, cwb + tap:cwb + tap + 1],
                                                                                     scalar2=VEC[:, V_CB + j:V_CB + j + 1], op0=ALU.mult, op1=ALU.add),
                                         [tUH, tVEC], ([tKK] if j == 0 else []) + [tCVa[j]]))
                    else:
                        tgt, ttk = (CVA, tCVa) if k % 2 == 0 else (CVBf, tCVb)
                        conv_ops.append((lambda j=j, cwb=cwb, tap=tap, tgt=tgt: V.scalar_tensor_tensor(out=tgt[:, j, 0:n], in0=UH[:, j, tap:tap + n], scalar=VEC[:, cwb + tap:cwb + tap + 1],
                                                                                                      in1=tgt[:, j, 0:n], op0=ALU.mult, op1=ALU.add),
                                         [tUH, tVEC, ttk[j]], [ttk[j]]))
            conv_ops.reverse()

            def drain(k):
                for _ in range(k):
                    if not conv_ops:
                        return
                    fn_, r_, w_ = conv_ops.pop()
                    S.op("dve", fn_, r=r_, w=w_)
            side_units, side_tail = (side if side is not None else ([], []))
            side_units = list(side_units)
            side_units.reverse()
            if side_units:
                emit_unit(side_units.pop())
            for g in range(2):
                heads = [(2 * g + hh // 2, hh % 2) for hh in range(4)]
                for hh, (j, i) in enumerate(heads):
                    bk = hh // 2
                    S.op("pe", lambda hh=hh, j=j, i=i, bk=bk: PE.matmul(b4(PS[0 + bk][:, :], 2, 2)[0:n, hh % 2, :, 0:n], lhsT=BTZ[:, j, i, 0:n], rhs=AR[:, j, :, 0:n], start=True, stop=True),
                         r=[tBTZ, tAR], w=[tPS[0 + bk]], sig=(False))
                    S.op("pe", lambda hh=hh, j=j, i=i, bk=bk: PE.matmul(b4(PS[2 + bk][:, :], 2, 2)[0:n, hh % 2, :, 0:n], lhsT=KTZ[:, j, i, 0:n], rhs=AR[:, j, :, 0:n], start=True, stop=True),
                         r=[tKTZ, tAR], w=[tPS[2 + bk]], sig=(False))
                    S.op("pe", lambda hh=hh, j=j, i=i: PE.matmul(b3(PS[4][:, :], 4)[0:n, hh, 0:n], lhsT=ATZ[:, j, i, 0:n], rhs=BTU[:, j, 0:n], start=True, stop=True),
                         r=[tATZ, tBTU], w=[tPS[4]], sig=(hh == 3))
                mk_ur = MKUR[0:n, :, 0:n].unsqueeze(1).broadcast_to([n, 2, 2, n])
                for bk in range(2):
                    S.op("dve", lambda bk=bk: V.tensor_tensor(out=AMA[0:n, 2 * bk:2 * bk + 2, :, 0:n], in0=b4(PS[0 + bk][:, :], 2, 2)[0:n, :, :, 0:n], in1=mk_ur, op=ALU.mult),
                         r=[tPS[0 + bk], tC], w=[tAMA])
                    S.op("dve", lambda bk=bk: V.tensor_tensor(out=AMK[0:n, 2 * bk:2 * bk + 2, :, 0:n], in0=b4(PS[2 + bk][:, :], 2, 2)[0:n, :, :, 0:n], in1=mk_ur, op=ALU.mult),
                         r=[tPS[2 + bk], tC], w=[tAMK])
                S.op("dve", lambda: V.tensor_tensor(out=NQ[0][0:n, :, 0:n], in0=b3(PS[4][:, :], 4)[0:n, :, 0:n], in1=MKSL[0:n, 0:n].unsqueeze(1).broadcast_to([n, 4, n]), op=ALU.mult),
                     r=[tPS[4], tC], w=[tNQ[0]])
                idb_bc = IDB[0:n, 0:n].unsqueeze(1).broadcast_to([n, 4, n])
                S.op("dve", lambda: V.tensor_tensor(out=ACC[0][0:n, :, 0:n], in0=AMA[0:n, :, 0, 0:n], in1=idb_bc, op=ALU.add), r=[tAMA, tC], w=[tACC[0]])
                S.op("act", lambda: ACT.copy(out=QT[0][0:n, :, 0:n], in_=AMA[0:n, :, 0, 0:n]), r=[tAMA], w=[tQT[0]])
                cur = 0
                ac = 0
                MMb = [MM, KK2]
                tMMb = [tMM, tKK2]

                def emit_acc(rd_, ac_):
                    mmb, tmmb = MMb[rd_ % 2], tMMb[rd_ % 2]
                    for hh in range(4):
                        S.op("pe", lambda hh=hh: PE.matmul(b3(PS[2][:, :], 4)[0:n, hh, 0:n], lhsT=mmb[0:n, hh, 0:n], rhs=ACC[ac_][0:n, hh, 0:n], start=True, stop=True),
                             r=[tmmb, tACC[ac_]], w=[tPS[2]], sig=(hh == 3))
                    S.op("act", lambda: ACT.copy(out=ACC[1 - ac_][0:n, :, 0:n], in_=b3(PS[2][:, :], 4)[0:n, :, 0:n]), r=[tPS[2]], w=[tACC[1 - ac_]])

                for rd in range(1, R):
                    nxt = 1 - cur
                    last = (rd == R - 1)
                    mmb, tmmb = MMb[rd % 2], tMMb[rd % 2]
                    for hh in range(4):
                        S.op("pe", lambda hh=hh, cur=cur: PE.matmul(b3(PS[0][:, :], 4)[0:n, hh, 0:n], lhsT=QT[cur][0:n, hh, 0:n], rhs=NQ[cur][0:n, hh, 0:n], start=True, stop=True),
                             r=[tQT[cur], tNQ[cur]], w=[tPS[0]], sig=(hh == 3 and last))
                    if not last:
                        for hh in range(4):
                            S.op("pe", lambda hh=hh, cur=cur: PE.matmul(b3(PS[1][:, :], 4)[0:n, hh, 0:n], lhsT=NQ[cur][0:n, hh, 0:n], rhs=QT[cur][0:n, hh, 0:n], start=True, stop=True),
                                 r=[tQT[cur], tNQ[cur]], w=[tPS[1]], sig=(hh == 3))
                        S.op("act", lambda nxt=nxt: ACT.copy(out=NQ[nxt][0:n, :, 0:n], in_=b3(PS[0][:, :], 4)[0:n, :, 0:n]), r=[tPS[0]], w=[tNQ[nxt]])
                        S.op("dve", lambda nxt=nxt: V.tensor_copy(out=QT[nxt][0:n, :, 0:n], in_=b3(PS[1][:, :], 4)[0:n, :, 0:n]), r=[tPS[1]], w=[tQT[nxt]])
                        S.op("dve", lambda nxt=nxt, mmb=mmb: V.tensor_tensor(out=mmb[0:n, :, 0:n], in0=NQ[nxt][0:n, :, 0:n], in1=idb_bc, op=ALU.add), r=[tNQ[nxt], tC], w=[tmmb])
                    else:
                        S.op("dve", lambda mmb=mmb: V.tensor_tensor(out=mmb[0:n, :, 0:n], in0=b3(PS[0][:, :], 4)[0:n, :, 0:n], in1=idb_bc, op=ALU.add), r=[tPS[0], tC], w=[tmmb])
                    drain(5)
                    if side_units:
                        emit_unit(side_units.pop())
                    if rd >= 2:
                        emit_acc(rd - 1, ac)
                        ac = 1 - ac
                    cur = nxt
                emit_acc(R - 1, ac)
                ac = 1 - ac
                TTt = ACC[ac]
                tTT = tACC[ac]
                for jj in range(2):
                    j = 2 * g + jj
                    if has_state:
                        S.op("pe", lambda j=j, jj=jj: PE.matmul(PS[3][0:n, jj * 128:(jj + 1) * 128], lhsT=AR[:, j, 0, 0:n], rhs=STB[:, j, :], start=True, stop=False),
                             r=[tAR, tSTB], w=[tPS[3]], sig=(False))
                    for i in range(2):
                        hh = jj * 2 + i
                        hc = (j * 2 + i) * 64
                        S.op("pe", lambda hh=hh, hc=hc: PE.matmul(PS[3][0:n, hh * 64:hh * 64 + 64], lhsT=AMK[0:n, hh, 0, 0:n], rhs=VT[0:n, hc:hc + 64], start=(not has_state), stop=True),
                             r=[tAMK, tVT], w=[tPS[3]], sig=(jj == 1 and i == 1))
                S.op("act", lambda: ACT.copy(out=XB[0:n, :], in_=PS[3][0:n, 0:256]), r=[tPS[3]], w=[tXB])
                drain(3)
                for hh in range(4):
                    S.op("pe", lambda hh=hh: PE.matmul(PS[4][0:n, hh * 64:hh * 64 + 64], lhsT=TTt[0:n, hh, 0:n], rhs=XB[0:n, hh * 64:hh * 64 + 64], start=True, stop=True),
                         r=[tTT, tXB], w=[tPS[4]], sig=(hh == 3))
                S.op("act", lambda: ACT.copy(out=UB[0:n, :], in_=PS[4][0:n, 0:256]), r=[tPS[4]], w=[tUB])
                for jj in range(2):
                    j = 2 * g + jj
                    oc = g * 256 + jj * 128
                    if has_state:
                        S.op("pe", lambda j=j, oc=oc: PE.matmul(PS[5][0:n, oc:oc + 128], lhsT=AR[:, j, 1, 0:n], rhs=STB[:, j, :], start=True, stop=False),
                             r=[tAR, tSTB], w=[tPS[5]], sig=(False))
                    for i in range(2):
                        hh = jj * 2 + i
                        hc = (j * 2 + i) * 64
                        S.op("pe", lambda hh=hh, oc=oc, i=i: PE.matmul(PS[5][0:n, oc + i * 64:oc + i * 64 + 64], lhsT=AMA[0:n, hh, 1, 0:n], rhs=UB[0:n, hh * 64:hh * 64 + 64],
                                                                      start=(not has_state), stop=False), r=[tAMA, tUB], w=[tPS[5]], sig=(False))
                        S.op("pe", lambda hh=hh, oc=oc, i=i, hc=hc: PE.matmul(PS[5][0:n, oc + i * 64:oc + i * 64 + 64], lhsT=AMK[0:n, hh, 1, 0:n], rhs=VT[0:n, hc:hc + 64],
                                                                             start=False, stop=True), r=[tAMK, tVT], w=[tPS[5]], sig=(jj == 1 and i == 1))
                for jj in range(2):
                    j = 2 * g + jj
                    S.op("pe", lambda j=j, jj=jj: PE.matmul(PS[6][:, jj * 128:(jj + 1) * 128], lhsT=BTT[0:n, 0, j * 128:(j + 1) * 128], rhs=UB[0:n, jj * 128:(jj + 1) * 128], start=True, stop=False),
                         r=[tBTT, tUB], w=[tPS[6]], sig=(False))
                    S.op("pe", lambda j=j, jj=jj: PE.matmul(PS[6][:, jj * 128:(jj + 1) * 128], lhsT=BTT[0:n, 1, j * 128:(j + 1) * 128], rhs=VT[0:n, j * 128:(j + 1) * 128], start=False, stop=True),
                         r=[tBTT, tVT], w=[tPS[6]], sig=(jj == 1))
                for i in range(2):
                    ps_ = slice(64 * i, 64 * i + 64)
                    fs_ = slice(64 * i, 64 * i + 64)
                    stv = ST[ps_, 2 * g:2 * g + 2, fs_]
                    psv = b3(PS[6][:, 0:256], 2)[ps_, :, fs_]
                    pcv = EP[ps_, 2 * g:2 * g + 2, n - 1:n].broadcast_to([64, 2, 64])
                    if has_state:
                        S.op("dve", lambda stv=stv, psv=psv: V.tensor_tensor(out=stv, in0=stv, in1=psv, op=ALU.add), r=[tST, tPS[6]], w=[tST])
                        S.op("dve", lambda stv=stv, pcv=pcv: V.tensor_tensor(out=stv, in0=stv, in1=pcv, op=ALU.mult), r=[tST, tEP], w=[tST])
                    else:
                        S.op("dve", lambda stv=stv, psv=psv, pcv=pcv: V.tensor_tensor(out=stv, in0=psv, in1=pcv, op=ALU.mult), r=[tPS[6], tEP], w=[tST])
            S.op("act", lambda: ACT.copy(out=STB[:], in_=ST[:]), r=[tST], w=[tSTB])
            drain(1000)
            S.op("dve", lambda: V.tensor_tensor(out=CV[:, :, 0:n], in0=CVA[:, :, 0:n], in1=CVBf[:, :, 0:n], op=ALU.add), r=tCVa + tCVb, w=[tCV, tSW] + tCVa + tCVb)
            S.op("act", lambda: ACT.copy(out=TMPH[:, :, :], in_=UH[:, :, n:n + 30]), r=[tUH], w=[tTMPH])
            S.op("act", lambda: ACT.copy(out=UH[:, :, 0:30], in_=TMPH[:, :, :]), r=[tTMPH], w=[tUH])
            assert not side_units, len(side_units)
            emit_unit(side_tail)
            for j in range(4):
                S.op("pe", lambda j=j: PE.matmul(PS[6][0:n, 2 * j:2 * j + 2], lhsT=T2B[:, j, 0:n], rhs=HIND[:, :], start=True, stop=True), r=[tT2B, tC], w=[tPS[6]], sig=(j == 3))
            S.op("dve", lambda: V.tensor_copy(out=STAT[0:n, 5, :], in_=PS[6][0:n, 0:8]), r=[tPS[6]], w=[tSTAT])
            S.op("pe", lambda: PE.matmul(PS7[0:n, :], lhsT=SG[:, 0, 0:n], rhs=LG[:, 0, :], start=True, stop=False), r=[tSG, tLG], w=[tPSB], sig=(False))
            S.op("pe", lambda: PE.matmul(PS7[0:n, :], lhsT=SG[:, 1, 0:n], rhs=LG[:, 1, :], start=False, stop=True), r=[tSG, tLG], w=[tPSB], sig=(True))

        def back(l, c0, n):
            S.rec = []
            o3 = b3(PS[5][:, :], 8)[0:n]
            S.op("act", lambda: ACT.activation(out=OSQ[0:n, :], in_=PS[5][0:n, :], func=AF.Square), r=[tPS[5]], w=[tOSQ])
            S.op("dve", lambda: V.tensor_reduce(out=STAT[0:n, 0, :], in_=o3, axis=AX.X, op=ALU.add), r=[tPS[5]], w=[tSTAT])
            S.op("dve", lambda: V.tensor_reduce(out=STAT[0:n, 1, :], in_=b3(OSQ[:, :], 8)[0:n], axis=AX.X, op=ALU.add), r=[tOSQ], w=[tSTAT])
            S.op("dve", lambda: V.tensor_scalar(out=STAT[0:n, 0, :], in0=STAT[0:n, 0, :], scalar1=1.0 / 64.0, scalar2=None, op0=ALU.mult), r=[tSTAT], w=[tSTAT])
            S.op("dve", lambda: V.tensor_tensor(out=STAT[0:n, 2, :], in0=STAT[0:n, 0, :], in1=STAT[0:n, 0, :], op=ALU.mult), r=[tSTAT], w=[tSTAT])
            S.op("dve", lambda: V.scalar_tensor_tensor(out=STAT[0:n, 3, :], in0=STAT[0:n, 1, :], scalar=1.0 / 64.0, in1=STAT[0:n, 2, :], op0=ALU.mult, op1=ALU.subtract),
                 r=[tSTAT], w=[tSTAT])
            S.op("act", lambda: ACT.activation(out=STAT[0:n, 4, :], in_=STAT[0:n, 3, :], func=AF.Ln, bias=CEPS[0:n, 1:2], scale=1.0), r=[tSTAT, tC], w=[tSTAT])
            S.op("act", lambda: ACT.activation(out=STAT[0:n, 4, :], in_=STAT[0:n, 4, :], func=AF.Exp, scale=-0.5), r=[tSTAT], w=[tSTAT])
            ot3 = b3(OTM[:, :], 8)[0:n]
            S.op("dve", lambda: V.tensor_tensor(out=ot3, in0=o3, in1=STAT[0:n, 0, :].unsqueeze(2).broadcast_to([n, 8, 64]), op=ALU.subtract), r=[tPS[5], tSTAT], w=[tOTM])
            S.op("dve", lambda: V.tensor_tensor(out=ot3, in0=ot3, in1=STAT[0:n, 4, :].unsqueeze(2).broadcast_to([n, 8, 64]), op=ALU.mult), r=[tOTM, tSTAT], w=[tOTM])
            S.op("dve", lambda: V.tensor_tensor(out=OTM[0:n, :], in0=OTM[0:n, :], in1=GNG[0:n, :], op=ALU.mult), r=[tOTM, tGN], w=[tOTM])
            S.op("dve", lambda: V.tensor_tensor(out=OTM[0:n, :], in0=OTM[0:n, :], in1=GNB[0:n, :], op=ALU.add), r=[tOTM, tGN], w=[tOTM])
            S.op("dve", lambda: V.tensor_tensor(out=b3(OSQ[:, :], 8)[0:n], in0=b3(VT[:, :], 8)[0:n], in1=STAT[0:n, 5, :].unsqueeze(2).broadcast_to([n, 8, 64]), op=ALU.mult),
                 r=[tVT, tSTAT], w=[tOSQ])
            S.op("dve", lambda: V.tensor_tensor(out=OTM[0:n, :], in0=OTM[0:n, :], in1=OSQ[0:n, :], op=ALU.add), r=[tOTM, tOSQ], w=[tOTM])
            S.op("dve", lambda: V.tensor_tensor(out=YRT[0:n, :], in0=OTM[0:n, :], in1=PS7[0:n, :], op=ALU.mult), r=[tOTM, tPSB], w=[tYRT])
            for j in range(4):
                S.op("pe", lambda j=j: PE.transpose(out=PSB[:, j * 128:j * 128 + n], in_=YRT[0:n, j * 128:(j + 1) * 128], identity=IDB[0:n, 0:n]), r=[tYRT, tC], w=[tPSB], sig=(j == 3))
            S.op("act", lambda: ACT.copy(out=MIX[:, 0:4, 0:n], in_=b3(PSB[:, 0:512], 4)[:, :, 0:n]), r=[tPSB], w=[tMIX])
            listA = S.rec
            S.rec = []
            S.op("act", lambda: ACT.activation(out=CSQ[:, :, 0:n], in_=CV[:, :, 0:n], func=AF.Square), r=[tCV], w=[tCSQ])
            for j in range(4):
                S.op("pe", lambda j=j: PE.matmul(PS[0][:, 0:n], lhsT=ONF[:], rhs=CV[:, j, 0:n], start=(j == 0), stop=(j == 3)), r=[tCV, tC], w=[tPS[0]], sig=(j == 3))
            for j in range(4):
                S.op("pe", lambda j=j: PE.matmul(PS[1][:, 0:n], lhsT=ONF[:], rhs=CSQ[:, j, 0:n], start=(j == 0), stop=(j == 3)), r=[tCSQ, tC], w=[tPS[1]], sig=(j == 3))
            S.op("act", lambda: ACT.copy(out=LNS[:, 0, 0:n], in_=PS[0][:, 0:n]), r=[tPS[0]], w=[tLNS])
            S.op("dve", lambda: V.tensor_tensor(out=LNS[:, 1, 0:n], in0=LNS[:, 0, 0:n], in1=LNS[:, 0, 0:n], op=ALU.mult), r=[tLNS], w=[tLNS])
            S.op("dve", lambda: V.tensor_tensor(out=LNS[:, 1, 0:n], in0=PS[1][:, 0:n], in1=LNS[:, 1, 0:n], op=ALU.subtract), r=[tLNS, tPS[1]], w=[tLNS])
            S.op("act", lambda: ACT.activation(out=LNS[:, 2, 0:n], in_=LNS[:, 1, 0:n], func=AF.Ln, bias=CEPS[:, 2:3], scale=1.0), r=[tLNS, tC], w=[tLNS])
            S.op("act", lambda: ACT.activation(out=LNS[:, 2, 0:n], in_=LNS[:, 2, 0:n], func=AF.Exp, scale=-0.5), r=[tLNS], w=[tLNS])
            S.op("dve", lambda: V.tensor_tensor(out=CV[:, :, 0:n], in0=CV[:, :, 0:n], in1=LNS[:, 0, 0:n].unsqueeze(1).broadcast_to([128, 4, n]), op=ALU.subtract), r=[tCV, tLNS], w=[tCV])
            S.op("dve", lambda: V.tensor_tensor(out=CV[:, :, 0:n], in0=CV[:, :, 0:n], in1=LNS[:, 2, 0:n].unsqueeze(1).broadcast_to([128, 4, n]), op=ALU.mult), r=[tCV, tLNS], w=[tCV])
            for j in range(4):
                S.op("act", lambda j=j: ACT.activation(out=MIX[:, 4 + j, 0:n], in_=CV[:, j, 0:n], func=AF.Silu, bias=VEC[:, V_CBB + j:V_CBB + j + 1], scale=VEC[:, V_CG + j:V_CG + j + 1]),
                     r=[tCV, tVEC], w=[tMIX])
            listB = S.rec
            S.rec = None
            ia = ib = 0
            while ia < len(listA) or ib < len(listB):
                if ia < len(listA):
                    e_, fn_, r_, w_, sig_ = listA[ia]; ia += 1
                    S.op(e_, fn_, r=r_, w=w_, sig=sig_)
                    while ia < len(listA) and listA[ia][0] == "pe" and e_ == "pe":
                        e_, fn_, r_, w_, sig_ = listA[ia]; ia += 1
                        S.op(e_, fn_, r=r_, w=w_, sig=sig_)
                if ib < len(listB):
                    e_, fn_, r_, w_, sig_ = listB[ib]; ib += 1
                    S.op(e_, fn_, r=r_, w=w_, sig=sig_)
                    while ib < len(listB) and listB[ib][0] == "pe" and e_ == "pe":
                        e_, fn_, r_, w_, sig_ = listB[ib]; ib += 1
                        S.op(e_, fn_, r=r_, w=w_, sig=sig_)
            for m in range(8):
                bank = m // 4
                sl = m % 4
                for kc in range(8):
                    S.op("pe", lambda m=m, bank=bank, sl=sl, kc=kc: PE.matmul(PS[bank][:, sl * 128:sl * 128 + n], lhsT=WOUT[:, kc, m * 128:(m + 1) * 128], rhs=MIX[:, kc, 0:n],
                                                                            start=(kc == 0), stop=(kc == 7)), r=[tWOUT, tMIX], w=[tPS[bank]], sig=(kc == 7 and m % 4 == 3))
            for bank in range(2):
                hv = H[:, bank * 4:bank * 4 + 4, c0:c0 + n]
                S.op("dve", lambda bank=bank, hv=hv: V.tensor_tensor(out=hv, in0=hv, in1=b3(PS[bank][:, :], 4)[:, :, 0:n], op=ALU.add),
                     r=htks(c0, n) + [tPS[bank]], w=htks(c0, n))

        def phaseA_tile(l, c0, n, has_state):
            front(l, c0, n)
            mid(l, c0, n, has_state)
            back(l, c0, n)

        out_toks = []

        def emit_states(l, wkv_dst, sh_dst, cv_dst, n):
            with nc.allow_non_contiguous_dma(reason="small state vectors"):
                out_toks.append(S.dma("sp", lambda e: e.dma_start(out=sh_dst[0:1792].rearrange("(b p) -> p b", p=128), in_=P[:, 0:14, 0]), r=[tP]))
                out_toks.append(S.dma("sp", lambda e: e.dma_start(out=sh_dst[1792:1824].rearrange("(b p) -> p b", p=32), in_=P[0:32, 14:15, 0]), r=[tP]))
            for j in range(4):
                S.op("pe", lambda j=j: PE.transpose(out=PS[0][0:30, j * 128:(j + 1) * 128], in_=UH[:, j, 0:30], identity=IDF[:]), r=[tUH, tC], w=[tPS[0]], sig=(j == 3))
            S.op("act", lambda: ACT.copy(out=IOT[0:30, 0:512], in_=PS[0][0:30, :]), r=[tPS[0]], w=[tIOT])
            out_toks.append(S.dma("sp", lambda e: e.dma_start(out=cv_dst, in_=IOT[0:30, 0:512]), r=[tIOT]))
            for j in range(4):
                S.op("pe", lambda j=j: PE.transpose(out=PS[1][:, j * 128:(j + 1) * 128], in_=ST[:, j, :], identity=IDF[:]), r=[tST, tC], w=[tPS[1]], sig=(j == 3))
            S.op("act", lambda: ACT.copy(out=OSQ[:, :], in_=PS[1][:, :]), r=[tPS[1]], w=[tOSQ])
            for i in range(2):
                out_toks.append(S.dma("sp", lambda e, i=i: e.dma_start(out=wkv_dst.rearrange("(j i) v k -> i v j k", i=2)[i], in_=b3(OSQ[:, :], 4)[64 * i:64 * i + 64, :, 64 * i:64 * i + 64]),
                                      r=[tOSQ]))

        def load_sample_state(l, j):
            S.op("pool", lambda: POOL.memset(OSQ[:], 0.0), w=[tOSQ])
            for i in range(2):
                S.dma("sp", lambda e, i=i: e.dma_start(out=b3(OSQ[:, :], 4)[64 * i:64 * i + 64, :, 64 * i:64 * i + 64], in_=swkv[l, j].rearrange("(j i) v k -> i v j k", i=2)[i]), w=[tOSQ])
            for jp in range(4):
                S.op("pe", lambda jp=jp: PE.transpose(out=PS[1][:, jp * 128:(jp + 1) * 128], in_=OSQ[:, jp * 128:(jp + 1) * 128], identity=IDF[:]), r=[tOSQ, tC], w=[tPS[1]], sig=(jp == 3))
            S.op("act", lambda: ACT.copy(out=ST[:], in_=b3(PS[1][:, :], 4)), r=[tPS[1]], w=[tST])
            S.op("act", lambda: ACT.copy(out=STB[:], in_=ST[:]), r=[tST], w=[tSTB])
            with nc.allow_non_contiguous_dma(reason="small state vectors"):
                S.dma("sp", lambda e: e.dma_start(out=P[:, 0:14, 0], in_=sshift[l, j, 0:1792].rearrange("(b p) -> p b", p=128)), w=[tP])
                S.dma("sp", lambda e: e.dma_start(out=P[0:32, 14:15, 0], in_=sshift[l, j, 1792:1824].rearrange("(b p) -> p b", p=32)), w=[tP])
            S.dma("sp", lambda e: e.dma_start(out=IOT[0:30, 0:512], in_=sconv[l, j]), w=[tIOT])
            for jb in range(4):
                S.op("pe", lambda jb=jb: PE.transpose(out=PS[0][:, jb * 128:jb * 128 + 30], in_=IOT[0:30, jb * 128:(jb + 1) * 128], identity=IDF[0:30, 0:30]), r=[tIOT, tC], w=[tPS[0]], sig=(jb == 3))
            S.op("act", lambda: ACT.copy(out=UH[:, :, 0:30], in_=b3(PS[0][:, :], 4)[:, :, 0:30]), r=[tPS[0]], w=[tUH])

        def barrier_all():
            toks = []
            for e in ("pe", "dve", "act", "pool"):
                c = S.cnt[e]
                if c > 0:
                    ep, v = divmod(c - 1, EPOCH)
                    toks.append(((e, ep), v + 1))
            for i in range(NDSEM):
                if S.dcnt[i] > 0:
                    toks.append((("d", i), S.dcnt[i]))
            for e in ("pe", "dve", "act", "pool", "sp"):
                for tok in toks:
                    if tok[0][0] == e:
                        continue
                    S._wait(e, tok)

        def load_layer_weights_old(l):
            for kc in range(8):
                S.dma("pool", lambda e, kc=kc: e.dma_start(out=WIN[:, kc, :], in_=w_in[l, kc * 128:(kc + 1) * 128, :], max_dma_last_dim=4096), w=[tWIN])
            for kc in range(8):
                S.dma("pool", lambda e, kc=kc: e.dma_start(out=WOUT[:, kc, :], in_=w_out[l, kc * 128:(kc + 1) * 128, :], max_dma_last_dim=4096), w=[tWOUT])

        for l in range(DEPTH if STAGE >= 1 else 0):
            if l == 0:
                while w0_pieces:
                    w0_step()
                cur_ring[0] = RINGB
            barrier_all()
            for buf, tk_ in ((ATZ, tATZ), (BTZ, tBTZ), (KTZ, tKTZ), (SG, tSG), (P, tP), (LG, tLG)):
                S.op("pool", (lambda b=buf: POOL.memset(b[:], 0.0)), w=[tk_])
            S.dma("sp", lambda e: e.dma_start(out=VEC[:], in_=vecs[l]), w=[tVEC])
            for j in range(4):
                for ti in range(NPE):
                    S.op("dve", lambda j=j, ti=ti: V.tensor_scalar(out=DG[:, j * NPE + ti, :], in0=IDB[:], scalar1=VEC[:, V_CW + j * CW + ti:V_CW + j * CW + ti + 1], scalar2=None, op0=ALU.mult), r=[tVEC, tC], w=[tDG])
            stream_cast(RINGA, LW[0:64, :], lora_w[l], tLW, 0, 64)
            stream_cast(RINGA, LA[64:128, :], lora_a[l], tLA, 64, 128)
            stream_cast(RINGA, LG[:, 0, :], lora_g[l, 0:128, :], tLG, 0, 128)
            stream_cast(RINGA, LG[0:32, 1, :], lora_g[l, 128:160, :], tLG, 0, 32)
            S.dma("sp", lambda e: e.dma_start(out=GNG[:], in_=gn_g[l:l + 1, :].partition_broadcast(128)), w=[tGN])
            S.dma("sp", lambda e: e.dma_start(out=GNB[:], in_=gn_b[l:l + 1, :].partition_broadcast(128)), w=[tGN])
            S.op("pool", lambda: POOL.memset(ST[:], 0.0), w=[tST])
            S.op("pool", lambda: POOL.memset(STB[:], 0.0), w=[tSTB])
            S.op("pool", lambda: POOL.memset(P[:, :, 0:1], 0.0), w=[tP])
            S.op("pool", lambda: POOL.memset(UH[:, :, 0:30], 0.0), w=[tUH])
            KONLY = os.environ.get("KONLY", "")
            if KONLY != "128":
                phaseA_tile(l, 0, 16, False)
            front(l, 16, 128)
            for i in range(16):
                mid(l, 16 + i * 128, 128, True, side=(front_units(l, 16 + (i + 1) * 128, 128) if i < 15 else None))
                back(l, 16 + i * 128, 128)
            if STAGE >= 10:
                emit_states(l, wkvp[l], shp[l], cvp[l], 128)
            for j in range(NS if KONLY == "" else 0):
                load_sample_state(l, j)
                phaseA_tile(l, NMETA + NPT + j * 16, 16, True)
                if STAGE >= 10:
                    emit_states(l, wkvs[l, j], shs[l, j], cvs[l, j], 16)
            barrier_all()
            if STAGE >= 11:
                SQ2 = WORK[:, 26240:26240 + 1024].rearrange("p (a b) -> p a b", a=8)
                RS2 = WORK[:, 26240 + 1024:26240 + 1024 + 256].bitcast(F32)
                tSQ2, tRS2 = Tk(), Tk()
                assert 26240 + 1024 + 256 <= WORK_E
                c = 0
                kalt = 0
                while c < TT:
                    n = min(128, TT - c)
                    if kalt % 2 == 0:
                        rmsnorm_tile(c, n, lambda cc: VEC[:, V_NF + cc:V_NF + cc + 1], lambda cc, c=c, n=n: HN[:, cc, c:c + n], tHN)
                    else:
                        rmsnorm_tile(c, n, lambda cc: VEC[:, V_NF + cc:V_NF + cc + 1], lambda cc, c=c, n=n: HN[:, cc, c:c + n], tHN,
                                     SQ2, tSQ2, RS2, tRS2, PS7, tPSB)
                    kalt += 1
                    c += n
                def load_slot(e8_):
                    sl_ = e8_ % 2
                    for kc in range(8):
                        stream_cast(RINGB, WU[sl_][:, kc, :], w_up[l, kc * 128:(kc + 1) * 128, e8_ * 512:(e8_ + 1) * 512], tWU[sl_])
                    for fc in range(4):
                        r0 = e8_ * 512 + fc * 128
                        stream_cast(RINGB, WDf[sl_][fc], w_down[l, r0:r0 + 128, :], tWD[sl_])

                tiles_ = [(0, 480), (480, 480), (960, 480), (1440, 480), (1920, TT - 1920)]
                assert sum(N for _, N in tiles_) == TT and all(N <= 512 and t0 % 16 == 0 for t0, N in tiles_)
                items = [(e8, t0, N) for e8 in range(8) for (t0, N) in tiles_]
                UPB = [0, 1, 2, 3, 6, 7]

                def bk(i):
                    return (PS[i], tPS[i]) if i < 7 else (PS7, tPSB)

                def emit_up(idx):
                    e8, t0, N = items[idx]
                    sl = e8 % 2
                    for fc in range(4):
                        pb_, tpb_ = bk(UPB[(4 * idx + fc) % 6])
                        for kc in range(8):
                            S.op("pe", lambda fc=fc, kc=kc, sl=sl, t0=t0, N=N, pb_=pb_: PE.matmul(pb_[:, 0:N], lhsT=WU[sl][:, kc, fc * 128:(fc + 1) * 128], rhs=HN[:, kc, t0:t0 + N],
                                                                                                  start=(kc == 0), stop=(kc == 7)), r=[tWU[sl], tHN], w=[tpb_], sig=(kc == 7))

                def emit_act(idx):
                    e8, t0, N = items[idx]
                    for fc in range(4):
                        pb_, tpb_ = bk(UPB[(4 * idx + fc) % 6])
                        S.op("act", lambda N=N, pb_=pb_: ACT.activation(out=R32[:, 0:N], in_=pb_[:, 0:N], func=AF.Relu), r=[tpb_], w=[tR32])
                        S.op("dve", lambda fc=fc, N=N, pb_=pb_: V.tensor_tensor(out=A2B[:, fc, 0:N], in0=R32[:, 0:N], in1=pb_[:, 0:N], op=ALU.mult), r=[tR32, tpb_], w=[tA2B])

                def emit_down(idx):
                    e8, t0, N = items[idx]
                    sl = e8 % 2
                    for m in range(8):
                        pb = 4 + (m % 2)
                        for fc in range(4):
                            S.op("pe", lambda m=m, fc=fc, sl=sl, pb=pb, N=N: PE.matmul(PS[pb][:, 0:N], lhsT=WDf[sl][fc][:, m * 128:(m + 1) * 128], rhs=A2B[:, fc, 0:N],
                                                                                     start=(fc == 0), stop=(fc == 3)), r=[tWD[sl], tA2B], w=[tPS[pb]], sig=(fc == 3))
                        hv = H[:, m, t0:t0 + N]
                        S.op("dve", lambda hv=hv, pb=pb, N=N: V.tensor_tensor(out=hv, in0=hv, in1=PS[pb][:, 0:N], op=ALU.add),
                             r=htks(t0, N) + [tPS[pb]], w=htks(t0, N))

                load_slot(0)
                load_slot(1)
                pf_pieces = layer_weight_pieces(l + 1) if l + 1 < DEPTH else []
                pf_pieces.reverse()
                emit_up(0)
                emit_act(0)
                for idx in range(len(items)):
                    if idx + 1 < len(items):
                        emit_up(idx + 1)
                    emit_down(idx)
                    if idx + 1 < len(items):
                        emit_act(idx + 1)
                    e8 = items[idx][0]
                    if idx + 1 == len(items) or items[idx + 1][0] != e8:
                        if e8 + 2 < 8:
                            load_slot(e8 + 2)
                        for _ in range(3):
                            if pf_pieces:
                                pf_pieces.pop()()
                while pf_pieces:
                    pf_pieces.pop()()
        barrier_all()
        XFs = [REG[:, k * 2048:(k + 1) * 2048].bitcast(F32).rearrange("p (k n) -> p k n", k=8) for k in range(2)]
        tXFs = [Tk(), Tk()]
        IOTs = [REG[:, 4096 + k * 2048:4096 + (k + 1) * 2048].bitcast(F32) for k in range(2)]
        tIOTs = [Tk(), Tk()]
        fcount = [0]

        def final_tile(c0, n, dst):
            k_ = fcount[0] % 2
            fcount[0] += 1
            XF, tXF, IOT, tIOT = XFs[k_], tXFs[k_], IOTs[k_], tIOTs[k_]
            rmsnorm_tile(c0, n, lambda c: NFIN[:, c:c + 1], lambda c: XF[:, c, 0:n], tXF)
            for half in range(2):
                for c in range(4):
                    cc = half * 4 + c
                    S.op("pe", lambda half=half, c=c, cc=cc: PE.transpose(out=PS[half][0:n, c * 128:(c + 1) * 128], in_=XF[:, cc, 0:n], identity=IDF[:]), r=[tXF, tC], w=[tPS[half]], sig=(c == 3))
                S.op("act", lambda half=half: ACT.copy(out=IOT[0:n, half * 512:(half + 1) * 512], in_=PS[half][0:n, :]), r=[tPS[half]], w=[tIOT])
            out_toks.append(S.dma("sp", lambda e: e.dma_start(out=dst, in_=IOT[0:n, :]), r=[tIOT]))

        for i in range(16):
            final_tile(16 + i * 128, 128, yp[i * 128:(i + 1) * 128, :])
        for j in range(NS):
            final_tile(NMETA + NPT + j * 16, 16, ys[j * 16:(j + 1) * 16, :])
        S.final_wait(out_toks)
    return nc


_NC_CACHE = {}


def _pack_vecs(inp):
    f = np.float32
    out = np.zeros((DEPTH, 128, NV), f)

    def fm(v, ncol):
        return v.reshape(DEPTH, ncol, 128).transpose(0, 2, 1)

    out[:, :, V_NM:V_NM + 8] = fm(inp["norm_mix"], 8)
    mu = np.zeros((DEPTH, 15 * 128), f)
    mu[:, :PR] = inp["mu_shift"]
    out[:, :, V_MU:V_MU + 15] = fm(mu, 15)
    out[:, :, V_W0:V_W0 + 4] = fm(inp["w0"], 4)
    out[:, :, V_A0:V_A0 + 4] = fm(inp["a0"], 4)
    out[:, :, V_KK:V_KK + 4] = fm(inp["k_k"], 4)
    out[:, :, V_KA:V_KA + 4] = fm(inp["k_a"], 4)
    out[:, :, V_RK:V_RK + 4] = fm(inp["r_k"].reshape(DEPTH, 512), 4)
    cw = inp["conv_w"].reshape(DEPTH, CW, 4, 128).transpose(0, 3, 2, 1)
    out[:, :, V_CW:V_CW + 4 * CW] = cw.reshape(DEPTH, 128, 4 * CW)
    out[:, :, V_CB:V_CB + 4] = fm(inp["conv_b"], 4)
    out[:, :, V_CG:V_CG + 4] = fm(inp["cln_g"], 4)
    out[:, :, V_CBB:V_CBB + 4] = fm(inp["cln_b"], 4)
    out[:, :, V_NF:V_NF + 8] = fm(inp["norm_ffn"], 8)
    return np.ascontiguousarray(out)


def kernel(**inputs):
    inp = {k: np.asarray(v) for k, v in inputs.items()}
    if "nc" not in _NC_CACHE:
        _NC_CACHE["nc"] = build_program()
    nc = _NC_CACHE["nc"]
    vecs = _pack_vecs(inp)
    nfin = np.ascontiguousarray(inp["norm_final"].reshape(8, 128).T.astype(np.float32))
    shared = {
        "meta": np.ascontiguousarray(inp["meta_tokens"]), "vecs": vecs, "nfin": nfin,
        "w_in": inp["w_in"], "lora_w": inp["lora_w"], "lora_a": inp["lora_a"], "lora_g": inp["lora_g"],
        "gn_g": inp["gn_g"], "gn_b": inp["gn_b"], "w_out": inp["w_out"], "w_up": inp["w_up"], "w_down": inp["w_down"],
    }
    in_maps = []
    for c in range(8):
        m = dict(shared)
        m["xp"] = np.ascontiguousarray(inp["x_prompt"][c])
        m["xs"] = np.ascontiguousarray(inp["x_sample"][4 * c:4 * c + 4].reshape(NS * TS, D))
        m["swkv"] = np.ascontiguousarray(inp["state_wkv"][:, 4 * c:4 * c + 4])
        m["sshift"] = np.ascontiguousarray(inp["state_shift"][:, 4 * c:4 * c + 4])
        m["sconv"] = np.ascontiguousarray(inp["cache_conv"][:, 4 * c:4 * c + 4])
        in_maps.append(m)
    res = run_bass_kernel_spmd(nc, in_maps, core_ids=list(range(8)))
    rs = res.results
    y_prompt = np.stack([r["yp"] for r in rs], 0).astype(np.float32)
    y_sample = np.concatenate([r["ys"].reshape(NS, TS, D) for r in rs], 0).astype(np.float32)
    wkv_p = np.stack([r["wkvp"] for r in rs], 1).astype(np.float32)
    sh_p = np.stack([r["shp"] for r in rs], 1).astype(np.float32)
    cv_p = np.stack([r["cvp"] for r in rs], 1).astype(np.float32)
    wkv_s = np.concatenate([r["wkvs"] for r in rs], 1).astype(np.float32)
    sh_s = np.concatenate([r["shs"] for r in rs], 1).astype(np.float32)
    cv_s = np.concatenate([r["cvs"] for r in rs], 1).astype(np.float32)
    return (y_prompt, y_sample, wkv_p, sh_p, cv_p, wkv_s, sh_s, cv_s)
```

```python
import numpy as np
from contextlib import ExitStack
import concourse.bass as bass
import concourse.mybir as mybir
from concourse.bass_utils import run_bass_kernel_spmd

F32 = mybir.dt.float32
BF16 = mybir.dt.bfloat16
AF = mybir.ActivationFunctionType
ALU = mybir.AluOpType
AX = mybir.AxisListType

DEPTH = 4
D = 1024
NPT = 2048
NMETA = 16
NS = 4
TS = 16
TT = NMETA + NPT + NS * TS
PR = 1824
PIN = 2848
DFF = 4096
CW = 31
DECAY = 0.606531
NV = 187
V_NM, V_MU, V_W0, V_A0, V_KK, V_KA, V_RK, V_CW, V_CB, V_CG, V_CBB, V_NF = 0, 8, 23, 27, 31, 35, 39, 43, 167, 171, 175, 179
EPOCH = 30000
NDSEM = 24
import os
STAGE = float(os.environ.get("KSTAGE", "999"))


class Tk:
    __slots__ = ("w", "r", "excl")

    def __init__(self, excl=False):
        self.w = None
        self.r = []
        self.excl = excl


class Sched:
    def __init__(self, nc, es):
        self.nc = nc
        self.es = es
        self.eng = {"pe": nc.tensor, "dve": nc.vector, "act": nc.scalar, "pool": nc.gpsimd, "sp": nc.sync}
        self.sems = {}
        self.cnt = {k: 0 for k in self.eng}
        self.seen = {k: {} for k in self.eng}
        self.dsem = [es.enter_context(nc.semaphore(f"dq{i}")) for i in range(NDSEM)]
        self.dcnt = [0] * NDSEM
        self.dnext = 0
        self.ninst = 0
        self.rec = None

    def _sem(self, e, ep):
        key = (e, ep)
        if key not in self.sems:
            self.sems[key] = self.es.enter_context(self.nc.semaphore(f"s_{e}_{ep}"))
        return self.sems[key]

    def _wait(self, e, tok):
        key, val = tok
        if self.seen[e].get(key, 0) >= val:
            return
        if key[0] == "pe" and e != "pe" and STAGE > 900:
            assert key[1] * EPOCH + val <= self.cnt["pe"], "wait on unsignalled PE op"
        if key[0] == "d":
            sem = self.dsem[key[1]]
        else:
            sem = self._sem(key[0], key[1])
        self.eng[e].wait_ge(sem, val)
        self.seen[e][key] = val

    def _deps(self, e, r, w):
        deps = []
        for t in r:
            if t.w is not None:
                deps.append(t.w)
            if t.excl:
                deps.extend(tok for tok in t.r if tok[0][0] != e)
        for t in w:
            if t.w is not None:
                deps.append(t.w)
            deps.extend(t.r)
        for tok in deps:
            if tok[0][0] == e and e == "pe":
                continue
            self._wait(e, tok)

    def op(self, e, fn, r=(), w=(), sig=True):
        if self.rec is not None:
            self.rec.append((e, fn, list(r), list(w), sig))
            return None
        self._deps(e, r, w)
        inst = fn()
        if sig or e != "pe":
            self.cnt[e] += 1
            c = self.cnt[e]
            ep, v = divmod(c - 1, EPOCH)
            v += 1
            inst.then_inc(self._sem(e, ep), 1)
        else:
            c = self.cnt[e] + 1
            ep, v = divmod(c - 1, EPOCH)
            v += 1
        tok = ((e, ep), v)
        self.seen[e][(e, ep)] = max(self.seen[e].get((e, ep), 0), 0)
        for t in r:
            t.r.append(tok)
        for t in w:
            t.w = tok
            t.r = []
        self.ninst += 1
        return tok

    def dma(self, q, fn, r=(), w=()):
        i = self.dnext
        self.dnext = (self.dnext + 1) % NDSEM
        if self.dcnt[i] > 0:
            self._wait(q, (("d", i), self.dcnt[i]))
        self._deps(q, r, w)
        inst = fn(self.eng[q])
        self.dcnt[i] += 16
        inst.then_inc(self.dsem[i], 16)
        tok = (("d", i), self.dcnt[i])
        for t in r:
            t.r.append(tok)
        for t in w:
            t.w = tok
            t.r = []
        self.ninst += 1
        return tok

    def final_wait(self, toks):
        for tok in toks:
            self._wait("sp", tok)


def build_program():
    nc = bass.Bass("TRN2", target_bir_lowering=False, dynamic_dma_scratch_size=4096)

    def din(name, shape):
        return nc.dram_tensor(name, list(shape), F32, kind="ExternalInput").ap()

    def dout(name, shape):
        return nc.dram_tensor(name, list(shape), F32, kind="ExternalOutput").ap()

    xp = din("xp", [NPT, D])
    meta = din("meta", [NMETA, D])
    xs_in = din("xs", [NS * TS, D])
    swkv = din("swkv", [DEPTH, NS, 8, 64, 64])
    sshift = din("sshift", [DEPTH, NS, PR])
    sconv = din("sconv", [DEPTH, NS, 30, 512])
    vecs = din("vecs", [DEPTH, 128, NV])
    nfin = din("nfin", [128, 8])
    w_in = din("w_in", [DEPTH, D, PIN])
    lora_w = din("lora_w", [DEPTH, 64, 512])
    lora_a = din("lora_a", [DEPTH, 64, 512])
    lora_g = din("lora_g", [DEPTH, 160, 512])
    gn_g = din("gn_g", [DEPTH, 512])
    gn_b = din("gn_b", [DEPTH, 512])
    w_out = din("w_out", [DEPTH, D, D])
    w_up = din("w_up", [DEPTH, D, DFF])
    w_down = din("w_down", [DEPTH, DFF, D])

    yp = dout("yp", [NPT, D])
    ys = dout("ys", [NS * TS, D])
    wkvp = dout("wkvp", [DEPTH, 8, 64, 64])
    shp = dout("shp", [DEPTH, PR])
    cvp = dout("cvp", [DEPTH, 30, 512])
    wkvs = dout("wkvs", [DEPTH, NS, 8, 64, 64])
    shs = dout("shs", [DEPTH, NS, PR])
    cvs = dout("cvs", [DEPTH, NS, 30, 512])

    es = ExitStack()
    with es:
        S = Sched(nc, es)

        def sb(name, shape, dt=F32):
            return es.enter_context(nc.sbuf_tensor(name, list(shape), dt))

        H = sb("H", [128, 8, TT])
        Htk = [Tk() for _ in range(TT // 16)]

        def htks(c0, n):
            return Htk[c0 // 16:(c0 + n + 15) // 16]

        REG_E = 8 * PIN + 8 * D
        REG = sb("REG", [128, REG_E], BF16)
        WIN = REG[:, 0:8 * PIN].rearrange("p (k n) -> p k n", k=8)
        WOUT = REG[:, 8 * PIN:8 * PIN + 8 * D].rearrange("p (k n) -> p k n", k=8)
        HALF = TT // 2
        tWIN, tWOUT, tHN, tA2B = Tk(), Tk(), Tk(), Tk()
        tWU = [Tk(), Tk()]
        tWD = [Tk(), Tk()]

        VEC = sb("VEC", [128, NV]); tVEC = Tk()
        NFIN = sb("NFIN", [128, 8]); tNFIN = Tk()
        LW = sb("LW", [128, 512], BF16); LA = sb("LA", [128, 512], BF16); LG = sb("LG", [128, 2, 512], BF16)
        tLW, tLA, tLG = Tk(), Tk(), Tk()
        GNG = sb("GNG", [128, 512]); GNB = sb("GNB", [128, 512]); tGN = Tk()
        IDF = sb("IDF", [128, 128]); IDB = sb("IDB", [128, 128], BF16)
        ONESC = sb("ONESC", [128, 128], BF16)
        ONF = sb("ONF", [128, 128])
        BONES = sb("BONES", [128, 128], BF16)
        HIND = sb("HIND", [128, 2], BF16)
        MKUR = sb("MKUR", [128, 2, 128], BF16)
        MKSL = sb("MKSL", [128, 128], BF16)
        ONESF = sb("ONESF", [128, 128])
        CEPS = sb("CEPS", [128, 4])
        tC = Tk()
        WORK_E = 28774
        WORK = sb("WORK", [128, WORK_E], BF16)
        wo = [0]

        def wk(name, shape, dt=F32):
            ne = 1
            for d in shape[1:]:
                ne *= d
            nb = ne * (2 if dt == BF16 else 4)
            nb = (nb + 3) // 4 * 4
            a = wo[0]
            wo[0] += nb // 2
            assert wo[0] <= WORK_E, (name, wo[0])
            v = WORK[:, a:a + nb // 2]
            if dt != BF16:
                v = v.bitcast(F32)
            else:
                v = v[:, 0:ne]
            if len(shape) == 2:
                return v
            if len(shape) == 3:
                return v.rearrange("p (a b) -> p a b", a=shape[1])
            return v.rearrange("p (a b c) -> p a b c", a=shape[1], b=shape[2])
        RS = sb("RS", [128, 128]); tRS = Tk()
        tXN = Tk()
        P = wk("P", [128, 15, 129]); tP = Tk()
        XS = sb("XS", [128, 15, 128]); tXS = Tk()
        TWX = wk("TWX", [128, 128], BF16); tTWX = Tk()
        SG = wk("SG", [128, 2, 128], BF16); tSG = Tk()
        A_ = wk("A_", [128, 4, 128]); tA = Tk()
        SW = wk("SW", [128, 4, 128]); tSW = Tk()
        LR = wk("LR", [128, 4, 128]); tLR = Tk()
        EN = wk("EN", [128, 4, 128]); tEN = Tk()
        EP = wk("EP", [128, 4, 128]); tEP = Tk()
        KK = wk("KK", [128, 4, 128]); tKK = Tk()
        T1 = wk("T1", [128, 4, 128]); tT1 = Tk()
        KK2 = wk("KK2", [128, 4, 128], BF16); tKK2 = Tk()
        T2B = wk("T2B", [128, 4, 128], BF16); tT2B = Tk()
        AR = wk("AR", [128, 4, 2, 128], BF16); tAR = Tk()
        ATZ = wk("ATZ", [128, 4, 2, 128], BF16); tATZ = Tk()
        BTZ = wk("BTZ", [128, 4, 2, 128], BF16); tBTZ = Tk()
        KTZ = wk("KTZ", [128, 4, 2, 128], BF16); tKTZ = Tk()
        BTU = wk("BTU", [128, 4, 128], BF16); tBTU = Tk()
        KTU = wk("KTU", [128, 4, 128], BF16); tKTU = Tk()
        VT = wk("VT", [128, 512], BF16); tVT = Tk()
        BTT = wk("BTT", [128, 2, 512], BF16); tBTT = Tk()
        AMA = wk("AMA", [128, 4, 2, 128], BF16); tAMA = tXN; XN = AMA[:, :, :, :].rearrange("p a b c -> p (a b) c")
        AMK = wk("AMK", [128, 4, 2, 128], BF16); tAMK = Tk()
        NQ = [wk(f"NQ{i}", [128, 4, 128], BF16) for i in range(2)]; tNQ = [Tk(), Tk()]
        QT = [wk(f"QT{i}", [128, 4, 128], BF16) for i in range(2)]; tQT = [Tk(), Tk()]
        MM = wk("MM", [128, 4, 128], BF16); tMM = Tk()
        ACC = [wk(f"ACC{i}", [128, 4, 128], BF16) for i in range(2)]; tACC = [Tk(), Tk()]
        XB = wk("XB", [128, 256], BF16); tXB = Tk()
        UB = wk("UB", [128, 256], BF16); tUB = Tk()
        ST = wk("ST", [128, 4, 128]); tST = Tk()
        STB = wk("STB", [128, 4, 128], BF16); tSTB = Tk()
        OTM = EN[:, :, :].rearrange("p a b -> p (a b)"); tOTM = tEN
        OSQ = A_[:, :, :].rearrange("p a b -> p (a b)"); tOSQ = tA
        STAT = wk("STAT", [128, 6, 8]); tSTAT = Tk()
        YRT = KK2[:, :, :].rearrange("p a b -> p (a b)"); tYRT = tKK2
        MIX = sb("MIX", [128, 8, 128], BF16); tMIX = Tk(); SQ_DEF = MIX; tSQ_DEF = tMIX
        SQA = BTT[:, :, :].rearrange("p a (b c) -> p (a b) c", b=4)
        UH = wk("UH", [128, 4, 30 + 128]); tUH = Tk()
        NPE = 14
        DG = sb("DG", [128, 4 * NPE, 128], BF16); tDG = Tk()
        UHB = wk("UHB", [128, 4, 30 + 128], BF16); tUHB = Tk()
        tCVa = [Tk() for _ in range(4)]; tCVb = [Tk() for _ in range(4)]
        TMPH = RS[:, 0:120].rearrange("p (a b) -> p a b", a=4); tTMPH = tRS
        CV = KK; tCV = tKK
        CSQ = T1; tCSQ = tT1
        LNS = LR[:, 0:3, :]; tLNS = tLR
        XSF = XS[:, :, :].rearrange("p a b -> p (a b)"); IOT = XSF[:, 0:1024]; tIOT = tXS
        R32 = XSF[:, 1024:1536]; tR32 = tXS
        PS = [es.enter_context(nc.psum_tensor(f"ps{i}", [128, 512], F32)) for i in range(7)]
        tPS = [Tk(True) for _ in range(7)]
        PS7 = es.enter_context(nc.psum_tensor("ps7", [128, 512], F32)); tPSB = Tk(True)
        PSB = PS7[:, :].bitcast(BF16)

        def b3(ap, a):
            return ap.rearrange("p (a b) -> p a b", a=a)

        def b4(ap, a, b):
            return ap.rearrange("p (a b c) -> p a b c", a=a, b=b)

        V = nc.vector
        ACT = nc.scalar
        PE = nc.tensor
        POOL = nc.gpsimd

        o = 0
        HN = WORK[:, o:o + 8 * TT].rearrange("p (k n) -> p k n", k=8); o += 8 * TT
        WU = [None, None]
        WDf = [None, None]
        WU[0] = WORK[:, o:o + 8 * 512].rearrange("p (k n) -> p k n", k=8); o += 8 * 512
        WDf[0] = [WORK[:, o + fc * 1024:o + (fc + 1) * 1024] for fc in range(4)]; o += 4 * 1024
        DGF = DG[:, :, :].rearrange("p a b -> p (a b)")
        WU[1] = DGF[:, 0:8 * 512].rearrange("p (k n) -> p k n", k=8)
        WDf[1] = [DGF[:, 4096 + fc * 1024:4096 + (fc + 1) * 1024] for fc in range(3)] + [WORK[:, o:o + 1024]]; o += 1024
        assert o <= WORK_E, o
        assert 4096 + 3 * 1024 <= 4 * NPE * 128
        A2B = XS[:, :, :].rearrange("p a b -> p (a b)").bitcast(BF16)[:, 0:2048].rearrange("p (k n) -> p k n", k=4)
        def mk(fn, w):
            S.op("pool", fn, w=w)

        mk(lambda: POOL.memset(IDF[:], 0.0), [tC])
        mk(lambda: POOL.memset(ONESF[:], 1.0), [tC])
        mk(lambda: POOL.affine_select(out=IDF[:], in_=ONESF[:], pattern=[[-1, 128]], compare_op=ALU.is_equal, fill=0.0, base=0, channel_multiplier=1), [tC])
        mk(lambda: POOL.tensor_copy(out=IDB[:], in_=IDF[:]), [tC])
        mk(lambda: POOL.memset(ONESC[:], 1.0 / 1024.0), [tC])
        mk(lambda: POOL.memset(ONF[:], 1.0 / 512.0), [tC])
        mk(lambda: POOL.memset(BONES[:], 0.0), [tC])
        mk(lambda: POOL.memset(BONES[0:64, 0:64], 1.0), [tC])
        mk(lambda: POOL.memset(BONES[64:128, 64:128], 1.0), [tC])
        mk(lambda: POOL.memset(HIND[:], 0.0), [tC])
        mk(lambda: POOL.memset(HIND[0:64, 0:1], 1.0), [tC])
        mk(lambda: POOL.memset(HIND[64:128, 1:2], 1.0), [tC])
        mk(lambda: POOL.affine_select(out=MKUR[:, 0, :], in_=ONESF[:], pattern=[[1, 128]], compare_op=ALU.is_gt, fill=0.0, base=0, channel_multiplier=-1), [tC])
        mk(lambda: POOL.affine_select(out=MKUR[:, 1, :], in_=ONESF[:], pattern=[[1, 128]], compare_op=ALU.is_ge, fill=0.0, base=0, channel_multiplier=-1), [tC])
        mk(lambda: POOL.affine_select(out=MKSL[:], in_=ONESF[:], pattern=[[-1, 128]], compare_op=ALU.is_gt, fill=0.0, base=0, channel_multiplier=1), [tC])
        mk(lambda: POOL.memset(CEPS[:, 0:1], 1e-6), [tC])
        mk(lambda: POOL.memset(CEPS[:, 1:2], 64e-5), [tC])
        mk(lambda: POOL.memset(CEPS[:, 2:3], 1e-5), [tC])
        mk(lambda: POOL.memset(CEPS[:, 3:4], 1e-19), [tC])
        for buf, tk_ in ((ATZ, tATZ), (BTZ, tBTZ), (KTZ, tKTZ), (LW, tLW), (LA, tLA), (LG, tLG),
                         (ST, tST), (UH, tUH)):
            mk((lambda b=buf: POOL.memset(b[:], 0.0)), [tk_])

        LST = [WORK[:, k * 2048:(k + 1) * 2048].bitcast(F32) for k in range(3)]
        tLST = [Tk(), Tk(), Tk()]
        lcount = [0]

        def load_tokens(src_ap, n, c0):
            k_ = lcount[0] % 3
            lcount[0] += 1
            IOT, tIOT = LST[k_], tLST[k_]
            S.dma("sp", lambda e: e.dma_start(out=IOT[0:n, :], in_=src_ap), w=[tIOT])
            for half in range(2):
                for c in range(4):
                    cc = half * 4 + c
                    S.op("pe", lambda cc=cc, c=c: PE.transpose(out=PS[half][:, c * 128:c * 128 + n], in_=IOT[0:n, cc * 128:(cc + 1) * 128], identity=IDF[0:n, 0:n]),
                         r=[tIOT, tC], w=[tPS[half]], sig=(c == 3))
                S.op("act", lambda half=half: ACT.copy(out=H[:, half * 4:half * 4 + 4, c0:c0 + n], in_=b3(PS[half][:, :], 4)[:, :, 0:n]),
                     r=[tPS[half]], w=htks(c0, n))

        RING0 = [(WORK[:, 6144 + k * 1024:6144 + (k + 1) * 1024].bitcast(F32), Tk()) for k in range(4)]
        RINGB = [(GNG, Tk()), (GNB, Tk()), (LG[:, :, :].rearrange("p a b -> p (a b)").bitcast(F32), Tk())]
        RINGA = [(XSF[:, k * 512:(k + 1) * 512], tXS) for k in range(3)]
        ring_pos = [0]

        def stream_cast(ring, dst, src, dst_tk, P0=0, P1=128):
            W = src.shape[-1]
            c = 0
            while c < W:
                w_ = min(512, W - c)
                stg, stk = ring[ring_pos[0] % len(ring)]
                ring_pos[0] += 1
                S.dma("sp", lambda e, stg=stg, c=c, w_=w_: e.dma_start(out=stg[P0:P1, 0:w_], in_=src[:, c:c + w_]), w=[stk])
                S.op("pool", lambda stg=stg, c=c, w_=w_: POOL.tensor_copy(out=dst[:, c:c + w_], in_=stg[P0:P1, 0:w_]), r=[stk], w=[dst_tk])
                c += w_

        cur_ring = [RING0]

        def layer_weight_pieces(l):
            ps_ = []
            for kc in range(8):
                ps_.append(lambda kc=kc: stream_cast(cur_ring[0], WIN[:, kc, :], w_in[l, kc * 128:(kc + 1) * 128, :], tWIN))
            for kc in range(8):
                ps_.append(lambda kc=kc: stream_cast(cur_ring[0], WOUT[:, kc, :], w_out[l, kc * 128:(kc + 1) * 128, :], tWOUT))
            return ps_

        def load_layer_weights(l):
            for p_ in layer_weight_pieces(l):
                p_()

        w0_pieces = layer_weight_pieces(0)
        w0_pieces.reverse()

        def w0_step():
            if w0_pieces:
                w0_pieces.pop()()

        load_tokens(meta[:, :], 16, 0)
        w0_step()
        for i in range(16):
            load_tokens(xp[i * 128:(i + 1) * 128, :], 128, 16 + i * 128)
            w0_step()
        for j in range(NS):
            load_tokens(xs_in[j * 16:(j + 1) * 16, :], 16, NMETA + NPT + j * 16)
        S.dma("sp", lambda e: e.dma_start(out=NFIN[:], in_=nfin[:, :]), w=[tNFIN])

        def rmsnorm_tile(c0, n, gcol_ap, out_fn, out_tk, SQ=None, tSQ=None, RSb=None, tRSb=None, PSn=None, tPSn=None):
            if SQ is None:
                SQ, tSQ = SQ_DEF, tSQ_DEF
            if RSb is None:
                RSb, tRSb = RS, tRS
            if PSn is None:
                PSn, tPSn = PS[6], tPS[6]
            hT = H[:, :, c0:c0 + n]
            S.op("act", lambda: ACT.activation(out=SQ[:, :, 0:n], in_=hT, func=AF.Square), r=htks(c0, n), w=[tSQ])
            for c in range(8):
                S.op("pe", lambda c=c: PE.matmul(PSn[:, 0:n], lhsT=ONESC[:], rhs=SQ[:, c, 0:n], start=(c == 0), stop=(c == 7)),
                     r=[tSQ, tC], w=[tPSn], sig=(c == 7))
            S.op("act", lambda: ACT.activation(out=RSb[:, 0:n], in_=PSn[:, 0:n], func=AF.Ln, bias=CEPS[:, 0:1], scale=1.0),
                 r=[tPSn, tC], w=[tRSb])
            S.op("act", lambda: ACT.activation(out=RSb[:, 0:n], in_=RSb[:, 0:n], func=AF.Exp, scale=-0.5), r=[tRSb], w=[tRSb])
            for c in range(8):
                S.op("dve", lambda c=c: V.scalar_tensor_tensor(out=out_fn(c), in0=H[:, c, c0:c0 + n], scalar=gcol_ap(c), in1=RSb[:, 0:n],
                                                                op0=ALU.mult, op1=ALU.mult),
                     r=htks(c0, n) + [tRSb, tVEC, tNFIN], w=[out_tk])

        def nrounds_for(n):
            r = 0
            while (1 << r) < n:
                r += 1
            return r

        def front(l, c0, n):
            rmsnorm_tile(c0, n, lambda c: VEC[:, V_NM + c:V_NM + c + 1], lambda c: XN[:, c, 0:n], tXN, SQA, tBTT)
            for m in range(15):
                M = 128 if m < 14 else 32
                bank = m // 4
                sl = m % 4
                for kc in range(8):
                    S.op("pe", lambda m=m, M=M, bank=bank, sl=sl, kc=kc: PE.matmul(PS[bank][0:M, sl * 128:sl * 128 + n], lhsT=WIN[:, kc, m * 128:m * 128 + M],
                                                                                 rhs=XN[:, kc, 0:n], start=(kc == 0), stop=(kc == 7)),
                         r=[tWIN, tXN], w=[tPS[bank]], sig=(kc == 7 and (m % 4 == 3 or m == 14)))
            for bank in range(3):
                S.op("act", lambda bank=bank: ACT.copy(out=P[:, bank * 4:bank * 4 + 4, 1:n + 1], in_=b3(PS[bank][:, :], 4)[:, :, 0:n]),
                     r=[tPS[bank]], w=[tP])
            S.op("act", lambda: ACT.copy(out=P[:, 12:14, 1:n + 1], in_=b3(PS[3][:, :], 4)[:, 0:2, 0:n]), r=[tPS[3]], w=[tP])
            S.op("act", lambda: ACT.copy(out=P[0:32, 14, 1:n + 1], in_=PS[3][0:32, 256:256 + n]), r=[tPS[3]], w=[tP])
            for m in range(8):
                bank = 4 if m < 4 else 0
                sl = m % 4
                col = PR + m * 128
                for kc in range(8):
                    S.op("pe", lambda bank=bank, sl=sl, col=col, kc=kc: PE.matmul(PS[bank][:, sl * 128:sl * 128 + n], lhsT=WIN[:, kc, col:col + 128],
                                                                                rhs=XN[:, kc, 0:n], start=(kc == 0), stop=(kc == 7)),
                         r=[tWIN, tXN], w=[tPS[bank]], sig=(kc == 7 and m % 4 == 3))
            S.op("act", lambda: ACT.activation(out=CSQ[:, :, 0:n], in_=b3(PS[0][:, :], 4)[:, :, 0:n], func=AF.Sigmoid), r=[tPS[0]], w=[tCSQ])
            S.op("dve", lambda: V.tensor_tensor(out=UH[:, :, 30:30 + n], in0=CSQ[:, :, 0:n], in1=b3(PS[4][:, :], 4)[:, :, 0:n], op=ALU.mult),
                 r=[tCSQ, tPS[4]], w=[tUH])
            S.op("dve", lambda: V.tensor_tensor(out=XS[:, :, 0:n], in0=P[:, :, 0:n], in1=P[:, :, 1:n + 1], op=ALU.subtract), r=[tP], w=[tXS])
            S.op("dve", lambda: V.tensor_tensor(out=XS[:, :, 0:n], in0=XS[:, :, 0:n], in1=VEC[:, V_MU:V_MU + 15].unsqueeze(2).broadcast_to([128, 15, n]), op=ALU.mult),
                 r=[tXS, tVEC], w=[tXS])
            S.op("dve", lambda: V.tensor_tensor(out=XS[:, :, 0:n], in0=XS[:, :, 0:n], in1=P[:, :, 1:n + 1], op=ALU.add), r=[tXS, tP], w=[tXS])
            S.op("act", lambda: ACT.copy(out=P[:, :, 0:1], in_=P[:, :, n:n + 1]), r=[tP], w=[tP])

        def front_units(l, c0, n):
            XNa = LR[:, :, :].rearrange("p a b -> p (a b)").bitcast(BF16).rearrange("p (a b) -> p a b", a=8)
            SQe = EN[:, :, :].rearrange("p a b -> p (a b)").bitcast(BF16).rearrange("p (a b) -> p a b", a=8)
            units = []
            u0 = []
            hT = H[:, :, c0:c0 + n]
            u0.append(("act", lambda: ACT.activation(out=SQe[:, :, 0:n], in_=hT, func=AF.Square), htks(c0, n), [tEN], True))
            for c in range(8):
                u0.append(("pe", lambda c=c: PE.matmul(PS[6][:, 0:n], lhsT=ONESC[:], rhs=SQe[:, c, 0:n], start=(c == 0), stop=(c == 7)), [tEN, tC], [tPS[6]], c == 7))
            u0.append(("act", lambda: ACT.activation(out=RS[:, 0:n], in_=PS[6][:, 0:n], func=AF.Ln, bias=CEPS[:, 0:1], scale=1.0), [tPS[6], tC], [tRS], True))
            u0.append(("act", lambda: ACT.activation(out=RS[:, 0:n], in_=RS[:, 0:n], func=AF.Exp, scale=-0.5), [tRS], [tRS], True))
            for c in range(8):
                u0.append(("dve", lambda c=c: V.scalar_tensor_tensor(out=XNa[:, c, 0:n], in0=H[:, c, c0:c0 + n], scalar=VEC[:, V_NM + c:V_NM + c + 1], in1=RS[:, 0:n],
                                                                      op0=ALU.mult, op1=ALU.mult), htks(c0, n) + [tRS, tVEC], [tLR], True))
            units.append(u0)
            fills = [([0, 1, 2, 3], 3), ([4, 5, 6, 7], 4), ([8, 9, 10, 11], 3), ([12, 13, 14], 3), ("ca", 4), ("cg", 3)]
            for blocks, bank in fills:
                if blocks == "ca" or blocks == "cg":
                    cols = [(PR + (0 if blocks == "ca" else 512) + q * 128, 128) for q in range(4)]
                else:
                    cols = [(m * 128, 128 if m < 14 else 32) for m in blocks]
                halves = [cols[0:2], cols[2:]]
                for hi, hcols in enumerate(halves):
                    u = []
                    for qi, (col, M) in enumerate(hcols):
                        sl = hi * 2 + qi
                        for kc in range(8):
                            lastmm = (hi == 1 and qi == len(hcols) - 1 and kc == 7)
                            u.append(("pe", lambda col=col, M=M, sl=sl, kc=kc, bank=bank: PE.matmul(PS[bank][0:M, sl * 128:sl * 128 + n], lhsT=WIN[:, kc, col:col + M], rhs=XNa[:, kc, 0:n],
                                                                                                   start=(kc == 0), stop=(kc == 7)), [tWIN, tLR], [tPS[bank]], lastmm))
                    if hi == 1:
                        if blocks == "ca":
                            u.append(("act", lambda bank=bank: ACT.copy(out=A_[:, :, 0:n], in_=b3(PS[bank][:, :], 4)[:, :, 0:n]), [tPS[bank]], [tA], True))
                        elif blocks == "cg":
                            u.append(("act", lambda bank=bank: ACT.activation(out=CSQ[:, :, 0:n], in_=b3(PS[bank][:, :], 4)[:, :, 0:n], func=AF.Sigmoid), [tPS[bank]], [tCSQ], True))
                        elif len(blocks) == 4:
                            b0 = blocks[0]
                            u.append(("act", lambda bank=bank, b0=b0: ACT.copy(out=P[:, b0:b0 + 4, 1:n + 1], in_=b3(PS[bank][:, :], 4)[:, :, 0:n]), [tPS[bank]], [tP], True))
                        else:
                            u.append(("act", lambda bank=bank: ACT.copy(out=P[:, 12:14, 1:n + 1], in_=b3(PS[bank][:, :], 4)[:, 0:2, 0:n]), [tPS[bank]], [tP], True))
                            u.append(("act", lambda bank=bank: ACT.copy(out=P[0:32, 14, 1:n + 1], in_=PS[bank][0:32, 256:256 + n]), [tPS[bank]], [tP], True))
                    units.append(u)
            def shift_ops(b0, b1):
                nb = b1 - b0
                return [
                    ("dve", lambda: V.tensor_tensor(out=XS[:, b0:b1, 0:n], in0=P[:, b0:b1, 0:n], in1=P[:, b0:b1, 1:n + 1], op=ALU.subtract), [tP], [tXS], True),
                    ("dve", lambda: V.tensor_tensor(out=XS[:, b0:b1, 0:n], in0=XS[:, b0:b1, 0:n], in1=VEC[:, V_MU + b0:V_MU + b1].unsqueeze(2).broadcast_to([128, nb, n]), op=ALU.mult),
                     [tXS, tVEC], [tXS], True),
                    ("dve", lambda: V.tensor_tensor(out=XS[:, b0:b1, 0:n], in0=XS[:, b0:b1, 0:n], in1=P[:, b0:b1, 1:n + 1], op=ALU.add), [tXS, tP], [tXS], True),
                ]
            units[7].extend(shift_ops(0, 12))
            tail = []
            tail.append(("dve", lambda: V.tensor_tensor(out=UH[:, :, 30:30 + n], in0=CSQ[:, :, 0:n], in1=A_[:, :, 0:n], op=ALU.mult), [tCSQ, tA], [tUH], True))
            tail.extend(shift_ops(12, 15))
            tail.append(("act", lambda: ACT.copy(out=P[:, :, 0:1], in_=P[:, :, n:n + 1]), [tP], [tP], True))
            return units, tail

        def emit_unit(u):
            for e_, fn_, r_, w_, sig_ in u:
                S.op(e_, fn_, r=r_, w=w_, sig=sig_)

        def mid(l, c0, n, has_state, side=None):
            R = nrounds_for(n)
            rr = XS[:, 0:4, 0:n]
            kx = XS[:, 4:8, 0:n]
            S.op("act", lambda: ACT.activation(out=TWX[0:64, 0:n], in_=XS[0:64, 12, 0:n], func=AF.Tanh), r=[tXS], w=[tTWX])
            S.op("act", lambda: ACT.copy(out=TWX[64:128, 0:n], in_=XS[64:128, 12, 0:n]), r=[tXS], w=[tTWX])
            S.op("act", lambda: ACT.activation(out=SG[:, 0, 0:n], in_=XS[:, 13, 0:n], func=AF.Sigmoid), r=[tXS], w=[tSG])
            S.op("act", lambda: ACT.activation(out=SG[0:32, 1, 0:n], in_=XS[0:32, 14, 0:n], func=AF.Sigmoid), r=[tXS], w=[tSG])
            for j in range(4):
                S.op("pe", lambda j=j: PE.matmul(PS[0][:, j * 128:j * 128 + n], lhsT=LW[:, j * 128:(j + 1) * 128], rhs=TWX[:, 0:n], start=True, stop=True),
                     r=[tLW, tTWX], w=[tPS[0]], sig=(j == 3))
            for j in range(4):
                S.op("pe", lambda j=j: PE.matmul(PS[1][:, j * 128:j * 128 + n], lhsT=LA[:, j * 128:(j + 1) * 128], rhs=TWX[:, 0:n], start=True, stop=True),
                     r=[tLA, tTWX], w=[tPS[1]], sig=(j == 3))
            for j in range(4):
                S.op("act", lambda j=j: ACT.activation(out=SW[:, j, 0:n], in_=PS[0][:, j * 128:j * 128 + n], func=AF.Sigmoid, bias=VEC[:, V_W0 + j:V_W0 + j + 1], scale=1.0),
                     r=[tPS[0], tVEC], w=[tSW])
            for j in range(4):
                S.op("act", lambda j=j: ACT.activation(out=A_[:, j, 0:n], in_=PS[1][:, j * 128:j * 128 + n], func=AF.Sigmoid, bias=VEC[:, V_A0 + j:V_A0 + j + 1], scale=1.0),
                     r=[tPS[1], tVEC], w=[tA])
            for j in range(4):
                S.op("dve", lambda j=j: V.tensor_tensor_scan(out=LR[:, j, 0:n], data0=ONESF[:, 0:n], data1=SW[:, j, 0:n], initial=0.0, op0=ALU.mult, op1=ALU.add),
                     r=[tSW, tC], w=[tLR])
            S.op("act", lambda: ACT.activation(out=EN[:, :, 0:n], in_=LR[:, :, 0:n], func=AF.Exp, scale=DECAY), r=[tLR], w=[tEN])
            S.op("act", lambda: ACT.activation(out=EP[:, :, 0:n], in_=LR[:, :, 0:n], func=AF.Exp, scale=-DECAY), r=[tLR], w=[tEP])
            S.op("dve", lambda: V.tensor_tensor(out=SW[:, :, 0:n], in0=LR[:, :, 0:n], in1=SW[:, :, 0:n], op=ALU.subtract), r=[tLR, tSW], w=[tSW])
            S.op("act", lambda: ACT.activation(out=SW[:, :, 0:n], in_=SW[:, :, 0:n], func=AF.Exp, scale=-DECAY), r=[tSW], w=[tSW])
            S.op("dve", lambda: V.tensor_tensor(out=KK[:, :, 0:n], in0=kx, in1=VEC[:, V_KK:V_KK + 4].unsqueeze(2).broadcast_to([128, 4, n]), op=ALU.mult),
                 r=[tXS, tVEC], w=[tKK])
            S.op("act", lambda: ACT.activation(out=KK2[:, :, 0:n], in_=KK[:, :, 0:n], func=AF.Square), r=[tKK], w=[tKK2])
            for j in range(4):
                S.op("pe", lambda j=j: PE.matmul(PS[3][:, j * 128:j * 128 + n], lhsT=BONES[:], rhs=KK2[:, j, 0:n], start=True, stop=True),
                     r=[tKK2, tC], w=[tPS[3]], sig=(j == 3))
            S.op("act", lambda: ACT.activation(out=T1[:, :, 0:n], in_=b3(PS[3][:, :], 4)[:, :, 0:n], func=AF.Ln, bias=CEPS[:, 3:4], scale=1.0), r=[tPS[3], tC], w=[tT1])
            S.op("act", lambda: ACT.activation(out=T1[:, :, 0:n], in_=T1[:, :, 0:n], func=AF.Exp, scale=-0.5), r=[tT1], w=[tT1])
            S.op("dve", lambda: V.tensor_tensor(out=KK[:, :, 0:n], in0=KK[:, :, 0:n], in1=T1[:, :, 0:n], op=ALU.mult), r=[tKK, tT1], w=[tKK])
            S.op("dve", lambda: V.scalar_tensor_tensor(out=T1[:, :, 0:n], in0=A_[:, :, 0:n], scalar=-1.0, in1=VEC[:, V_KA:V_KA + 4].unsqueeze(2).broadcast_to([128, 4, n]),
                                                       op0=ALU.add, op1=ALU.mult), r=[tA, tVEC], w=[tT1])
            S.op("dve", lambda: V.scalar_tensor_tensor(out=kx, in0=T1[:, :, 0:n], scalar=1.0, in1=kx, op0=ALU.add, op1=ALU.mult), r=[tT1, tXS], w=[tXS])
            S.op("dve", lambda: V.tensor_tensor(out=T1[:, :, 0:n], in0=KK[:, :, 0:n], in1=A_[:, :, 0:n], op=ALU.mult), r=[tKK, tA], w=[tT1])
            S.op("dve", lambda: V.scalar_tensor_tensor(out=AR[:, :, 0, 0:n], in0=KK[:, :, 0:n], scalar=-1.0, in1=SW[:, :, 0:n], op0=ALU.mult, op1=ALU.mult),
                 r=[tKK, tSW], w=[tAR])
            S.op("dve", lambda: V.tensor_tensor(out=BTU[:, :, 0:n], in0=T1[:, :, 0:n], in1=EN[:, :, 0:n], op=ALU.mult), r=[tT1, tEN], w=[tBTU])
            S.op("dve", lambda: V.tensor_tensor(out=KTU[:, :, 0:n], in0=kx, in1=EN[:, :, 0:n], op=ALU.mult), r=[tXS, tEN], w=[tKTU])
            S.op("dve", lambda: V.tensor_tensor(out=AR[:, :, 1, 0:n], in0=rr, in1=EP[:, :, 0:n], op=ALU.mult), r=[tXS, tEP], w=[tAR])
            for i in range(2):
                ps_ = slice(64 * i, 64 * i + 64)
                S.op("dve", lambda i=i, ps_=ps_: V.tensor_copy(out=ATZ[ps_, :, i, 0:n], in_=AR[ps_, :, 0, 0:n]), r=[tAR], w=[tATZ])
                S.op("dve", lambda i=i, ps_=ps_: V.tensor_copy(out=BTZ[ps_, :, i, 0:n], in_=BTU[ps_, :, 0:n]), r=[tBTU], w=[tBTZ])
                S.op("dve", lambda i=i, ps_=ps_: V.tensor_copy(out=KTZ[ps_, :, i, 0:n], in_=KTU[ps_, :, 0:n]), r=[tKTU], w=[tKTZ])
            S.op("dve", lambda: V.tensor_tensor(out=T1[:, :, 0:n], in0=kx, in1=VEC[:, V_RK:V_RK + 4].unsqueeze(2).broadcast_to([128, 4, n]), op=ALU.mult),
                 r=[tXS, tVEC], w=[tT1])
            S.op("dve", lambda: V.tensor_tensor(out=T2B[:, :, 0:n], in0=T1[:, :, 0:n], in1=rr, op=ALU.mult), r=[tT1, tXS], w=[tT2B])
            for j in range(4):
                S.op("pe", lambda j=j: PE.transpose(out=PS[4][0:n, j * 128:(j + 1) * 128], in_=XS[:, 8 + j, 0:n], identity=IDF[:]), r=[tXS, tC], w=[tPS[4]], sig=(j == 3))
            S.op("act", lambda: ACT.copy(out=VT[0:n, :], in_=PS[4][0:n, :]), r=[tPS[4]], w=[tVT])
            for j in range(4):
                S.op("pe", lambda j=j: PE.transpose(out=PSB[0:n, j * 128:(j + 1) * 128], in_=BTU[:, j, 0:n], identity=IDB[:]), r=[tBTU, tC], w=[tPSB], sig=(False))
            for j in range(4):
                S.op("pe", lambda j=j: PE.transpose(out=PSB[0:n, 512 + j * 128:512 + (j + 1) * 128], in_=KTU[:, j, 0:n], identity=IDB[:]), r=[tKTU, tC], w=[tPSB], sig=(j == 3))
            S.op("act", lambda: ACT.copy(out=BTT[0:n, :, :], in_=PSB[0:n, :].rearrange("p (a b) -> p a b", a=2)), r=[tPSB], w=[tBTT])
            CVA = KK
            CVBf = SW
            S.op("act", lambda: ACT.copy(out=UHB[:, :, 0:30 + n], in_=UH[:, :, 0:30 + n]), r=[tUH], w=[tUHB])
            for j in range(4):
                for ti in range(NPE):
                    S.op("pe", lambda j=j, ti=ti: PE.matmul(PS[6][:, j * 128:j * 128 + n], lhsT=DG[:, j * NPE + ti, :], rhs=UHB[:, j, ti:ti + n], start=(ti == 0), stop=(ti == NPE - 1)),
                         r=[tDG, tUHB], w=[tPS[6]], sig=(j == 3 and ti == NPE - 1))
            S.op("act", lambda: ACT.copy(out=CVBf[:, :, 0:n], in_=b3(PS[6][:, :], 4)[:, :, 0:n]), r=[tPS[6]], w=[tSW] + tCVb)
            conv_ops = []
            for k, tap in enumerate(range(NPE, CW)):
                for j in range(4):
                    cwb = V_CW + j * CW
                    if k == 0:
                        conv_ops.append((lambda j=j, cwb=cwb, tap=tap: V.tensor_scalar(out=CVA[:, j, 0:n], in0=UH[:, j, tap:tap + n], scalar1=VEC[:, cwb + tap:cwb + tap + 1],
                                                                                     scalar2=VEC[:, V_CB + j:V_CB + j + 1], op0=ALU.mult, op1=ALU.add),
                                         [tUH, tVEC], ([tKK] if j == 0 else []) + [tCVa[j]]))
                    else:
                        tgt, ttk = (CVA, tCVa) if k % 2 == 0 else (CVBf, tCVb)
                        conv_ops.append((lambda j=j, cwb=cwb, tap=tap, tgt=tgt: V.scalar_tensor_tensor(out=tgt[:, j, 0:n], in0=UH[:, j, tap:tap + n], scalar=VEC[:, cwb + tap:cwb + tap + 1],
                                                                                                      in1=tgt[:, j, 0:n], op0=ALU.mult, op1=ALU.add),
                                         [tUH, tVEC, ttk[j]], [ttk[j]]))
            conv_ops.reverse()

            def drain(k):
                for _ in range(k):
                    if not conv_ops:
                        return
                    fn_, r_, w_ = conv_ops.pop()
                    S.op("dve", fn_, r=r_, w=w_)
            side_units, side_tail = (side if side is not None else ([], []))
            side_units = list(side_units)
            side_units.reverse()
            if side_units:
                emit_unit(side_units.pop())
            for g in range(2):
                heads = [(2 * g + hh // 2, hh % 2) for hh in range(4)]
                for hh, (j, i) in enumerate(heads):
                    bk = hh // 2
                    S.op("pe", lambda hh=hh, j=j, i=i, bk=bk: PE.matmul(b4(PS[0 + bk][:, :], 2, 2)[0:n, hh % 2, :, 0:n], lhsT=BTZ[:, j, i, 0:n], rhs=AR[:, j, :, 0:n], start=True, stop=True),
                         r=[tBTZ, tAR], w=[tPS[0 + bk]], sig=(False))
                    S.op("pe", lambda hh=hh, j=j, i=i, bk=bk: PE.matmul(b4(PS[2 + bk][:, :], 2, 2)[0:n, hh % 2, :, 0:n], lhsT=KTZ[:, j, i, 0:n], rhs=AR[:, j, :, 0:n], start=True, stop=True),
                         r=[tKTZ, tAR], w=[tPS[2 + bk]], sig=(False))
                    S.op("pe", lambda hh=hh, j=j, i=i: PE.matmul(b3(PS[4][:, :], 4)[0:n, hh, 0:n], lhsT=ATZ[:, j, i, 0:n], rhs=BTU[:, j, 0:n], start=True, stop=True),
                         r=[tATZ, tBTU], w=[tPS[4]], sig=(hh == 3))
                mk_ur = MKUR[0:n, :, 0:n].unsqueeze(1).broadcast_to([n, 2, 2, n])
                for bk in range(2):
                    S.op("dve", lambda bk=bk: V.tensor_tensor(out=AMA[0:n, 2 * bk:2 * bk + 2, :, 0:n], in0=b4(PS[0 + bk][:, :], 2, 2)[0:n, :, :, 0:n], in1=mk_ur, op=ALU.mult),
                         r=[tPS[0 + bk], tC], w=[tAMA])
                    S.op("dve", lambda bk=bk: V.tensor_tensor(out=AMK[0:n, 2 * bk:2 * bk + 2, :, 0:n], in0=b4(PS[2 + bk][:, :], 2, 2)[0:n, :, :, 0:n], in1=mk_ur, op=ALU.mult),
                         r=[tPS[2 + bk], tC], w=[tAMK])
                S.op("dve", lambda: V.tensor_tensor(out=NQ[0][0:n, :, 0:n], in0=b3(PS[4][:, :], 4)[0:n, :, 0:n], in1=MKSL[0:n, 0:n].unsqueeze(1).broadcast_to([n, 4, n]), op=ALU.mult),
                     r=[tPS[4], tC], w=[tNQ[0]])
                idb_bc = IDB[0:n, 0:n].unsqueeze(1).broadcast_to([n, 4, n])
                S.op("dve", lambda: V.tensor_tensor(out=ACC[0][0:n, :, 0:n], in0=AMA[0:n, :, 0, 0:n], in1=idb_bc, op=ALU.add), r=[tAMA, tC], w=[tACC[0]])
                S.op("act", lambda: ACT.copy(out=QT[0][0:n, :, 0:n], in_=AMA[0:n, :, 0, 0:n]), r=[tAMA], w=[tQT[0]])
                cur = 0
                ac = 0
                MMb = [MM, KK2]
                tMMb = [tMM, tKK2]

                def emit_acc(rd_, ac_):
                    mmb, tmmb = MMb[rd_ % 2], tMMb[rd_ % 2]
                    for hh in range(4):
                        S.op("pe", lambda hh=hh: PE.matmul(b3(PS[2][:, :], 4)[0:n, hh, 0:n], lhsT=mmb[0:n, hh, 0:n], rhs=ACC[ac_][0:n, hh, 0:n], start=True, stop=True),
                             r=[tmmb, tACC[ac_]], w=[tPS[2]], sig=(hh == 3))
                    S.op("act", lambda: ACT.copy(out=ACC[1 - ac_][0:n, :, 0:n], in_=b3(PS[2][:, :], 4)[0:n, :, 0:n]), r=[tPS[2]], w=[tACC[1 - ac_]])

                for rd in range(1, R):
                    nxt = 1 - cur
                    last = (rd == R - 1)
                    mmb, tmmb = MMb[rd % 2], tMMb[rd % 2]
                    for hh in range(4):
                        S.op("pe", lambda hh=hh, cur=cur: PE.matmul(b3(PS[0][:, :], 4)[0:n, hh, 0:n], lhsT=QT[cur][0:n, hh, 0:n], rhs=NQ[cur][0:n, hh, 0:n], start=True, stop=True),
                             r=[tQT[cur], tNQ[cur]], w=[tPS[0]], sig=(hh == 3 and last))
                    if not last:
                        for hh in range(4):
                            S.op("pe", lambda hh=hh, cur=cur: PE.matmul(b3(PS[1][:, :], 4)[0:n, hh, 0:n], lhsT=NQ[cur][0:n, hh, 0:n], rhs=QT[cur][0:n, hh, 0:n], start=True, stop=True),
                                 r=[tQT[cur], tNQ[cur]], w=[tPS[1]], sig=(hh == 3))
                        S.op("act", lambda nxt=nxt: ACT.copy(out=NQ[nxt][0:n, :, 0:n], in_=b3(PS[0][:, :], 4)[0:n, :, 0:n]), r=[tPS[0]], w=[tNQ[nxt]])
                        S.op("dve", lambda nxt=nxt: V.tensor_copy(out=QT[nxt][0:n, :, 0:n], in_=b3(PS[1][:, :], 4)[0:n, :, 0:n]), r=[tPS[1]], w=[tQT[nxt]])
                        S.op("dve", lambda nxt=nxt, mmb=mmb: V.tensor_tensor(out=mmb[0:n, :, 0:n], in0=NQ[nxt][0:n, :, 0:n], in1=idb_bc, op=ALU.add), r=[tNQ[nxt], tC], w=[tmmb])
                    else:
                        S.op("dve", lambda mmb=mmb: V.tensor_tensor(out=mmb[0:n, :, 0:n], in0=b3(PS[0][:, :], 4)[0:n, :, 0:n], in1=idb_bc, op=ALU.add), r=[tPS[0], tC], w=[tmmb])
                    drain(5)
                    if side_units:
                        emit_unit(side_units.pop())
                    if rd >= 2:
                        emit_acc(rd - 1, ac)
                        ac = 1 - ac
                    cur = nxt
                emit_acc(R - 1, ac)
                ac = 1 - ac
                TTt = ACC[ac]
                tTT = tACC[ac]
                for jj in range(2):
                    j = 2 * g + jj
                    if has_state:
                        S.op("pe", lambda j=j, jj=jj: PE.matmul(PS[3][0:n, jj * 128:(jj + 1) * 128], lhsT=AR[:, j, 0, 0:n], rhs=STB[:, j, :], start=True, stop=False),
                             r=[tAR, tSTB], w=[tPS[3]], sig=(False))
                    for i in range(2):
                        hh = jj * 2 + i
                        hc = (j * 2 + i) * 64
                        S.op("pe", lambda hh=hh, hc=hc: PE.matmul(PS[3][0:n, hh * 64:hh * 64 + 64], lhsT=AMK[0:n, hh, 0, 0:n], rhs=VT[0:n, hc:hc + 64], start=(not has_state), stop=True),
                             r=[tAMK, tVT], w=[tPS[3]], sig=(jj == 1 and i == 1))
                S.op("act", lambda: ACT.copy(out=XB[0:n, :], in_=PS[3][0:n, 0:256]), r=[tPS[3]], w=[tXB])
                drain(3)
                for hh in range(4):
                    S.op("pe", lambda hh=hh: PE.matmul(PS[4][0:n, hh * 64:hh * 64 + 64], lhsT=TTt[0:n, hh, 0:n], rhs=XB[0:n, hh * 64:hh * 64 + 64], start=True, stop=True),
                         r=[tTT, tXB], w=[tPS[4]], sig=(hh == 3))
                S.op("act", lambda: ACT.copy(out=UB[0:n, :], in_=PS[4][0:n, 0:256]), r=[tPS[4]], w=[tUB])
                for jj in range(2):
                    j = 2 * g + jj
                    oc = g * 256 + jj * 128
                    if has_state:
                        S.op("pe", lambda j=j, oc=oc: PE.matmul(PS[5][0:n, oc:oc + 128], lhsT=AR[:, j, 1, 0:n], rhs=STB[:, j, :], start=True, stop=False),
                             r=[tAR, tSTB], w=[tPS[5]], sig=(False))
                    for i in range(2):
                        hh = jj * 2 + i
                        hc = (j * 2 + i) * 64
                        S.op("pe", lambda hh=hh, oc=oc, i=i: PE.matmul(PS[5][0:n, oc + i * 64:oc + i * 64 + 64], lhsT=AMA[0:n, hh, 1, 0:n], rhs=UB[0:n, hh * 64:hh * 64 + 64],
                                                                      start=(not has_state), stop=False), r=[tAMA, tUB], w=[tPS[5]], sig=(False))
                        S.op("pe", lambda hh=hh, oc=oc, i=i, hc=hc: PE.matmul(PS[5][0:n, oc + i * 64:oc + i * 64 + 64], lhsT=AMK[0:n, hh, 1, 0:n], rhs=VT[0:n, hc:hc + 64],
                                                                             start=False, stop=True), r=[tAMK, tVT], w=[tPS[5]], sig=(jj == 1 and i == 1))
                for jj in range(2):
                    j = 2 * g + jj
                    S.op("pe", lambda j=j, jj=jj: PE.matmul(PS[6][:, jj * 128:(jj + 1) * 128], lhsT=BTT[0:n, 0, j * 128:(j + 1) * 128], rhs=UB[0:n, jj * 128:(jj + 1) * 128], start=True, stop=False),
                         r=[tBTT, tUB], w=[tPS[6]], sig=(False))
                    S.op("pe", lambda j=j, jj=jj: PE.matmul(PS[6][:, jj * 128:(jj + 1) * 128], lhsT=BTT[0:n, 1, j * 128:(j + 1) * 128], rhs=VT[0:n, j * 128:(j + 1) * 128], start=False, stop=True),
                         r=[tBTT, tVT], w=[tPS[6]], sig=(jj == 1))
                for i in range(2):
                    ps_ = slice(64 * i, 64 * i + 64)
                    fs_ = slice(64 * i, 64 * i + 64)
                    stv = ST[ps_, 2 * g:2 * g + 2, fs_]
                    psv = b3(PS[6][:, 0:256], 2)[ps_, :, fs_]
                    pcv = EP[ps_, 2 * g:2 * g + 2, n - 1:n].broadcast_to([64, 2, 64])
                    if has_state:
                        S.op("dve", lambda stv=stv, psv=psv: V.tensor_tensor(out=stv, in0=stv, in1=psv, op=ALU.add), r=[tST, tPS[6]], w=[tST])
                        S.op("dve", lambda stv=stv, pcv=pcv: V.tensor_tensor(out=stv, in0=stv, in1=pcv, op=ALU.mult), r=[tST, tEP], w=[tST])
                    else:
                        S.op("dve", lambda stv=stv, psv=psv, pcv=pcv: V.tensor_tensor(out=stv, in0=psv, in1=pcv, op=ALU.mult), r=[tPS[6], tEP], w=[tST])
            S.op("act", lambda: ACT.copy(out=STB[:], in_=ST[:]), r=[tST], w=[tSTB])
            drain(1000)
            S.op("dve", lambda: V.tensor_tensor(out=CV[:, :, 0:n], in0=CVA[:, :, 0:n], in1=CVBf[:, :, 0:n], op=ALU.add), r=tCVa + tCVb, w=[tCV, tSW] + tCVa + tCVb)
            S.op("act", lambda: ACT.copy(out=TMPH[:, :, :], in_=UH[:, :, n:n + 30]), r=[tUH], w=[tTMPH])
            S.op("act", lambda: ACT.copy(out=UH[:, :, 0:30], in_=TMPH[:, :, :]), r=[tTMPH], w=[tUH])
            assert not side_units, len(side_units)
            emit_unit(side_tail)
            for j in range(4):
                S.op("pe", lambda j=j: PE.matmul(PS[6][0:n, 2 * j:2 * j + 2], lhsT=T2B[:, j, 0:n], rhs=HIND[:, :], start=True, stop=True), r=[tT2B, tC], w=[tPS[6]], sig=(j == 3))
            S.op("dve", lambda: V.tensor_copy(out=STAT[0:n, 5, :], in_=PS[6][0:n, 0:8]), r=[tPS[6]], w=[tSTAT])
            S.op("pe", lambda: PE.matmul(PS7[0:n, :], lhsT=SG[:, 0, 0:n], rhs=LG[:, 0, :], start=True, stop=False), r=[tSG, tLG], w=[tPSB], sig=(False))
            S.op("pe", lambda: PE.matmul(PS7[0:n, :], lhsT=SG[:, 1, 0:n], rhs=LG[:, 1, :], start=False, stop=True), r=[tSG, tLG], w=[tPSB], sig=(True))

        def back(l, c0, n):
            S.rec = []
            o3 = b3(PS[5][:, :], 8)[0:n]
            S.op("act", lambda: ACT.activation(out=OSQ[0:n, :], in_=PS[5][0:n, :], func=AF.Square), r=[tPS[5]], w=[tOSQ])
            S.op("dve", lambda: V.tensor_reduce(out=STAT[0:n, 0, :], in_=o3, axis=AX.X, op=ALU.add), r=[tPS[5]], w=[tSTAT])
            S.op("dve", lambda: V.tensor_reduce(out=STAT[0:n, 1, :], in_=b3(OSQ[:, :], 8)[0:n], axis=AX.X, op=ALU.add), r=[tOSQ], w=[tSTAT])
            S.op("dve", lambda: V.tensor_scalar(out=STAT[0:n, 0, :], in0=STAT[0:n, 0, :], scalar1=1.0 / 64.0, scalar2=None, op0=ALU.mult), r=[tSTAT], w=[tSTAT])
            S.op("dve", lambda: V.tensor_tensor(out=STAT[0:n, 2, :], in0=STAT[0:n, 0, :], in1=STAT[0:n, 0, :], op=ALU.mult), r=[tSTAT], w=[tSTAT])
            S.op("dve", lambda: V.scalar_tensor_tensor(out=STAT[0:n, 3, :], in0=STAT[0:n, 1, :], scalar=1.0 / 64.0, in1=STAT[0:n, 2, :], op0=ALU.mult, op1=ALU.subtract),
                 r=[tSTAT], w=[tSTAT])
            S.op("act", lambda: ACT.activation(out=STAT[0:n, 4, :], in_=STAT[0:n, 3, :], func=AF.Ln, bias=CEPS[0:n, 1:2], scale=1.0), r=[tSTAT, tC], w=[tSTAT])
            S.op("act", lambda: ACT.activation(out=STAT[0:n, 4, :], in_=STAT[0:n, 4, :], func=AF.Exp, scale=-0.5), r=[tSTAT], w=[tSTAT])
            ot3 = b3(OTM[:, :], 8)[0:n]
            S.op("dve", lambda: V.tensor_tensor(out=ot3, in0=o3, in1=STAT[0:n, 0, :].unsqueeze(2).broadcast_to([n, 8, 64]), op=ALU.subtract), r=[tPS[5], tSTAT], w=[tOTM])
            S.op("dve", lambda: V.tensor_tensor(out=ot3, in0=ot3, in1=STAT[0:n, 4, :].unsqueeze(2).broadcast_to([n, 8, 64]), op=ALU.mult), r=[tOTM, tSTAT], w=[tOTM])
            S.op("dve", lambda: V.tensor_tensor(out=OTM[0:n, :], in0=OTM[0:n, :], in1=GNG[0:n, :], op=ALU.mult), r=[tOTM, tGN], w=[tOTM])
            S.op("dve", lambda: V.tensor_tensor(out=OTM[0:n, :], in0=OTM[0:n, :], in1=GNB[0:n, :], op=ALU.add), r=[tOTM, tGN], w=[tOTM])
            S.op("dve", lambda: V.tensor_tensor(out=b3(OSQ[:, :], 8)[0:n], in0=b3(VT[:, :], 8)[0:n], in1=STAT[0:n, 5, :].unsqueeze(2).broadcast_to([n, 8, 64]), op=ALU.mult),
                 r=[tVT, tSTAT], w=[tOSQ])
            S.op("dve", lambda: V.tensor_tensor(out=OTM[0:n, :], in0=OTM[0:n, :], in1=OSQ[0:n, :], op=ALU.add), r=[tOTM, tOSQ], w=[tOTM])
            S.op("dve", lambda: V.tensor_tensor(out=YRT[0:n, :], in0=OTM[0:n, :], in1=PS7[0:n, :], op=ALU.mult), r=[tOTM, tPSB], w=[tYRT])
            for j in range(4):
                S.op("pe", lambda j=j: PE.transpose(out=PSB[:, j * 128:j * 128 + n], in_=YRT[0:n, j * 128:(j + 1) * 128], identity=IDB[0:n, 0:n]), r=[tYRT, tC], w=[tPSB], sig=(j == 3))
            S.op("act", lambda: ACT.copy(out=MIX[:, 0:4, 0:n], in_=b3(PSB[:, 0:512], 4)[:, :, 0:n]), r=[tPSB], w=[tMIX])
            listA = S.rec
            S.rec = []
            S.op("act", lambda: ACT.activation(out=CSQ[:, :, 0:n], in_=CV[:, :, 0:n], func=AF.Square), r=[tCV], w=[tCSQ])
            for j in range(4):
                S.op("pe", lambda j=j: PE.matmul(PS[0][:, 0:n], lhsT=ONF[:], rhs=CV[:, j, 0:n], start=(j == 0), stop=(j == 3)), r=[tCV, tC], w=[tPS[0]], sig=(j == 3))
            for j in range(4):
                S.op("pe", lambda j=j: PE.matmul(PS[1][:, 0:n], lhsT=ONF[:], rhs=CSQ[:, j, 0:n], start=(j == 0), stop=(j == 3)), r=[tCSQ, tC], w=[tPS[1]], sig=(j == 3))
            S.op("act", lambda: ACT.copy(out=LNS[:, 0, 0:n], in_=PS[0][:, 0:n]), r=[tPS[0]], w=[tLNS])
            S.op("dve", lambda: V.tensor_tensor(out=LNS[:, 1, 0:n], in0=LNS[:, 0, 0:n], in1=LNS[:, 0, 0:n], op=ALU.mult), r=[tLNS], w=[tLNS])
            S.op("dve", lambda: V.tensor_tensor(out=LNS[:, 1, 0:n], in0=PS[1][:, 0:n], in1=LNS[:, 1, 0:n], op=ALU.subtract), r=[tLNS, tPS[1]], w=[tLNS])
            S.op("act", lambda: ACT.activation(out=LNS[:, 2, 0:n], in_=LNS[:, 1, 0:n], func=AF.Ln, bias=CEPS[:, 2:3], scale=1.0), r=[tLNS, tC], w=[tLNS])
            S.op("act", lambda: ACT.activation(out=LNS[:, 2, 0:n], in_=LNS[:, 2, 0:n], func=AF.Exp, scale=-0.5), r=[tLNS], w=[tLNS])
            S.op("dve", lambda: V.tensor_tensor(out=CV[:, :, 0:n], in0=CV[:, :, 0:n], in1=LNS[:, 0, 0:n].unsqueeze(1).broadcast_to([128, 4, n]), op=ALU.subtract), r=[tCV, tLNS], w=[tCV])
            S.op("dve", lambda: V.tensor_tensor(out=CV[:, :, 0:n], in0=CV[:, :, 0:n], in1=LNS[:, 2, 0:n].unsqueeze(1).broadcast_to([128, 4, n]), op=ALU.mult), r=[tCV, tLNS], w=[tCV])
            for j in range(4):
                S.op("act", lambda j=j: ACT.activation(out=MIX[:, 4 + j, 0:n], in_=CV[:, j, 0:n], func=AF.Silu, bias=VEC[:, V_CBB + j:V_CBB + j + 1], scale=VEC[:, V_CG + j:V_CG + j + 1]),
                     r=[tCV, tVEC], w=[tMIX])
            listB = S.rec
            S.rec = None
            ia = ib = 0
            while ia < len(listA) or ib < len(listB):
                if ia < len(listA):
                    e_, fn_, r_, w_, sig_ = listA[ia]; ia += 1
                    S.op(e_, fn_, r=r_, w=w_, sig=sig_)
                    while ia < len(listA) and listA[ia][0] == "pe" and e_ == "pe":
                        e_, fn_, r_, w_, sig_ = listA[ia]; ia += 1
                        S.op(e_, fn_, r=r_, w=w_, sig=sig_)
                if ib < len(listB):
                    e_, fn_, r_, w_, sig_ = listB[ib]; ib += 1
                    S.op(e_, fn_, r=r_, w=w_, sig=sig_)
                    while ib < len(listB) and listB[ib][0] == "pe" and e_ == "pe":
                        e_, fn_, r_, w_, sig_ = listB[ib]; ib += 1
                        S.op(e_, fn_, r=r_, w=w_, sig=sig_)
            for m in range(8):
                bank = m // 4
                sl = m % 4
                for kc in range(8):
                    S.op("pe", lambda m=m, bank=bank, sl=sl, kc=kc: PE.matmul(PS[bank][:, sl * 128:sl * 128 + n], lhsT=WOUT[:, kc, m * 128:(m + 1) * 128], rhs=MIX[:, kc, 0:n],
                                                                            start=(kc == 0), stop=(kc == 7)), r=[tWOUT, tMIX], w=[tPS[bank]], sig=(kc == 7 and m % 4 == 3))
            for bank in range(2):
                hv = H[:, bank * 4:bank * 4 + 4, c0:c0 + n]
                S.op("dve", lambda bank=bank, hv=hv: V.tensor_tensor(out=hv, in0=hv, in1=b3(PS[bank][:, :], 4)[:, :, 0:n], op=ALU.add),
                     r=htks(c0, n) + [tPS[bank]], w=htks(c0, n))

        def phaseA_tile(l, c0, n, has_state):
            front(l, c0, n)
            mid(l, c0, n, has_state)
            back(l, c0, n)

        out_toks = []

        def emit_states(l, wkv_dst, sh_dst, cv_dst, n):
            with nc.allow_non_contiguous_dma(reason="small state vectors"):
                out_toks.append(S.dma("sp", lambda e: e.dma_start(out=sh_dst[0:1792].rearrange("(b p) -> p b", p=128), in_=P[:, 0:14, 0]), r=[tP]))
                out_toks.append(S.dma("sp", lambda e: e.dma_start(out=sh_dst[1792:1824].rearrange("(b p) -> p b", p=32), in_=P[0:32, 14:15, 0]), r=[tP]))
            for j in range(4):
                S.op("pe", lambda j=j: PE.transpose(out=PS[0][0:30, j * 128:(j + 1) * 128], in_=UH[:, j, 0:30], identity=IDF[:]), r=[tUH, tC], w=[tPS[0]], sig=(j == 3))
            S.op("act", lambda: ACT.copy(out=IOT[0:30, 0:512], in_=PS[0][0:30, :]), r=[tPS[0]], w=[tIOT])
            out_toks.append(S.dma("sp", lambda e: e.dma_start(out=cv_dst, in_=IOT[0:30, 0:512]), r=[tIOT]))
            for j in range(4):
                S.op("pe", lambda j=j: PE.transpose(out=PS[1][:, j * 128:(j + 1) * 128], in_=ST[:, j, :], identity=IDF[:]), r=[tST, tC], w=[tPS[1]], sig=(j == 3))
            S.op("act", lambda: ACT.copy(out=OSQ[:, :], in_=PS[1][:, :]), r=[tPS[1]], w=[tOSQ])
            for i in range(2):
                out_toks.append(S.dma("sp", lambda e, i=i: e.dma_start(out=wkv_dst.rearrange("(j i) v k -> i v j k", i=2)[i], in_=b3(OSQ[:, :], 4)[64 * i:64 * i + 64, :, 64 * i:64 * i + 64]),
                                      r=[tOSQ]))

        def load_sample_state(l, j):
            S.op("pool", lambda: POOL.memset(OSQ[:], 0.0), w=[tOSQ])
            for i in range(2):
                S.dma("sp", lambda e, i=i: e.dma_start(out=b3(OSQ[:, :], 4)[64 * i:64 * i + 64, :, 64 * i:64 * i + 64], in_=swkv[l, j].rearrange("(j i) v k -> i v j k", i=2)[i]), w=[tOSQ])
            for jp in range(4):
                S.op("pe", lambda jp=jp: PE.transpose(out=PS[1][:, jp * 128:(jp + 1) * 128], in_=OSQ[:, jp * 128:(jp + 1) * 128], identity=IDF[:]), r=[tOSQ, tC], w=[tPS[1]], sig=(jp == 3))
            S.op("act", lambda: ACT.copy(out=ST[:], in_=b3(PS[1][:, :], 4)), r=[tPS[1]], w=[tST])
            S.op("act", lambda: ACT.copy(out=STB[:], in_=ST[:]), r=[tST], w=[tSTB])
            with nc.allow_non_contiguous_dma(reason="small state vectors"):
                S.dma("sp", lambda e: e.dma_start(out=P[:, 0:14, 0], in_=sshift[l, j, 0:1792].rearrange("(b p) -> p b", p=128)), w=[tP])
                S.dma("sp", lambda e: e.dma_start(out=P[0:32, 14:15, 0], in_=sshift[l, j, 1792:1824].rearrange("(b p) -> p b", p=32)), w=[tP])
            S.dma("sp", lambda e: e.dma_start(out=IOT[0:30, 0:512], in_=sconv[l, j]), w=[tIOT])
            for jb in range(4):
                S.op("pe", lambda jb=jb: PE.transpose(out=PS[0][:, jb * 128:jb * 128 + 30], in_=IOT[0:30, jb * 128:(jb + 1) * 128], identity=IDF[0:30, 0:30]), r=[tIOT, tC], w=[tPS[0]], sig=(jb == 3))
            S.op("act", lambda: ACT.copy(out=UH[:, :, 0:30], in_=b3(PS[0][:, :], 4)[:, :, 0:30]), r=[tPS[0]], w=[tUH])

        def barrier_all():
            toks = []
            for e in ("pe", "dve", "act", "pool"):
                c = S.cnt[e]
                if c > 0:
                    ep, v = divmod(c - 1, EPOCH)
                    toks.append(((e, ep), v + 1))
            for i in range(NDSEM):
                if S.dcnt[i] > 0:
                    toks.append((("d", i), S.dcnt[i]))
            for e in ("pe", "dve", "act", "pool", "sp"):
                for tok in toks:
                    if tok[0][0] == e:
                        continue
                    S._wait(e, tok)

        def load_layer_weights_old(l):
            for kc in range(8):
                S.dma("pool", lambda e, kc=kc: e.dma_start(out=WIN[:, kc, :], in_=w_in[l, kc * 128:(kc + 1) * 128, :], max_dma_last_dim=4096), w=[tWIN])
            for kc in range(8):
                S.dma("pool", lambda e, kc=kc: e.dma_start(out=WOUT[:, kc, :], in_=w_out[l, kc * 128:(kc + 1) * 128, :], max_dma_last_dim=4096), w=[tWOUT])

        for l in range(DEPTH if STAGE >= 1 else 0):
            if l == 0:
                while w0_pieces:
                    w0_step()
                cur_ring[0] = RINGB
            barrier_all()
            for buf, tk_ in ((ATZ, tATZ), (BTZ, tBTZ), (KTZ, tKTZ), (SG, tSG), (P, tP), (LG, tLG)):
                S.op("pool", (lambda b=buf: POOL.memset(b[:], 0.0)), w=[tk_])
            S.dma("sp", lambda e: e.dma_start(out=VEC[:], in_=vecs[l]), w=[tVEC])
            for j in range(4):
                for ti in range(NPE):
                    S.op("dve", lambda j=j, ti=ti: V.tensor_scalar(out=DG[:, j * NPE + ti, :], in0=IDB[:], scalar1=VEC[:, V_CW + j * CW + ti:V_CW + j * CW + ti + 1], scalar2=None, op0=ALU.mult), r=[tVEC, tC], w=[tDG])
            stream_cast(RINGA, LW[0:64, :], lora_w[l], tLW, 0, 64)
            stream_cast(RINGA, LA[64:128, :], lora_a[l], tLA, 64, 128)
            stream_cast(RINGA, LG[:, 0, :], lora_g[l, 0:128, :], tLG, 0, 128)
            stream_cast(RINGA, LG[0:32, 1, :], lora_g[l, 128:160, :], tLG, 0, 32)
            S.dma("sp", lambda e: e.dma_start(out=GNG[:], in_=gn_g[l:l + 1, :].partition_broadcast(128)), w=[tGN])
            S.dma("sp", lambda e: e.dma_start(out=GNB[:], in_=gn_b[l:l + 1, :].partition_broadcast(128)), w=[tGN])
            S.op("pool", lambda: POOL.memset(ST[:], 0.0), w=[tST])
            S.op("pool", lambda: POOL.memset(STB[:], 0.0), w=[tSTB])
            S.op("pool", lambda: POOL.memset(P[:, :, 0:1], 0.0), w=[tP])
            S.op("pool", lambda: POOL.memset(UH[:, :, 0:30], 0.0), w=[tUH])
            KONLY = os.environ.get("KONLY", "")
            if KONLY != "128":
                phaseA_tile(l, 0, 16, False)
            front(l, 16, 128)
            for i in range(16):
                mid(l, 16 + i * 128, 128, True, side=(front_units(l, 16 + (i + 1) * 128, 128) if i < 15 else None))
                back(l, 16 + i * 128, 128)
            if STAGE >= 10:
                emit_states(l, wkvp[l], shp[l], cvp[l], 128)
            for j in range(NS if KONLY == "" else 0):
                load_sample_state(l, j)
                phaseA_tile(l, NMETA + NPT + j * 16, 16, True)
                if STAGE >= 10:
                    emit_states(l, wkvs[l, j], shs[l, j], cvs[l, j], 16)
            barrier_all()
            if STAGE >= 11:
                SQ2 = WORK[:, 26240:26240 + 1024].rearrange("p (a b) -> p a b", a=8)
                RS2 = WORK[:, 26240 + 1024:26240 + 1024 + 256].bitcast(F32)
                tSQ2, tRS2 = Tk(), Tk()
                assert 26240 + 1024 + 256 <= WORK_E
                c = 0
                kalt = 0
                while c < TT:
                    n = min(128, TT - c)
                    if kalt % 2 == 0:
                        rmsnorm_tile(c, n, lambda cc: VEC[:, V_NF + cc:V_NF + cc + 1], lambda cc, c=c, n=n: HN[:, cc, c:c + n], tHN)
                    else:
                        rmsnorm_tile(c, n, lambda cc: VEC[:, V_NF + cc:V_NF + cc + 1], lambda cc, c=c, n=n: HN[:, cc, c:c + n], tHN,
                                     SQ2, tSQ2, RS2, tRS2, PS7, tPSB)
                    kalt += 1
                    c += n
                def load_slot(e8_):
                    sl_ = e8_ % 2
                    for kc in range(8):
                        stream_cast(RINGB, WU[sl_][:, kc, :], w_up[l, kc * 128:(kc + 1) * 128, e8_ * 512:(e8_ + 1) * 512], tWU[sl_])
                    for fc in range(4):
                        r0 = e8_ * 512 + fc * 128
                        stream_cast(RINGB, WDf[sl_][fc], w_down[l, r0:r0 + 128, :], tWD[sl_])

                tiles_ = [(0, 448), (448, 448), (896, 448), (1344, 448), (1792, TT - 1792)]
                assert sum(N for _, N in tiles_) == TT and all(N <= 512 and t0 % 16 == 0 for t0, N in tiles_)
                items = [(e8, t0, N) for e8 in range(8) for (t0, N) in tiles_]
                UPB = [0, 1, 2, 3, 6, 7]

                def bk(i):
                    return (PS[i], tPS[i]) if i < 7 else (PS7, tPSB)

                def emit_up(idx):
                    e8, t0, N = items[idx]
                    sl = e8 % 2
                    for fc in range(4):
                        pb_, tpb_ = bk(UPB[(4 * idx + fc) % 6])
                        for kc in range(8):
                            S.op("pe", lambda fc=fc, kc=kc, sl=sl, t0=t0, N=N, pb_=pb_: PE.matmul(pb_[:, 0:N], lhsT=WU[sl][:, kc, fc * 128:(fc + 1) * 128], rhs=HN[:, kc, t0:t0 + N],
                                                                                                  start=(kc == 0), stop=(kc == 7)), r=[tWU[sl], tHN], w=[tpb_], sig=(kc == 7))

                def emit_act(idx):
                    e8, t0, N = items[idx]
                    for fc in range(4):
                        pb_, tpb_ = bk(UPB[(4 * idx + fc) % 6])
                        S.op("act", lambda N=N, pb_=pb_: ACT.activation(out=R32[:, 0:N], in_=pb_[:, 0:N], func=AF.Relu), r=[tpb_], w=[tR32])
                        S.op("dve", lambda fc=fc, N=N, pb_=pb_: V.tensor_tensor(out=A2B[:, fc, 0:N], in0=R32[:, 0:N], in1=pb_[:, 0:N], op=ALU.mult), r=[tR32, tpb_], w=[tA2B])

                def emit_down(idx):
                    e8, t0, N = items[idx]
                    sl = e8 % 2
                    for m in range(8):
                        pb = 4 + (m % 2)
                        for fc in range(4):
                            S.op("pe", lambda m=m, fc=fc, sl=sl, pb=pb, N=N: PE.matmul(PS[pb][:, 0:N], lhsT=WDf[sl][fc][:, m * 128:(m + 1) * 128], rhs=A2B[:, fc, 0:N],
                                                                                     start=(fc == 0), stop=(fc == 3)), r=[tWD[sl], tA2B], w=[tPS[pb]], sig=(fc == 3))
                        hv = H[:, m, t0:t0 + N]
                        S.op("dve", lambda hv=hv, pb=pb, N=N: V.tensor_tensor(out=hv, in0=hv, in1=PS[pb][:, 0:N], op=ALU.add),
                             r=htks(t0, N) + [tPS[pb]], w=htks(t0, N))

                load_slot(0)
                load_slot(1)
                pf_pieces = layer_weight_pieces(l + 1) if l + 1 < DEPTH else []
                pf_pieces.reverse()
                emit_up(0)
                emit_act(0)
                for idx in range(len(items)):
                    if idx + 1 < len(items):
                        emit_up(idx + 1)
                    emit_down(idx)
                    if idx + 1 < len(items):
                        emit_act(idx + 1)
                    e8 = items[idx][0]
                    if idx + 1 == len(items) or items[idx + 1][0] != e8:
                        if e8 + 2 < 8:
                            load_slot(e8 + 2)
                        for _ in range(3):
                            if pf_pieces:
                                pf_pieces.pop()()
                while pf_pieces:
                    pf_pieces.pop()()
        barrier_all()
        XFs = [REG[:, k * 2048:(k + 1) * 2048].bitcast(F32).rearrange("p (k n) -> p k n", k=8) for k in range(2)]
        tXFs = [Tk(), Tk()]
        IOTs = [REG[:, 4096 + k * 2048:4096 + (k + 1) * 2048].bitcast(F32) for k in range(2)]
        tIOTs = [Tk(), Tk()]
        fcount = [0]

        def final_tile(c0, n, dst):
            k_ = fcount[0] % 2
            fcount[0] += 1
            XF, tXF, IOT, tIOT = XFs[k_], tXFs[k_], IOTs[k_], tIOTs[k_]
            rmsnorm_tile(c0, n, lambda c: NFIN[:, c:c + 1], lambda c: XF[:, c, 0:n], tXF)
            for half in range(2):
                for c in range(4):
                    cc = half * 4 + c
                    S.op("pe", lambda half=half, c=c, cc=cc: PE.transpose(out=PS[half][0:n, c * 128:(c + 1) * 128], in_=XF[:, cc, 0:n], identity=IDF[:]), r=[tXF, tC], w=[tPS[half]], sig=(c == 3))
                S.op("act", lambda half=half: ACT.copy(out=IOT[0:n, half * 512:(half + 1) * 512], in_=PS[half][0:n, :]), r=[tPS[half]], w=[tIOT])
            out_toks.append(S.dma("sp", lambda e: e.dma_start(out=dst, in_=IOT[0:n, :]), r=[tIOT]))

        for i in range(16):
            final_tile(16 + i * 128, 128, yp[i * 128:(i + 1) * 128, :])
        for j in range(NS):
            final_tile(NMETA + NPT + j * 16, 16, ys[j * 16:(j + 1) * 16, :])
        S.final_wait(out_toks)
    return nc


_NC_CACHE = {}


def _pack_vecs(inp):
    f = np.float32
    out = np.zeros((DEPTH, 128, NV), f)

    def fm(v, ncol):
        return v.reshape(DEPTH, ncol, 128).transpose(0, 2, 1)

    out[:, :, V_NM:V_NM + 8] = fm(inp["norm_mix"], 8)
    mu = np.zeros((DEPTH, 15 * 128), f)
    mu[:, :PR] = inp["mu_shift"]
    out[:, :, V_MU:V_MU + 15] = fm(mu, 15)
    out[:, :, V_W0:V_W0 + 4] = fm(inp["w0"], 4)
    out[:, :, V_A0:V_A0 + 4] = fm(inp["a0"], 4)
    out[:, :, V_KK:V_KK + 4] = fm(inp["k_k"], 4)
    out[:, :, V_KA:V_KA + 4] = fm(inp["k_a"], 4)
    out[:, :, V_RK:V_RK + 4] = fm(inp["r_k"].reshape(DEPTH, 512), 4)
    cw = inp["conv_w"].reshape(DEPTH, CW, 4, 128).transpose(0, 3, 2, 1)
    out[:, :, V_CW:V_CW + 4 * CW] = cw.reshape(DEPTH, 128, 4 * CW)
    out[:, :, V_CB:V_CB + 4] = fm(inp["conv_b"], 4)
    out[:, :, V_CG:V_CG + 4] = fm(inp["cln_g"], 4)
    out[:, :, V_CBB:V_CBB + 4] = fm(inp["cln_b"], 4)
    out[:, :, V_NF:V_NF + 8] = fm(inp["norm_ffn"], 8)
    return np.ascontiguousarray(out)


def kernel(**inputs):
    inp = {k: np.asarray(v) for k, v in inputs.items()}
    if "nc" not in _NC_CACHE:
        _NC_CACHE["nc"] = build_program()
    nc = _NC_CACHE["nc"]
    vecs = _pack_vecs(inp)
    nfin = np.ascontiguousarray(inp["norm_final"].reshape(8, 128).T.astype(np.float32))
    shared = {
        "meta": np.ascontiguousarray(inp["meta_tokens"]), "vecs": vecs, "nfin": nfin,
        "w_in": inp["w_in"], "lora_w": inp["lora_w"], "lora_a": inp["lora_a"], "lora_g": inp["lora_g"],
        "gn_g": inp["gn_g"], "gn_b": inp["gn_b"], "w_out": inp["w_out"], "w_up": inp["w_up"], "w_down": inp["w_down"],
    }
    in_maps = []
    for c in range(8):
        m = dict(shared)
        m["xp"] = np.ascontiguousarray(inp["x_prompt"][c])
        m["xs"] = np.ascontiguousarray(inp["x_sample"][4 * c:4 * c + 4].reshape(NS * TS, D))
        m["swkv"] = np.ascontiguousarray(inp["state_wkv"][:, 4 * c:4 * c + 4])
        m["sshift"] = np.ascontiguousarray(inp["state_shift"][:, 4 * c:4 * c + 4])
        m["sconv"] = np.ascontiguousarray(inp["cache_conv"][:, 4 * c:4 * c + 4])
        in_maps.append(m)
    res = run_bass_kernel_spmd(nc, in_maps, core_ids=list(range(8)))
    rs = res.results
    y_prompt = np.stack([r["yp"] for r in rs], 0).astype(np.float32)
    y_sample = np.concatenate([r["ys"].reshape(NS, TS, D) for r in rs], 0).astype(np.float32)
    wkv_p = np.stack([r["wkvp"] for r in rs], 1).astype(np.float32)
    sh_p = np.stack([r["shp"] for r in rs], 1).astype(np.float32)
    cv_p = np.stack([r["cvp"] for r in rs], 1).astype(np.float32)
    wkv_s = np.concatenate([r["wkvs"] for r in rs], 1).astype(np.float32)
    sh_s = np.concatenate([r["shs"] for r in rs], 1).astype(np.float32)
    cv_s = np.concatenate([r["cvs"] for r in rs], 1).astype(np.float32)
    return (y_prompt, y_sample, wkv_p, sh_p, cv_p, wkv_s, sh_s, cv_s)
```

```python
import numpy as np
from contextlib import ExitStack
import concourse.bass as bass
import concourse.mybir as mybir
from concourse.bass_utils import run_bass_kernel_spmd

F32 = mybir.dt.float32
BF16 = mybir.dt.bfloat16
AF = mybir.ActivationFunctionType
ALU = mybir.AluOpType
AX = mybir.AxisListType

DEPTH = 4
D = 1024
NPT = 2048
NMETA = 16
NS = 4
TS = 16
TT = NMETA + NPT + NS * TS
PR = 1824
PIN = 2848
DFF = 4096
CW = 31
DECAY = 0.606531
NV = 187
V_NM, V_MU, V_W0, V_A0, V_KK, V_KA, V_RK, V_CW, V_CB, V_CG, V_CBB, V_NF = 0, 8, 23, 27, 31, 35, 39, 43, 167, 171, 175, 179
EPOCH = 30000
NDSEM = 24
import os
STAGE = float(os.environ.get("KSTAGE", "999"))


class Tk:
    __slots__ = ("w", "r", "excl")

    def __init__(self, excl=False):
        self.w = None
        self.r = []
        self.excl = excl


class Sched:
    def __init__(self, nc, es):
        self.nc = nc
        self.es = es
        self.eng = {"pe": nc.tensor, "dve": nc.vector, "act": nc.scalar, "pool": nc.gpsimd, "sp": nc.sync}
        self.sems = {}
        self.cnt = {k: 0 for k in self.eng}
        self.seen = {k: {} for k in self.eng}
        self.dsem = [es.enter_context(nc.semaphore(f"dq{i}")) for i in range(NDSEM)]
        self.dcnt = [0] * NDSEM
        self.dnext = 0
        self.ninst = 0
        self.rec = None

    def _sem(self, e, ep):
        key = (e, ep)
        if key not in self.sems:
            self.sems[key] = self.es.enter_context(self.nc.semaphore(f"s_{e}_{ep}"))
        return self.sems[key]

    def _wait(self, e, tok):
        key, val = tok
        if self.seen[e].get(key, 0) >= val:
            return
        if key[0] == "pe" and e != "pe" and STAGE > 900:
            assert key[1] * EPOCH + val <= self.cnt["pe"], "wait on unsignalled PE op"
        if key[0] == "d":
            sem = self.dsem[key[1]]
        else:
            sem = self._sem(key[0], key[1])
        self.eng[e].wait_ge(sem, val)
        self.seen[e][key] = val

    def _deps(self, e, r, w):
        deps = []
        for t in r:
            if t.w is not None:
                deps.append(t.w)
            if t.excl:
                deps.extend(tok for tok in t.r if tok[0][0] != e)
        for t in w:
            if t.w is not None:
                deps.append(t.w)
            deps.extend(t.r)
        for tok in deps:
            if tok[0][0] == e and e == "pe":
                continue
            self._wait(e, tok)

    def op(self, e, fn, r=(), w=(), sig=True):
        if self.rec is not None:
            self.rec.append((e, fn, list(r), list(w), sig))
            return None
        self._deps(e, r, w)
        inst = fn()
        if sig or e != "pe":
            self.cnt[e] += 1
            c = self.cnt[e]
            ep, v = divmod(c - 1, EPOCH)
            v += 1
            inst.then_inc(self._sem(e, ep), 1)
        else:
            c = self.cnt[e] + 1
            ep, v = divmod(c - 1, EPOCH)
            v += 1
        tok = ((e, ep), v)
        self.seen[e][(e, ep)] = max(self.seen[e].get((e, ep), 0), 0)
        for t in r:
            t.r.append(tok)
        for t in w:
            t.w = tok
            t.r = []
        self.ninst += 1
        return tok

    def dma(self, q, fn, r=(), w=()):
        i = self.dnext
        self.dnext = (self.dnext + 1) % NDSEM
        if self.dcnt[i] > 0:
            self._wait(q, (("d", i), self.dcnt[i]))
        self._deps(q, r, w)
        inst = fn(self.eng[q])
        self.dcnt[i] += 16
        inst.then_inc(self.dsem[i], 16)
        tok = (("d", i), self.dcnt[i])
        for t in r:
            t.r.append(tok)
        for t in w:
            t.w = tok
            t.r = []
        self.ninst += 1
        return tok

    def final_wait(self, toks):
        for tok in toks:
            self._wait("sp", tok)


def build_program():
    nc = bass.Bass("TRN2", target_bir_lowering=False, dynamic_dma_scratch_size=4096)

    def din(name, shape):
        return nc.dram_tensor(name, list(shape), F32, kind="ExternalInput").ap()

    def dout(name, shape):
        return nc.dram_tensor(name, list(shape), F32, kind="ExternalOutput").ap()

    xp = din("xp", [NPT, D])
    meta = din("meta", [NMETA, D])
    xs_in = din("xs", [NS * TS, D])
    swkv = din("swkv", [DEPTH, NS, 8, 64, 64])
    sshift = din("sshift", [DEPTH, NS, PR])
    sconv = din("sconv", [DEPTH, NS, 30, 512])
    vecs = din("vecs", [DEPTH, 128, NV])
    nfin = din("nfin", [128, 8])
    w_in = din("w_in", [DEPTH, D, PIN])
    lora_w = din("lora_w", [DEPTH, 64, 512])
    lora_a = din("lora_a", [DEPTH, 64, 512])
    lora_g = din("lora_g", [DEPTH, 160, 512])
    gn_g = din("gn_g", [DEPTH, 512])
    gn_b = din("gn_b", [DEPTH, 512])
    w_out = din("w_out", [DEPTH, D, D])
    w_up = din("w_up", [DEPTH, D, DFF])
    w_down = din("w_down", [DEPTH, DFF, D])

    yp = dout("yp", [NPT, D])
    ys = dout("ys", [NS * TS, D])
    wkvp = dout("wkvp", [DEPTH, 8, 64, 64])
    shp = dout("shp", [DEPTH, PR])
    cvp = dout("cvp", [DEPTH, 30, 512])
    wkvs = dout("wkvs", [DEPTH, NS, 8, 64, 64])
    shs = dout("shs", [DEPTH, NS, PR])
    cvs = dout("cvs", [DEPTH, NS, 30, 512])

    es = ExitStack()
    with es:
        S = Sched(nc, es)

        def sb(name, shape, dt=F32):
            return es.enter_context(nc.sbuf_tensor(name, list(shape), dt))

        H = sb("H", [128, 8, TT])
        Htk = [Tk() for _ in range(TT // 16)]

        def htks(c0, n):
            return Htk[c0 // 16:(c0 + n + 15) // 16]

        REG_E = 8 * PIN + 8 * D
        REG = sb("REG", [128, REG_E], BF16)
        WIN = REG[:, 0:8 * PIN].rearrange("p (k n) -> p k n", k=8)
        WOUT = REG[:, 8 * PIN:8 * PIN + 8 * D].rearrange("p (k n) -> p k n", k=8)
        HALF = TT // 2
        tWIN, tWOUT, tHN, tA2B = Tk(), Tk(), Tk(), Tk()
        tWU = [Tk(), Tk()]
        tWD = [Tk(), Tk()]

        VEC = sb("VEC", [128, NV]); tVEC = Tk()
        NFIN = sb("NFIN", [128, 8]); tNFIN = Tk()
        LW = sb("LW", [128, 512], BF16); LA = sb("LA", [128, 512], BF16); LG = sb("LG", [128, 2, 512], BF16)
        tLW, tLA, tLG = Tk(), Tk(), Tk()
        GNG = sb("GNG", [128, 512]); GNB = sb("GNB", [128, 512]); tGN = Tk()
        IDF = sb("IDF", [128, 128]); IDB = sb("IDB", [128, 128], BF16)
        ONESC = sb("ONESC", [128, 128], BF16)
        ONF = sb("ONF", [128, 128])
        BONES = sb("BONES", [128, 128], BF16)
        HIND = sb("HIND", [128, 2], BF16)
        MKUR = sb("MKUR", [128, 2, 128], BF16)
        MKSL = sb("MKSL", [128, 128], BF16)
        ONESF = sb("ONESF", [128, 128])
        CEPS = sb("CEPS", [128, 4])
        tC = Tk()
        WORK_E = 28774
        WORK = sb("WORK", [128, WORK_E], BF16)
        wo = [0]

        def wk(name, shape, dt=F32):
            ne = 1
            for d in shape[1:]:
                ne *= d
            nb = ne * (2 if dt == BF16 else 4)
            nb = (nb + 3) // 4 * 4
            a = wo[0]
            wo[0] += nb // 2
            assert wo[0] <= WORK_E, (name, wo[0])
            v = WORK[:, a:a + nb // 2]
            if dt != BF16:
                v = v.bitcast(F32)
            else:
                v = v[:, 0:ne]
            if len(shape) == 2:
                return v
            if len(shape) == 3:
                return v.rearrange("p (a b) -> p a b", a=shape[1])
            return v.rearrange("p (a b c) -> p a b c", a=shape[1], b=shape[2])
        RS = sb("RS", [128, 128]); tRS = Tk()
        tXN = Tk()
        P = wk("P", [128, 15, 129]); tP = Tk()
        XS = sb("XS", [128, 15, 128]); tXS = Tk()
        TWX = wk("TWX", [128, 128], BF16); tTWX = Tk()
        SG = wk("SG", [128, 2, 128], BF16); tSG = Tk()
        A_ = wk("A_", [128, 4, 128]); tA = Tk()
        SW = wk("SW", [128, 4, 128]); tSW = Tk()
        LR = wk("LR", [128, 4, 128]); tLR = Tk()
        EN = wk("EN", [128, 4, 128]); tEN = Tk()
        EP = wk("EP", [128, 4, 128]); tEP = Tk()
        KK = wk("KK", [128, 4, 128]); tKK = Tk()
        T1 = wk("T1", [128, 4, 128]); tT1 = Tk()
        KK2 = wk("KK2", [128, 4, 128], BF16); tKK2 = Tk()
        T2B = wk("T2B", [128, 4, 128], BF16); tT2B = Tk()
        AR = wk("AR", [128, 4, 2, 128], BF16); tAR = Tk()
        ATZ = wk("ATZ", [128, 4, 2, 128], BF16); tATZ = Tk()
        BTZ = wk("BTZ", [128, 4, 2, 128], BF16); tBTZ = Tk()
        KTZ = wk("KTZ", [128, 4, 2, 128], BF16); tKTZ = Tk()
        BTU = wk("BTU", [128, 4, 128], BF16); tBTU = Tk()
        KTU = wk("KTU", [128, 4, 128], BF16); tKTU = Tk()
        VT = wk("VT", [128, 512], BF16); tVT = Tk()
        BTT = wk("BTT", [128, 2, 512], BF16); tBTT = Tk()
        AMA = wk("AMA", [128, 4, 2, 128], BF16); tAMA = tXN; XN = AMA[:, :, :, :].rearrange("p a b c -> p (a b) c")
        AMK = wk("AMK", [128, 4, 2, 128], BF16); tAMK = Tk()
        NQ = [wk(f"NQ{i}", [128, 4, 128], BF16) for i in range(2)]; tNQ = [Tk(), Tk()]
        QT = [wk(f"QT{i}", [128, 4, 128], BF16) for i in range(2)]; tQT = [Tk(), Tk()]
        MM = wk("MM", [128, 4, 128], BF16); tMM = Tk()
        ACC = [wk(f"ACC{i}", [128, 4, 128], BF16) for i in range(2)]; tACC = [Tk(), Tk()]
        XB = wk("XB", [128, 256], BF16); tXB = Tk()
        UB = wk("UB", [128, 256], BF16); tUB = Tk()
        ST = wk("ST", [128, 4, 128]); tST = Tk()
        STB = wk("STB", [128, 4, 128], BF16); tSTB = Tk()
        OTM = EN[:, :, :].rearrange("p a b -> p (a b)"); tOTM = tEN
        OSQ = A_[:, :, :].rearrange("p a b -> p (a b)"); tOSQ = tA
        STAT = wk("STAT", [128, 6, 8]); tSTAT = Tk()
        YRT = KK2[:, :, :].rearrange("p a b -> p (a b)"); tYRT = tKK2
        MIX = sb("MIX", [128, 8, 128], BF16); tMIX = Tk(); SQ_DEF = MIX; tSQ_DEF = tMIX
        SQA = BTT[:, :, :].rearrange("p a (b c) -> p (a b) c", b=4)
        UH = wk("UH", [128, 4, 30 + 128]); tUH = Tk()
        NPE = 14
        DG = sb("DG", [128, 4 * NPE, 128], BF16); tDG = Tk()
        UHB = wk("UHB", [128, 4, 30 + 128], BF16); tUHB = Tk()
        tCVa = [Tk() for _ in range(4)]; tCVb = [Tk() for _ in range(4)]
        TMPH = RS[:, 0:120].rearrange("p (a b) -> p a b", a=4); tTMPH = tRS
        CV = KK; tCV = tKK
        CSQ = T1; tCSQ = tT1
        LNS = LR[:, 0:3, :]; tLNS = tLR
        XSF = XS[:, :, :].rearrange("p a b -> p (a b)"); IOT = XSF[:, 0:1024]; tIOT = tXS
        R32 = XSF[:, 1024:1536]; tR32 = tXS
        PS = [es.enter_context(nc.psum_tensor(f"ps{i}", [128, 512], F32)) for i in range(7)]
        tPS = [Tk(True) for _ in range(7)]
        PS7 = es.enter_context(nc.psum_tensor("ps7", [128, 512], F32)); tPSB = Tk(True)
        PSB = PS7[:, :].bitcast(BF16)

        def b3(ap, a):
            return ap.rearrange("p (a b) -> p a b", a=a)

        def b4(ap, a, b):
            return ap.rearrange("p (a b c) -> p a b c", a=a, b=b)

        V = nc.vector
        ACT = nc.scalar
        PE = nc.tensor
        POOL = nc.gpsimd

        o = 0
        HN = WORK[:, o:o + 8 * TT].rearrange("p (k n) -> p k n", k=8); o += 8 * TT
        WU = [None, None]
        WDf = [None, None]
        WU[0] = WORK[:, o:o + 8 * 512].rearrange("p (k n) -> p k n", k=8); o += 8 * 512
        WDf[0] = [WORK[:, o + fc * 1024:o + (fc + 1) * 1024] for fc in range(4)]; o += 4 * 1024
        DGF = DG[:, :, :].rearrange("p a b -> p (a b)")
        WU[1] = DGF[:, 0:8 * 512].rearrange("p (k n) -> p k n", k=8)
        WDf[1] = [DGF[:, 4096 + fc * 1024:4096 + (fc + 1) * 1024] for fc in range(3)] + [WORK[:, o:o + 1024]]; o += 1024
        assert o <= WORK_E, o
        assert 4096 + 3 * 1024 <= 4 * NPE * 128
        A2B = XS[:, :, :].rearrange("p a b -> p (a b)").bitcast(BF16)[:, 0:2048].rearrange("p (k n) -> p k n", k=4)
        def mk(fn, w):
            S.op("pool", fn, w=w)

        mk(lambda: POOL.memset(IDF[:], 0.0), [tC])
        mk(lambda: POOL.memset(ONESF[:], 1.0), [tC])
        mk(lambda: POOL.affine_select(out=IDF[:], in_=ONESF[:], pattern=[[-1, 128]], compare_op=ALU.is_equal, fill=0.0, base=0, channel_multiplier=1), [tC])
        mk(lambda: POOL.tensor_copy(out=IDB[:], in_=IDF[:]), [tC])
        mk(lambda: POOL.memset(ONESC[:], 1.0 / 1024.0), [tC])
        mk(lambda: POOL.memset(ONF[:], 1.0 / 512.0), [tC])
        mk(lambda: POOL.memset(BONES[:], 0.0), [tC])
        mk(lambda: POOL.memset(BONES[0:64, 0:64], 1.0), [tC])
        mk(lambda: POOL.memset(BONES[64:128, 64:128], 1.0), [tC])
        mk(lambda: POOL.memset(HIND[:], 0.0), [tC])
        mk(lambda: POOL.memset(HIND[0:64, 0:1], 1.0), [tC])
        mk(lambda: POOL.memset(HIND[64:128, 1:2], 1.0), [tC])
        mk(lambda: POOL.affine_select(out=MKUR[:, 0, :], in_=ONESF[:], pattern=[[1, 128]], compare_op=ALU.is_gt, fill=0.0, base=0, channel_multiplier=-1), [tC])
        mk(lambda: POOL.affine_select(out=MKUR[:, 1, :], in_=ONESF[:], pattern=[[1, 128]], compare_op=ALU.is_ge, fill=0.0, base=0, channel_multiplier=-1), [tC])
        mk(lambda: POOL.affine_select(out=MKSL[:], in_=ONESF[:], pattern=[[-1, 128]], compare_op=ALU.is_gt, fill=0.0, base=0, channel_multiplier=1), [tC])
        mk(lambda: POOL.memset(CEPS[:, 0:1], 1e-6), [tC])
        mk(lambda: POOL.memset(CEPS[:, 1:2], 64e-5), [tC])
        mk(lambda: POOL.memset(CEPS[:, 2:3], 1e-5), [tC])
        mk(lambda: POOL.memset(CEPS[:, 3:4], 1e-19), [tC])
        for buf, tk_ in ((ATZ, tATZ), (BTZ, tBTZ), (KTZ, tKTZ), (LW, tLW), (LA, tLA), (LG, tLG),
                         (ST, tST), (UH, tUH)):
            mk((lambda b=buf: POOL.memset(b[:], 0.0)), [tk_])

        LST = [WORK[:, k * 2048:(k + 1) * 2048].bitcast(F32) for k in range(3)]
        tLST = [Tk(), Tk(), Tk()]
        lcount = [0]

        def load_tokens(src_ap, n, c0):
            k_ = lcount[0] % 3
            lcount[0] += 1
            IOT, tIOT = LST[k_], tLST[k_]
            S.dma("sp", lambda e: e.dma_start(out=IOT[0:n, :], in_=src_ap), w=[tIOT])
            for half in range(2):
                for c in range(4):
                    cc = half * 4 + c
                    S.op("pe", lambda cc=cc, c=c: PE.transpose(out=PS[half][:, c * 128:c * 128 + n], in_=IOT[0:n, cc * 128:(cc + 1) * 128], identity=IDF[0:n, 0:n]),
                         r=[tIOT, tC], w=[tPS[half]], sig=(c == 3))
                S.op("act", lambda half=half: ACT.copy(out=H[:, half * 4:half * 4 + 4, c0:c0 + n], in_=b3(PS[half][:, :], 4)[:, :, 0:n]),
                     r=[tPS[half]], w=htks(c0, n))

        RING0 = [(WORK[:, 6144 + k * 1024:6144 + (k + 1) * 1024].bitcast(F32), Tk()) for k in range(4)]
        RINGB = [(GNG, Tk()), (GNB, Tk()), (LG[:, :, :].rearrange("p a b -> p (a b)").bitcast(F32), Tk())]
        RINGA = [(XSF[:, k * 512:(k + 1) * 512], tXS) for k in range(3)]
        ring_pos = [0]

        def stream_cast(ring, dst, src, dst_tk, P0=0, P1=128):
            W = src.shape[-1]
            c = 0
            while c < W:
                w_ = min(512, W - c)
                stg, stk = ring[ring_pos[0] % len(ring)]
                ring_pos[0] += 1
                S.dma("sp", lambda e, stg=stg, c=c, w_=w_: e.dma_start(out=stg[P0:P1, 0:w_], in_=src[:, c:c + w_]), w=[stk])
                S.op("pool", lambda stg=stg, c=c, w_=w_: POOL.tensor_copy(out=dst[:, c:c + w_], in_=stg[P0:P1, 0:w_]), r=[stk], w=[dst_tk])
                c += w_

        cur_ring = [RING0]

        def layer_weight_pieces(l):
            ps_ = []
            for kc in range(8):
                ps_.append(lambda kc=kc: stream_cast(cur_ring[0], WIN[:, kc, :], w_in[l, kc * 128:(kc + 1) * 128, :], tWIN))
            for kc in range(8):
                ps_.append(lambda kc=kc: stream_cast(cur_ring[0], WOUT[:, kc, :], w_out[l, kc * 128:(kc + 1) * 128, :], tWOUT))
            return ps_

        def load_layer_weights(l):
            for p_ in layer_weight_pieces(l):
                p_()

        w0_pieces = layer_weight_pieces(0)
        w0_pieces.reverse()

        def w0_step():
            if w0_pieces:
                w0_pieces.pop()()

        load_tokens(meta[:, :], 16, 0)
        w0_step()
        for i in range(16):
            load_tokens(xp[i * 128:(i + 1) * 128, :], 128, 16 + i * 128)
            w0_step()
        for j in range(NS):
            load_tokens(xs_in[j * 16:(j + 1) * 16, :], 16, NMETA + NPT + j * 16)
        S.dma("sp", lambda e: e.dma_start(out=NFIN[:], in_=nfin[:, :]), w=[tNFIN])

        def rmsnorm_tile(c0, n, gcol_ap, out_fn, out_tk, SQ=None, tSQ=None, RSb=None, tRSb=None, PSn=None, tPSn=None):
            if SQ is None:
                SQ, tSQ = SQ_DEF, tSQ_DEF
            if RSb is None:
                RSb, tRSb = RS, tRS
            if PSn is None:
                PSn, tPSn = PS[6], tPS[6]
            hT = H[:, :, c0:c0 + n]
            S.op("act", lambda: ACT.activation(out=SQ[:, :, 0:n], in_=hT, func=AF.Square), r=htks(c0, n), w=[tSQ])
            for c in range(8):
                S.op("pe", lambda c=c: PE.matmul(PSn[:, 0:n], lhsT=ONESC[:], rhs=SQ[:, c, 0:n], start=(c == 0), stop=(c == 7)),
                     r=[tSQ, tC], w=[tPSn], sig=(c == 7))
            S.op("act", lambda: ACT.activation(out=RSb[:, 0:n], in_=PSn[:, 0:n], func=AF.Ln, bias=CEPS[:, 0:1], scale=1.0),
                 r=[tPSn, tC], w=[tRSb])
            S.op("act", lambda: ACT.activation(out=RSb[:, 0:n], in_=RSb[:, 0:n], func=AF.Exp, scale=-0.5), r=[tRSb], w=[tRSb])
            for c in range(8):
                S.op("dve", lambda c=c: V.scalar_tensor_tensor(out=out_fn(c), in0=H[:, c, c0:c0 + n], scalar=gcol_ap(c), in1=RSb[:, 0:n],
                                                                op0=ALU.mult, op1=ALU.mult),
                     r=htks(c0, n) + [tRSb, tVEC, tNFIN], w=[out_tk])

        def nrounds_for(n):
            r = 0
            while (1 << r) < n:
                r += 1
            return r

        def front(l, c0, n):
            rmsnorm_tile(c0, n, lambda c: VEC[:, V_NM + c:V_NM + c + 1], lambda c: XN[:, c, 0:n], tXN, SQA, tBTT)
            for m in range(15):
                M = 128 if m < 14 else 32
                bank = m // 4
                sl = m % 4
                for kc in range(8):
                    S.op("pe", lambda m=m, M=M, bank=bank, sl=sl, kc=kc: PE.matmul(PS[bank][0:M, sl * 128:sl * 128 + n], lhsT=WIN[:, kc, m * 128:m * 128 + M],
                                                                                 rhs=XN[:, kc, 0:n], start=(kc == 0), stop=(kc == 7)),
                         r=[tWIN, tXN], w=[tPS[bank]], sig=(kc == 7 and (m % 4 == 3 or m == 14)))
            for bank in range(3):
                S.op("act", lambda bank=bank: ACT.copy(out=P[:, bank * 4:bank * 4 + 4, 1:n + 1], in_=b3(PS[bank][:, :], 4)[:, :, 0:n]),
                     r=[tPS[bank]], w=[tP])
            S.op("act", lambda: ACT.copy(out=P[:, 12:14, 1:n + 1], in_=b3(PS[3][:, :], 4)[:, 0:2, 0:n]), r=[tPS[3]], w=[tP])
            S.op("act", lambda: ACT.copy(out=P[0:32, 14, 1:n + 1], in_=PS[3][0:32, 256:256 + n]), r=[tPS[3]], w=[tP])
            for m in range(8):
                bank = 4 if m < 4 else 0
                sl = m % 4
                col = PR + m * 128
                for kc in range(8):
                    S.op("pe", lambda bank=bank, sl=sl, col=col, kc=kc: PE.matmul(PS[bank][:, sl * 128:sl * 128 + n], lhsT=WIN[:, kc, col:col + 128],
                                                                                rhs=XN[:, kc, 0:n], start=(kc == 0), stop=(kc == 7)),
                         r=[tWIN, tXN], w=[tPS[bank]], sig=(kc == 7 and m % 4 == 3))
            S.op("act", lambda: ACT.activation(out=CSQ[:, :, 0:n], in_=b3(PS[0][:, :], 4)[:, :, 0:n], func=AF.Sigmoid), r=[tPS[0]], w=[tCSQ])
            S.op("dve", lambda: V.tensor_tensor(out=UH[:, :, 30:30 + n], in0=CSQ[:, :, 0:n], in1=b3(PS[4][:, :], 4)[:, :, 0:n], op=ALU.mult),
                 r=[tCSQ, tPS[4]], w=[tUH])
            S.op("dve", lambda: V.tensor_tensor(out=XS[:, :, 0:n], in0=P[:, :, 0:n], in1=P[:, :, 1:n + 1], op=ALU.subtract), r=[tP], w=[tXS])
            S.op("dve", lambda: V.tensor_tensor(out=XS[:, :, 0:n], in0=XS[:, :, 0:n], in1=VEC[:, V_MU:V_MU + 15].unsqueeze(2).broadcast_to([128, 15, n]), op=ALU.mult),
                 r=[tXS, tVEC], w=[tXS])
            S.op("dve", lambda: V.tensor_tensor(out=XS[:, :, 0:n], in0=XS[:, :, 0:n], in1=P[:, :, 1:n + 1], op=ALU.add), r=[tXS, tP], w=[tXS])
            S.op("act", lambda: ACT.copy(out=P[:, :, 0:1], in_=P[:, :, n:n + 1]), r=[tP], w=[tP])

        def front_units(l, c0, n):
            XNa = LR[:, :, :].rearrange("p a b -> p (a b)").bitcast(BF16).rearrange("p (a b) -> p a b", a=8)
            SQe = EN[:, :, :].rearrange("p a b -> p (a b)").bitcast(BF16).rearrange("p (a b) -> p a b", a=8)
            units = []
            u0 = []
            hT = H[:, :, c0:c0 + n]
            u0.append(("act", lambda: ACT.activation(out=SQe[:, :, 0:n], in_=hT, func=AF.Square), htks(c0, n), [tEN], True))
            for c in range(8):
                u0.append(("pe", lambda c=c: PE.matmul(PS[6][:, 0:n], lhsT=ONESC[:], rhs=SQe[:, c, 0:n], start=(c == 0), stop=(c == 7)), [tEN, tC], [tPS[6]], c == 7))
            u0.append(("act", lambda: ACT.activation(out=RS[:, 0:n], in_=PS[6][:, 0:n], func=AF.Ln, bias=CEPS[:, 0:1], scale=1.0), [tPS[6], tC], [tRS], True))
            u0.append(("act", lambda: ACT.activation(out=RS[:, 0:n], in_=RS[:, 0:n], func=AF.Exp, scale=-0.5), [tRS], [tRS], True))
            for c in range(8):
                u0.append(("dve", lambda c=c: V.scalar_tensor_tensor(out=XNa[:, c, 0:n], in0=H[:, c, c0:c0 + n], scalar=VEC[:, V_NM + c:V_NM + c + 1], in1=RS[:, 0:n],
                                                                      op0=ALU.mult, op1=ALU.mult), htks(c0, n) + [tRS, tVEC], [tLR], True))
            units.append(u0)
            fills = [([0, 1, 2, 3], 3), ([4, 5, 6, 7], 4), ([8, 9, 10, 11], 3), ([12, 13, 14], 3), ("ca", 4), ("cg", 3)]
            for blocks, bank in fills:
                if blocks == "ca" or blocks == "cg":
                    cols = [(PR + (0 if blocks == "ca" else 512) + q * 128, 128) for q in range(4)]
                else:
                    cols = [(m * 128, 128 if m < 14 else 32) for m in blocks]
                halves = [cols[0:2], cols[2:]]
                for hi, hcols in enumerate(halves):
                    u = []
                    for qi, (col, M) in enumerate(hcols):
                        sl = hi * 2 + qi
                        for kc in range(8):
                            lastmm = (hi == 1 and qi == len(hcols) - 1 and kc == 7)
                            u.append(("pe", lambda col=col, M=M, sl=sl, kc=kc, bank=bank: PE.matmul(PS[bank][0:M, sl * 128:sl * 128 + n], lhsT=WIN[:, kc, col:col + M], rhs=XNa[:, kc, 0:n],
                                                                                                   start=(kc == 0), stop=(kc == 7)), [tWIN, tLR], [tPS[bank]], lastmm))
                    if hi == 1:
                        if blocks == "ca":
                            u.append(("act", lambda bank=bank: ACT.copy(out=A_[:, :, 0:n], in_=b3(PS[bank][:, :], 4)[:, :, 0:n]), [tPS[bank]], [tA], True))
                        elif blocks == "cg":
                            u.append(("act", lambda bank=bank: ACT.activation(out=CSQ[:, :, 0:n], in_=b3(PS[bank][:, :], 4)[:, :, 0:n], func=AF.Sigmoid), [tPS[bank]], [tCSQ], True))
                        elif len(blocks) == 4:
                            b0 = blocks[0]
                            u.append(("act", lambda bank=bank, b0=b0: ACT.copy(out=P[:, b0:b0 + 4, 1:n + 1], in_=b3(PS[bank][:, :], 4)[:, :, 0:n]), [tPS[bank]], [tP], True))
                        else:
                            u.append(("act", lambda bank=bank: ACT.copy(out=P[:, 12:14, 1:n + 1], in_=b3(PS[bank][:, :], 4)[:, 0:2, 0:n]), [tPS[bank]], [tP], True))
                            u.append(("act", lambda bank=bank: ACT.copy(out=P[0:32, 14, 1:n + 1], in_=PS[bank][0:32, 256:256 + n]), [tPS[bank]], [tP], True))
                    units.append(u)
            def shift_ops(b0, b1):
                nb = b1 - b0
                return [
                    ("dve", lambda: V.tensor_tensor(out=XS[:, b0:b1, 0:n], in0=P[:, b0:b1, 0:n], in1=P[:, b0:b1, 1:n + 1], op=ALU.subtract), [tP], [tXS], True),
                    ("dve", lambda: V.tensor_tensor(out=XS[:, b0:b1, 0:n], in0=XS[:, b0:b1, 0:n], in1=VEC[:, V_MU + b0:V_MU + b1].unsqueeze(2).broadcast_to([128, nb, n]), op=ALU.mult),
                     [tXS, tVEC], [tXS], True),
                    ("dve", lambda: V.tensor_tensor(out=XS[:, b0:b1, 0:n], in0=XS[:, b0:b1, 0:n], in1=P[:, b0:b1, 1:n + 1], op=ALU.add), [tXS, tP], [tXS], True),
                ]
            units[7].extend(shift_ops(0, 12))
            tail = []
            tail.append(("dve", lambda: V.tensor_tensor(out=UH[:, :, 30:30 + n], in0=CSQ[:, :, 0:n], in1=A_[:, :, 0:n], op=ALU.mult), [tCSQ, tA], [tUH], True))
            tail.extend(shift_ops(12, 15))
            tail.append(("act", lambda: ACT.copy(out=P[:, :, 0:1], in_=P[:, :, n:n + 1]), [tP], [tP], True))
            return units, tail

        def emit_unit(u):
            for e_, fn_, r_, w_, sig_ in u:
                S.op(e_, fn_, r=r_, w=w_, sig=sig_)

        def mid(l, c0, n, has_state, side=None):
            R = nrounds_for(n)
            rr = XS[:, 0:4, 0:n]
            kx = XS[:, 4:8, 0:n]
            S.op("act", lambda: ACT.activation(out=TWX[0:64, 0:n], in_=XS[0:64, 12, 0:n], func=AF.Tanh), r=[tXS], w=[tTWX])
            S.op("act", lambda: ACT.copy(out=TWX[64:128, 0:n], in_=XS[64:128, 12, 0:n]), r=[tXS], w=[tTWX])
            S.op("act", lambda: ACT.activation(out=SG[:, 0, 0:n], in_=XS[:, 13, 0:n], func=AF.Sigmoid), r=[tXS], w=[tSG])
            S.op("act", lambda: ACT.activation(out=SG[0:32, 1, 0:n], in_=XS[0:32, 14, 0:n], func=AF.Sigmoid), r=[tXS], w=[tSG])
            for j in range(4):
                S.op("pe", lambda j=j: PE.matmul(PS[0][:, j * 128:j * 128 + n], lhsT=LW[:, j * 128:(j + 1) * 128], rhs=TWX[:, 0:n], start=True, stop=True),
                     r=[tLW, tTWX], w=[tPS[0]], sig=(j == 3))
            for j in range(4):
                S.op("pe", lambda j=j: PE.matmul(PS[1][:, j * 128:j * 128 + n], lhsT=LA[:, j * 128:(j + 1) * 128], rhs=TWX[:, 0:n], start=True, stop=True),
                     r=[tLA, tTWX], w=[tPS[1]], sig=(j == 3))
            for j in range(4):
                S.op("act", lambda j=j: ACT.activation(out=SW[:, j, 0:n], in_=PS[0][:, j * 128:j * 128 + n], func=AF.Sigmoid, bias=VEC[:, V_W0 + j:V_W0 + j + 1], scale=1.0),
                     r=[tPS[0], tVEC], w=[tSW])
            for j in range(4):
                S.op("act", lambda j=j: ACT.activation(out=A_[:, j, 0:n], in_=PS[1][:, j * 128:j * 128 + n], func=AF.Sigmoid, bias=VEC[:, V_A0 + j:V_A0 + j + 1], scale=1.0),
                     r=[tPS[1], tVEC], w=[tA])
            for j in range(4):
                S.op("dve", lambda j=j: V.tensor_tensor_scan(out=LR[:, j, 0:n], data0=ONESF[:, 0:n], data1=SW[:, j, 0:n], initial=0.0, op0=ALU.mult, op1=ALU.add),
                     r=[tSW, tC], w=[tLR])
            S.op("act", lambda: ACT.activation(out=EN[:, :, 0:n], in_=LR[:, :, 0:n], func=AF.Exp, scale=DECAY), r=[tLR], w=[tEN])
            S.op("act", lambda: ACT.activation(out=EP[:, :, 0:n], in_=LR[:, :, 0:n], func=AF.Exp, scale=-DECAY), r=[tLR], w=[tEP])
            S.op("dve", lambda: V.tensor_tensor(out=SW[:, :, 0:n], in0=LR[:, :, 0:n], in1=SW[:, :, 0:n], op=ALU.subtract), r=[tLR, tSW], w=[tSW])
            S.op("act", lambda: ACT.activation(out=SW[:, :, 0:n], in_=SW[:, :, 0:n], func=AF.Exp, scale=-DECAY), r=[tSW], w=[tSW])
            S.op("dve", lambda: V.tensor_tensor(out=KK[:, :, 0:n], in0=kx, in1=VEC[:, V_KK:V_KK + 4].unsqueeze(2).broadcast_to([128, 4, n]), op=ALU.mult),
                 r=[tXS, tVEC], w=[tKK])
            S.op("act", lambda: ACT.activation(out=KK2[:, :, 0:n], in_=KK[:, :, 0:n], func=AF.Square), r=[tKK], w=[tKK2])
            for j in range(4):
                S.op("pe", lambda j=j: PE.matmul(PS[3][:, j * 128:j * 128 + n], lhsT=BONES[:], rhs=KK2[:, j, 0:n], start=True, stop=True),
                     r=[tKK2, tC], w=[tPS[3]], sig=(j == 3))
            S.op("act", lambda: ACT.activation(out=T1[:, :, 0:n], in_=b3(PS[3][:, :], 4)[:, :, 0:n], func=AF.Ln, bias=CEPS[:, 3:4], scale=1.0), r=[tPS[3], tC], w=[tT1])
            S.op("act", lambda: ACT.activation(out=T1[:, :, 0:n], in_=T1[:, :, 0:n], func=AF.Exp, scale=-0.5), r=[tT1], w=[tT1])
            S.op("dve", lambda: V.tensor_tensor(out=KK[:, :, 0:n], in0=KK[:, :, 0:n], in1=T1[:, :, 0:n], op=ALU.mult), r=[tKK, tT1], w=[tKK])
            S.op("dve", lambda: V.scalar_tensor_tensor(out=T1[:, :, 0:n], in0=A_[:, :, 0:n], scalar=-1.0, in1=VEC[:, V_KA:V_KA + 4].unsqueeze(2).broadcast_to([128, 4, n]),
                                                       op0=ALU.add, op1=ALU.mult), r=[tA, tVEC], w=[tT1])
            S.op("dve", lambda: V.scalar_tensor_tensor(out=kx, in0=T1[:, :, 0:n], scalar=1.0, in1=kx, op0=ALU.add, op1=ALU.mult), r=[tT1, tXS], w=[tXS])
            S.op("dve", lambda: V.tensor_tensor(out=T1[:, :, 0:n], in0=KK[:, :, 0:n], in1=A_[:, :, 0:n], op=ALU.mult), r=[tKK, tA], w=[tT1])
            S.op("dve", lambda: V.scalar_tensor_tensor(out=AR[:, :, 0, 0:n], in0=KK[:, :, 0:n], scalar=-1.0, in1=SW[:, :, 0:n], op0=ALU.mult, op1=ALU.mult),
                 r=[tKK, tSW], w=[tAR])
            S.op("dve", lambda: V.tensor_tensor(out=BTU[:, :, 0:n], in0=T1[:, :, 0:n], in1=EN[:, :, 0:n], op=ALU.mult), r=[tT1, tEN], w=[tBTU])
            S.op("dve", lambda: V.tensor_tensor(out=KTU[:, :, 0:n], in0=kx, in1=EN[:, :, 0:n], op=ALU.mult), r=[tXS, tEN], w=[tKTU])
            S.op("dve", lambda: V.tensor_tensor(out=AR[:, :, 1, 0:n], in0=rr, in1=EP[:, :, 0:n], op=ALU.mult), r=[tXS, tEP], w=[tAR])
            for i in range(2):
                ps_ = slice(64 * i, 64 * i + 64)
                S.op("dve", lambda i=i, ps_=ps_: V.tensor_copy(out=ATZ[ps_, :, i, 0:n], in_=AR[ps_, :, 0, 0:n]), r=[tAR], w=[tATZ])
                S.op("dve", lambda i=i, ps_=ps_: V.tensor_copy(out=BTZ[ps_, :, i, 0:n], in_=BTU[ps_, :, 0:n]), r=[tBTU], w=[tBTZ])
                S.op("dve", lambda i=i, ps_=ps_: V.tensor_copy(out=KTZ[ps_, :, i, 0:n], in_=KTU[ps_, :, 0:n]), r=[tKTU], w=[tKTZ])
            S.op("dve", lambda: V.tensor_tensor(out=T1[:, :, 0:n], in0=kx, in1=VEC[:, V_RK:V_RK + 4].unsqueeze(2).broadcast_to([128, 4, n]), op=ALU.mult),
                 r=[tXS, tVEC], w=[tT1])
            S.op("dve", lambda: V.tensor_tensor(out=T2B[:, :, 0:n], in0=T1[:, :, 0:n], in1=rr, op=ALU.mult), r=[tT1, tXS], w=[tT2B])
            for j in range(4):
                S.op("pe", lambda j=j: PE.transpose(out=PS[4][0:n, j * 128:(j + 1) * 128], in_=XS[:, 8 + j, 0:n], identity=IDF[:]), r=[tXS, tC], w=[tPS[4]], sig=(j == 3))
            S.op("act", lambda: ACT.copy(out=VT[0:n, :], in_=PS[4][0:n, :]), r=[tPS[4]], w=[tVT])
            for j in range(4):
                S.op("pe", lambda j=j: PE.transpose(out=PSB[0:n, j * 128:(j + 1) * 128], in_=BTU[:, j, 0:n], identity=IDB[:]), r=[tBTU, tC], w=[tPSB], sig=(False))
            for j in range(4):
                S.op("pe", lambda j=j: PE.transpose(out=PSB[0:n, 512 + j * 128:512 + (j + 1) * 128], in_=KTU[:, j, 0:n], identity=IDB[:]), r=[tKTU, tC], w=[tPSB], sig=(j == 3))
            S.op("act", lambda: ACT.copy(out=BTT[0:n, :, :], in_=PSB[0:n, :].rearrange("p (a b) -> p a b", a=2)), r=[tPSB], w=[tBTT])
            CVA = KK
            CVBf = SW
            S.op("act", lambda: ACT.copy(out=UHB[:, :, 0:30 + n], in_=UH[:, :, 0:30 + n]), r=[tUH], w=[tUHB])
            for j in range(4):
                for ti in range(NPE):
                    S.op("pe", lambda j=j, ti=ti: PE.matmul(PS[6][:, j * 128:j * 128 + n], lhsT=DG[:, j * NPE + ti, :], rhs=UHB[:, j, ti:ti + n], start=(ti == 0), stop=(ti == NPE - 1)),
                         r=[tDG, tUHB], w=[tPS[6]], sig=(j == 3 and ti == NPE - 1))
            S.op("act", lambda: ACT.copy(out=CVBf[:, :, 0:n], in_=b3(PS[6][:, :], 4)[:, :, 0:n]), r=[tPS[6]], w=[tSW] + tCVb)
            conv_ops = []
            for k, tap in enumerate(range(NPE, CW)):
                for j in range(4):
                    cwb = V_CW + j * CW
                    if k == 0:
                        conv_ops.append((lambda j=j, cwb=cwb, tap=tap: V.tensor_scalar(out=CVA[:, j, 0:n], in0=UH[:, j, tap:tap + n], scalar1=VEC[:, cwb + tap:cwb + tap + 1],
                                                                                     scalar2=VEC[:, V_CB + j:V_CB + j + 1], op0=ALU.mult, op1=ALU.add),
                                         [tUH, tVEC], ([tKK] if j == 0 else []) + [tCVa[j]]))
                    else:
                        tgt, ttk = (CVA, tCVa) if k % 2 == 0 else (CVBf, tCVb)
                        conv_ops.append((lambda j=j, cwb=cwb, tap=tap, tgt=tgt: V.scalar_tensor_tensor(out=tgt[:, j, 0:n], in0=UH[:, j, tap:tap + n], scalar=VEC[:, cwb + tap:cwb + tap + 1],
                                                                                                      in1=tgt[:, j, 0:n], op0=ALU.mult, op1=ALU.add),
                                         [tUH, tVEC, ttk[j]], [ttk[j]]))
            conv_ops.reverse()

            def drain(k):
                for _ in range(k):
                    if not conv_ops:
                        return
                    fn_, r_, w_ = conv_ops.pop()
                    S.op("dve", fn_, r=r_, w=w_)
            side_units, side_tail = (side if side is not None else ([], []))
            side_units = list(side_units)
            side_units.reverse()
            if side_units:
                emit_unit(side_units.pop())
            for g in range(2):
                heads = [(2 * g + hh // 2, hh % 2) for hh in range(4)]
                for hh, (j, i) in enumerate(heads):
                    bk = hh // 2
                    S.op("pe", lambda hh=hh, j=j, i=i, bk=bk: PE.matmul(b4(PS[0 + bk][:, :], 2, 2)[0:n, hh % 2, :, 0:n], lhsT=BTZ[:, j, i, 0:n], rhs=AR[:, j, :, 0:n], start=True, stop=True),
                         r=[tBTZ, tAR], w=[tPS[0 + bk]], sig=(False))
                    S.op("pe", lambda hh=hh, j=j, i=i, bk=bk: PE.matmul(b4(PS[2 + bk][:, :], 2, 2)[0:n, hh % 2, :, 0:n], lhsT=KTZ[:, j, i, 0:n], rhs=AR[:, j, :, 0:n], start=True, stop=True),
                         r=[tKTZ, tAR], w=[tPS[2 + bk]], sig=(False))
                    S.op("pe", lambda hh=hh, j=j, i=i: PE.matmul(b3(PS[4][:, :], 4)[0:n, hh, 0:n], lhsT=ATZ[:, j, i, 0:n], rhs=BTU[:, j, 0:n], start=True, stop=True),
                         r=[tATZ, tBTU], w=[tPS[4]], sig=(hh == 3))
                mk_ur = MKUR[0:n, :, 0:n].unsqueeze(1).broadcast_to([n, 2, 2, n])
                for bk in range(2):
                    S.op("dve", lambda bk=bk: V.tensor_tensor(out=AMA[0:n, 2 * bk:2 * bk + 2, :, 0:n], in0=b4(PS[0 + bk][:, :], 2, 2)[0:n, :, :, 0:n], in1=mk_ur, op=ALU.mult),
                         r=[tPS[0 + bk], tC], w=[tAMA])
                    S.op("dve", lambda bk=bk: V.tensor_tensor(out=AMK[0:n, 2 * bk:2 * bk + 2, :, 0:n], in0=b4(PS[2 + bk][:, :], 2, 2)[0:n, :, :, 0:n], in1=mk_ur, op=ALU.mult),
                         r=[tPS[2 + bk], tC], w=[tAMK])
                S.op("dve", lambda: V.tensor_tensor(out=NQ[0][0:n, :, 0:n], in0=b3(PS[4][:, :], 4)[0:n, :, 0:n], in1=MKSL[0:n, 0:n].unsqueeze(1).broadcast_to([n, 4, n]), op=ALU.mult),
                     r=[tPS[4], tC], w=[tNQ[0]])
                idb_bc = IDB[0:n, 0:n].unsqueeze(1).broadcast_to([n, 4, n])
                S.op("dve", lambda: V.tensor_tensor(out=ACC[0][0:n, :, 0:n], in0=AMA[0:n, :, 0, 0:n], in1=idb_bc, op=ALU.add), r=[tAMA, tC], w=[tACC[0]])
                S.op("act", lambda: ACT.copy(out=QT[0][0:n, :, 0:n], in_=AMA[0:n, :, 0, 0:n]), r=[tAMA], w=[tQT[0]])
                cur = 0
                ac = 0
                MMb = [MM, KK2]
                tMMb = [tMM, tKK2]

                def emit_acc(rd_, ac_):
                    mmb, tmmb = MMb[rd_ % 2], tMMb[rd_ % 2]
                    for hh in range(4):
                        S.op("pe", lambda hh=hh: PE.matmul(b3(PS[2][:, :], 4)[0:n, hh, 0:n], lhsT=mmb[0:n, hh, 0:n], rhs=ACC[ac_][0:n, hh, 0:n], start=True, stop=True),
                             r=[tmmb, tACC[ac_]], w=[tPS[2]], sig=(hh == 3))
                    S.op("act", lambda: ACT.copy(out=ACC[1 - ac_][0:n, :, 0:n], in_=b3(PS[2][:, :], 4)[0:n, :, 0:n]), r=[tPS[2]], w=[tACC[1 - ac_]])

                for rd in range(1, R):
                    nxt = 1 - cur
                    last = (rd == R - 1)
                    mmb, tmmb = MMb[rd % 2], tMMb[rd % 2]
                    for hh in range(4):
                        S.op("pe", lambda hh=hh, cur=cur: PE.matmul(b3(PS[0][:, :], 4)[0:n, hh, 0:n], lhsT=QT[cur][0:n, hh, 0:n], rhs=NQ[cur][0:n, hh, 0:n], start=True, stop=True),
                             r=[tQT[cur], tNQ[cur]], w=[tPS[0]], sig=(hh == 3 and last))
                    if not last:
                        for hh in range(4):
                            S.op("pe", lambda hh=hh, cur=cur: PE.matmul(b3(PS[1][:, :], 4)[0:n, hh, 0:n], lhsT=NQ[cur][0:n, hh, 0:n], rhs=QT[cur][0:n, hh, 0:n], start=True, stop=True),
                                 r=[tQT[cur], tNQ[cur]], w=[tPS[1]], sig=(hh == 3))
                        S.op("act", lambda nxt=nxt: ACT.copy(out=NQ[nxt][0:n, :, 0:n], in_=b3(PS[0][:, :], 4)[0:n, :, 0:n]), r=[tPS[0]], w=[tNQ[nxt]])
                        S.op("dve", lambda nxt=nxt: V.tensor_copy(out=QT[nxt][0:n, :, 0:n], in_=b3(PS[1][:, :], 4)[0:n, :, 0:n]), r=[tPS[1]], w=[tQT[nxt]])
                        S.op("dve", lambda nxt=nxt, mmb=mmb: V.tensor_tensor(out=mmb[0:n, :, 0:n], in0=NQ[nxt][0:n, :, 0:n], in1=idb_bc, op=ALU.add), r=[tNQ[nxt], tC], w=[tmmb])
                    else:
                        S.op("dve", lambda mmb=mmb: V.tensor_tensor(out=mmb[0:n, :, 0:n], in0=b3(PS[0][:, :], 4)[0:n, :, 0:n], in1=idb_bc, op=ALU.add), r=[tPS[0], tC], w=[tmmb])
                    drain(5)
                    if side_units:
                        emit_unit(side_units.pop())
                    if rd >= 2:
                        emit_acc(rd - 1, ac)
                        ac = 1 - ac
                    cur = nxt
                emit_acc(R - 1, ac)
                ac = 1 - ac
                TTt = ACC[ac]
                tTT = tACC[ac]
                for jj in range(2):
                    j = 2 * g + jj
                    if has_state:
                        S.op("pe", lambda j=j, jj=jj: PE.matmul(PS[3][0:n, jj * 128:(jj + 1) * 128], lhsT=AR[:, j, 0, 0:n], rhs=STB[:, j, :], start=True, stop=False),
                             r=[tAR, tSTB], w=[tPS[3]], sig=(False))
                    for i in range(2):
                        hh = jj * 2 + i
                        hc = (j * 2 + i) * 64
                        S.op("pe", lambda hh=hh, hc=hc: PE.matmul(PS[3][0:n, hh * 64:hh * 64 + 64], lhsT=AMK[0:n, hh, 0, 0:n], rhs=VT[0:n, hc:hc + 64], start=(not has_state), stop=True),
                             r=[tAMK, tVT], w=[tPS[3]], sig=(jj == 1 and i == 1))
                S.op("act", lambda: ACT.copy(out=XB[0:n, :], in_=PS[3][0:n, 0:256]), r=[tPS[3]], w=[tXB])
                drain(3)
                for hh in range(4):
                    S.op("pe", lambda hh=hh: PE.matmul(PS[4][0:n, hh * 64:hh * 64 + 64], lhsT=TTt[0:n, hh, 0:n], rhs=XB[0:n, hh * 64:hh * 64 + 64], start=True, stop=True),
                         r=[tTT, tXB], w=[tPS[4]], sig=(hh == 3))
                S.op("act", lambda: ACT.copy(out=UB[0:n, :], in_=PS[4][0:n, 0:256]), r=[tPS[4]], w=[tUB])
                for jj in range(2):
                    j = 2 * g + jj
                    oc = g * 256 + jj * 128
                    if has_state:
                        S.op("pe", lambda j=j, oc=oc: PE.matmul(PS[5][0:n, oc:oc + 128], lhsT=AR[:, j, 1, 0:n], rhs=STB[:, j, :], start=True, stop=False),
                             r=[tAR, tSTB], w=[tPS[5]], sig=(False))
                    for i in range(2):
                        hh = jj * 2 + i
                        hc = (j * 2 + i) * 64
                        S.op("pe", lambda hh=hh, oc=oc, i=i: PE.matmul(PS[5][0:n, oc + i * 64:oc + i * 64 + 64], lhsT=AMA[0:n, hh, 1, 0:n], rhs=UB[0:n, hh * 64:hh * 64 + 64],
                                                                      start=(not has_state), stop=False), r=[tAMA, tUB], w=[tPS[5]], sig=(False))
                        S.op("pe", lambda hh=hh, oc=oc, i=i, hc=hc: PE.matmul(PS[5][0:n, oc + i * 64:oc + i * 64 + 64], lhsT=AMK[0:n, hh, 1, 0:n], rhs=VT[0:n, hc:hc + 64],
                                                                             start=False, stop=True), r=[tAMK, tVT], w=[tPS[5]], sig=(jj == 1 and i == 1))
                for jj in range(2):
                    j = 2 * g + jj
                    S.op("pe", lambda j=j, jj=jj: PE.matmul(PS[6][:, jj * 128:(jj + 1) * 128], lhsT=BTT[0:n, 0, j * 128:(j + 1) * 128], rhs=UB[0:n, jj * 128:(jj + 1) * 128], start=True, stop=False),
                         r=[tBTT, tUB], w=[tPS[6]], sig=(False))
                    S.op("pe", lambda j=j, jj=jj: PE.matmul(PS[6][:, jj * 128:(jj + 1) * 128], lhsT=BTT[0:n, 1, j * 128:(j + 1) * 128], rhs=VT[0:n, j * 128:(j + 1) * 128], start=False, stop=True),
                         r=[tBTT, tVT], w=[tPS[6]], sig=(jj == 1))
                for i in range(2):
                    ps_ = slice(64 * i, 64 * i + 64)
                    fs_ = slice(64 * i, 64 * i + 64)
                    stv = ST[ps_, 2 * g:2 * g + 2, fs_]
                    psv = b3(PS[6][:, 0:256], 2)[ps_, :, fs_]
                    pcv = EP[ps_, 2 * g:2 * g + 2, n - 1:n].broadcast_to([64, 2, 64])
                    if has_state:
                        S.op("dve", lambda stv=stv, psv=psv: V.tensor_tensor(out=stv, in0=stv, in1=psv, op=ALU.add), r=[tST, tPS[6]], w=[tST])
                        S.op("dve", lambda stv=stv, pcv=pcv: V.tensor_tensor(out=stv, in0=stv, in1=pcv, op=ALU.mult), r=[tST, tEP], w=[tST])
                    else:
                        S.op("dve", lambda stv=stv, psv=psv, pcv=pcv: V.tensor_tensor(out=stv, in0=psv, in1=pcv, op=ALU.mult), r=[tPS[6], tEP], w=[tST])
            S.op("act", lambda: ACT.copy(out=STB[:], in_=ST[:]), r=[tST], w=[tSTB])
            drain(1000)
            S.op("dve", lambda: V.tensor_tensor(out=CV[:, :, 0:n], in0=CVA[:, :, 0:n], in1=CVBf[:, :, 0:n], op=ALU.add), r=tCVa + tCVb, w=[tCV, tSW] + tCVa + tCVb)
            S.op("act", lambda: ACT.copy(out=TMPH[:, :, :], in_=UH[:, :, n:n + 30]), r=[tUH], w=[tTMPH])
            S.op("act", lambda: ACT.copy(out=UH[:, :, 0:30], in_=TMPH[:, :, :]), r=[tTMPH], w=[tUH])
            assert not side_units, len(side_units)
            emit_unit(side_tail)
            for j in range(4):
                S.op("pe", lambda j=j: PE.matmul(PS[6][0:n, 2 * j:2 * j + 2], lhsT=T2B[:, j, 0:n], rhs=HIND[:, :], start=True, stop=True), r=[tT2B, tC], w=[tPS[6]], sig=(j == 3))
            S.op("dve", lambda: V.tensor_copy(out=STAT[0:n, 5, :], in_=PS[6][0:n, 0:8]), r=[tPS[6]], w=[tSTAT])
            S.op("pe", lambda: PE.matmul(PS7[0:n, :], lhsT=SG[:, 0, 0:n], rhs=LG[:, 0, :], start=True, stop=False), r=[tSG, tLG], w=[tPSB], sig=(False))
            S.op("pe", lambda: PE.matmul(PS7[0:n, :], lhsT=SG[:, 1, 0:n], rhs=LG[:, 1, :], start=False, stop=True), r=[tSG, tLG], w=[tPSB], sig=(True))

        def back(l, c0, n):
            S.rec = []
            o3 = b3(PS[5][:, :], 8)[0:n]
            S.op("act", lambda: ACT.activation(out=OSQ[0:n, :], in_=PS[5][0:n, :], func=AF.Square), r=[tPS[5]], w=[tOSQ])
            S.op("dve", lambda: V.tensor_reduce(out=STAT[0:n, 0, :], in_=o3, axis=AX.X, op=ALU.add), r=[tPS[5]], w=[tSTAT])
            S.op("dve", lambda: V.tensor_reduce(out=STAT[0:n, 1, :], in_=b3(OSQ[:, :], 8)[0:n], axis=AX.X, op=ALU.add), r=[tOSQ], w=[tSTAT])
            S.op("dve", lambda: V.tensor_scalar(out=STAT[0:n, 0, :], in0=STAT[0:n, 0, :], scalar1=1.0 / 64.0, scalar2=None, op0=ALU.mult), r=[tSTAT], w=[tSTAT])
            S.op("dve", lambda: V.tensor_tensor(out=STAT[0:n, 2, :], in0=STAT[0:n, 0, :], in1=STAT[0:n, 0, :], op=ALU.mult), r=[tSTAT], w=[tSTAT])
            S.op("dve", lambda: V.scalar_tensor_tensor(out=STAT[0:n, 3, :], in0=STAT[0:n, 1, :], scalar=1.0 / 64.0, in1=STAT[0:n, 2, :], op0=ALU.mult, op1=ALU.subtract),
                 r=[tSTAT], w=[tSTAT])
            S.op("act", lambda: ACT.activation(out=STAT[0:n, 4, :], in_=STAT[0:n, 3, :], func=AF.Ln, bias=CEPS[0:n, 1:2], scale=1.0), r=[tSTAT, tC], w=[tSTAT])
            S.op("act", lambda: ACT.activation(out=STAT[0:n, 4, :], in_=STAT[0:n, 4, :], func=AF.Exp, scale=-0.5), r=[tSTAT], w=[tSTAT])
            ot3 = b3(OTM[:, :], 8)[0:n]
            S.op("dve", lambda: V.tensor_tensor(out=ot3, in0=o3, in1=STAT[0:n, 0, :].unsqueeze(2).broadcast_to([n, 8, 64]), op=ALU.subtract), r=[tPS[5], tSTAT], w=[tOTM])
            S.op("dve", lambda: V.tensor_tensor(out=ot3, in0=ot3, in1=STAT[0:n, 4, :].unsqueeze(2).broadcast_to([n, 8, 64]), op=ALU.mult), r=[tOTM, tSTAT], w=[tOTM])
            S.op("dve", lambda: V.tensor_tensor(out=OTM[0:n, :], in0=OTM[0:n, :], in1=GNG[0:n, :], op=ALU.mult), r=[tOTM, tGN], w=[tOTM])
            S.op("dve", lambda: V.tensor_tensor(out=OTM[0:n, :], in0=OTM[0:n, :], in1=GNB[0:n, :], op=ALU.add), r=[tOTM, tGN], w=[tOTM])
            S.op("dve", lambda: V.tensor_tensor(out=b3(OSQ[:, :], 8)[0:n], in0=b3(VT[:, :], 8)[0:n], in1=STAT[0:n, 5, :].unsqueeze(2).broadcast_to([n, 8, 64]), op=ALU.mult),
                 r=[tVT, tSTAT], w=[tOSQ])
            S.op("dve", lambda: V.tensor_tensor(out=OTM[0:n, :], in0=OTM[0:n, :], in1=OSQ[0:n, :], op=ALU.add), r=[tOTM, tOSQ], w=[tOTM])
            S.op("dve", lambda: V.tensor_tensor(out=YRT[0:n, :], in0=OTM[0:n, :], in1=PS7[0:n, :], op=ALU.mult), r=[tOTM, tPSB], w=[tYRT])
            for j in range(4):
                S.op("pe", lambda j=j: PE.transpose(out=PSB[:, j * 128:j * 128 + n], in_=YRT[0:n, j * 128:(j + 1) * 128], identity=IDB[0:n, 0:n]), r=[tYRT, tC], w=[tPSB], sig=(j == 3))
            S.op("act", lambda: ACT.copy(out=MIX[:, 0:4, 0:n], in_=b3(PSB[:, 0:512], 4)[:, :, 0:n]), r=[tPSB], w=[tMIX])
            listA = S.rec
            S.rec = []
            S.op("act", lambda: ACT.activation(out=CSQ[:, :, 0:n], in_=CV[:, :, 0:n], func=AF.Square), r=[tCV], w=[tCSQ])
            for j in range(4):
                S.op("pe", lambda j=j: PE.matmul(PS[0][:, 0:n], lhsT=ONF[:], rhs=CV[:, j, 0:n], start=(j == 0), stop=(j == 3)), r=[tCV, tC], w=[tPS[0]], sig=(j == 3))
            for j in range(4):
                S.op("pe", lambda j=j: PE.matmul(PS[1][:, 0:n], lhsT=ONF[:], rhs=CSQ[:, j, 0:n], start=(j == 0), stop=(j == 3)), r=[tCSQ, tC], w=[tPS[1]], sig=(j == 3))
            S.op("act", lambda: ACT.copy(out=LNS[:, 0, 0:n], in_=PS[0][:, 0:n]), r=[tPS[0]], w=[tLNS])
            S.op("dve", lambda: V.tensor_tensor(out=LNS[:, 1, 0:n], in0=LNS[:, 0, 0:n], in1=LNS[:, 0, 0:n], op=ALU.mult), r=[tLNS], w=[tLNS])
            S.op("dve", lambda: V.tensor_tensor(out=LNS[:, 1, 0:n], in0=PS[1][:, 0:n], in1=LNS[:, 1, 0:n], op=ALU.subtract), r=[tLNS, tPS[1]], w=[tLNS])
            S.op("act", lambda: ACT.activation(out=LNS[:, 2, 0:n], in_=LNS[:, 1, 0:n], func=AF.Ln, bias=CEPS[:, 2:3], scale=1.0), r=[tLNS, tC], w=[tLNS])
            S.op("act", lambda: ACT.activation(out=LNS[:, 2, 0:n], in_=LNS[:, 2, 0:n], func=AF.Exp, scale=-0.5), r=[tLNS], w=[tLNS])
            S.op("dve", lambda: V.tensor_tensor(out=CV[:, :, 0:n], in0=CV[:, :, 0:n], in1=LNS[:, 0, 0:n].unsqueeze(1).broadcast_to([128, 4, n]), op=ALU.subtract), r=[tCV, tLNS], w=[tCV])
            S.op("dve", lambda: V.tensor_tensor(out=CV[:, :, 0:n], in0=CV[:, :, 0:n], in1=LNS[:, 2, 0:n].unsqueeze(1).broadcast_to([128, 4, n]), op=ALU.mult), r=[tCV, tLNS], w=[tCV])
            for j in range(4):
                S.op("act", lambda j=j: ACT.activation(out=MIX[:, 4 + j, 0:n], in_=CV[:, j, 0:n], func=AF.Silu, bias=VEC[:, V_CBB + j:V_CBB + j + 1], scale=VEC[:, V_CG + j:V_CG + j + 1]),
                     r=[tCV, tVEC], w=[tMIX])
            listB = S.rec
            S.rec = None
            ia = ib = 0
            while ia < len(listA) or ib < len(listB):
                if ia < len(listA):
                    e_, fn_, r_, w_, sig_ = listA[ia]; ia += 1
                    S.op(e_, fn_, r=r_, w=w_, sig=sig_)
                    while ia < len(listA) and listA[ia][0] == "pe" and e_ == "pe":
                        e_, fn_, r_, w_, sig_ = listA[ia]; ia += 1
                        S.op(e_, fn_, r=r_, w=w_, sig=sig_)
                if ib < len(listB):
                    e_, fn_, r_, w_, sig_ = listB[ib]; ib += 1
                    S.op(e_, fn_, r=r_, w=w_, sig=sig_)
                    while ib < len(listB) and listB[ib][0] == "pe" and e_ == "pe":
                        e_, fn_, r_, w_, sig_ = listB[ib]; ib += 1
                        S.op(e_, fn_, r=r_, w=w_, sig=sig_)
            for m in range(8):
                bank = m // 4
                sl = m % 4
                for kc in range(8):
                    S.op("pe", lambda m=m, bank=bank, sl=sl, kc=kc: PE.matmul(PS[bank][:, sl * 128:sl * 128 + n], lhsT=WOUT[:, kc, m * 128:(m + 1) * 128], rhs=MIX[:, kc, 0:n],
                                                                            start=(kc == 0), stop=(kc == 7)), r=[tWOUT, tMIX], w=[tPS[bank]], sig=(kc == 7 and m % 4 == 3))
            for bank in range(2):
                hv = H[:, bank * 4:bank * 4 + 4, c0:c0 + n]
                S.op("dve", lambda bank=bank, hv=hv: V.tensor_tensor(out=hv, in0=hv, in1=b3(PS[bank][:, :], 4)[:, :, 0:n], op=ALU.add),
                     r=htks(c0, n) + [tPS[bank]], w=htks(c0, n))

        def phaseA_tile(l, c0, n, has_state):
            front(l, c0, n)
            mid(l, c0, n, has_state)
            back(l, c0, n)

        out_toks = []

        def emit_states(l, wkv_dst, sh_dst, cv_dst, n):
            with nc.allow_non_contiguous_dma(reason="small state vectors"):
                out_toks.append(S.dma("sp", lambda e: e.dma_start(out=sh_dst[0:1792].rearrange("(b p) -> p b", p=128), in_=P[:, 0:14, 0]), r=[tP]))
                out_toks.append(S.dma("sp", lambda e: e.dma_start(out=sh_dst[1792:1824].rearrange("(b p) -> p b", p=32), in_=P[0:32, 14:15, 0]), r=[tP]))
            for j in range(4):
                S.op("pe", lambda j=j: PE.transpose(out=PS[0][0:30, j * 128:(j + 1) * 128], in_=UH[:, j, 0:30], identity=IDF[:]), r=[tUH, tC], w=[tPS[0]], sig=(j == 3))
            S.op("act", lambda: ACT.copy(out=IOT[0:30, 0:512], in_=PS[0][0:30, :]), r=[tPS[0]], w=[tIOT])
            out_toks.append(S.dma("sp", lambda e: e.dma_start(out=cv_dst, in_=IOT[0:30, 0:512]), r=[tIOT]))
            for j in range(4):
                S.op("pe", lambda j=j: PE.transpose(out=PS[1][:, j * 128:(j + 1) * 128], in_=ST[:, j, :], identity=IDF[:]), r=[tST, tC], w=[tPS[1]], sig=(j == 3))
            S.op("act", lambda: ACT.copy(out=OSQ[:, :], in_=PS[1][:, :]), r=[tPS[1]], w=[tOSQ])
            for i in range(2):
                out_toks.append(S.dma("sp", lambda e, i=i: e.dma_start(out=wkv_dst.rearrange("(j i) v k -> i v j k", i=2)[i], in_=b3(OSQ[:, :], 4)[64 * i:64 * i + 64, :, 64 * i:64 * i + 64]),
                                      r=[tOSQ]))

        def load_sample_state(l, j):
            S.op("pool", lambda: POOL.memset(OSQ[:], 0.0), w=[tOSQ])
            for i in range(2):
                S.dma("sp", lambda e, i=i: e.dma_start(out=b3(OSQ[:, :], 4)[64 * i:64 * i + 64, :, 64 * i:64 * i + 64], in_=swkv[l, j].rearrange("(j i) v k -> i v j k", i=2)[i]), w=[tOSQ])
            for jp in range(4):
                S.op("pe", lambda jp=jp: PE.transpose(out=PS[1][:, jp * 128:(jp + 1) * 128], in_=OSQ[:, jp * 128:(jp + 1) * 128], identity=IDF[:]), r=[tOSQ, tC], w=[tPS[1]], sig=(jp == 3))
            S.op("act", lambda: ACT.copy(out=ST[:], in_=b3(PS[1][:, :], 4)), r=[tPS[1]], w=[tST])
            S.op("act", lambda: ACT.copy(out=STB[:], in_=ST[:]), r=[tST], w=[tSTB])
            with nc.allow_non_contiguous_dma(reason="small state vectors"):
                S.dma("sp", lambda e: e.dma_start(out=P[:, 0:14, 0], in_=sshift[l, j, 0:1792].rearrange("(b p) -> p b", p=128)), w=[tP])
                S.dma("sp", lambda e: e.dma_start(out=P[0:32, 14:15, 0], in_=sshift[l, j, 1792:1824].rearrange("(b p) -> p b", p=32)), w=[tP])
            S.dma("sp", lambda e: e.dma_start(out=IOT[0:30, 0:512], in_=sconv[l, j]), w=[tIOT])
            for jb in range(4):
                S.op("pe", lambda jb=jb: PE.transpose(out=PS[0][:, jb * 128:jb * 128 + 30], in_=IOT[0:30, jb * 128:(jb + 1) * 128], identity=IDF[0:30, 0:30]), r=[tIOT, tC], w=[tPS[0]], sig=(jb == 3))
            S.op("act", lambda: ACT.copy(out=UH[:, :, 0:30], in_=b3(PS[0][:, :], 4)[:, :, 0:30]), r=[tPS[0]], w=[tUH])

        def barrier_all():
            toks = []
            for e in ("pe", "dve", "act", "pool"):
                c = S.cnt[e]
                if c > 0:
                    ep, v = divmod(c - 1, EPOCH)
                    toks.append(((e, ep), v + 1))
            for i in range(NDSEM):
                if S.dcnt[i] > 0:
                    toks.append((("d", i), S.dcnt[i]))
            for e in ("pe", "dve", "act", "pool", "sp"):
                for tok in toks:
                    if tok[0][0] == e:
                        continue
                    S._wait(e, tok)

        def load_layer_weights_old(l):
            for kc in range(8):
                S.dma("pool", lambda e, kc=kc: e.dma_start(out=WIN[:, kc, :], in_=w_in[l, kc * 128:(kc + 1) * 128, :], max_dma_last_dim=4096), w=[tWIN])
            for kc in range(8):
                S.dma("pool", lambda e, kc=kc: e.dma_start(out=WOUT[:, kc, :], in_=w_out[l, kc * 128:(kc + 1) * 128, :], max_dma_last_dim=4096), w=[tWOUT])

        for l in range(DEPTH if STAGE >= 1 else 0):
            if l == 0:
                while w0_pieces:
                    w0_step()
                cur_ring[0] = RINGB
            barrier_all()
            for buf, tk_ in ((ATZ, tATZ), (BTZ, tBTZ), (KTZ, tKTZ), (SG, tSG), (P, tP), (LG, tLG)):
                S.op("pool", (lambda b=buf: POOL.memset(b[:], 0.0)), w=[tk_])
            S.dma("sp", lambda e: e.dma_start(out=VEC[:], in_=vecs[l]), w=[tVEC])
            for j in range(4):
                for ti in range(NPE):
                    S.op("dve", lambda j=j, ti=ti: V.tensor_scalar(out=DG[:, j * NPE + ti, :], in0=IDB[:], scalar1=VEC[:, V_CW + j * CW + ti:V_CW + j * CW + ti + 1], scalar2=None, op0=ALU.mult), r=[tVEC, tC], w=[tDG])
            stream_cast(RINGA, LW[0:64, :], lora_w[l], tLW, 0, 64)
            stream_cast(RINGA, LA[64:128, :], lora_a[l], tLA, 64, 128)
            stream_cast(RINGA, LG[:, 0, :], lora_g[l, 0:128, :], tLG, 0, 128)
            stream_cast(RINGA, LG[0:32, 1, :], lora_g[l, 128:160, :], tLG, 0, 32)
            S.dma("sp", lambda e: e.dma_start(out=GNG[:], in_=gn_g[l:l + 1, :].partition_broadcast(128)), w=[tGN])
            S.dma("sp", lambda e: e.dma_start(out=GNB[:], in_=gn_b[l:l + 1, :].partition_broadcast(128)), w=[tGN])
            S.op("pool", lambda: POOL.memset(ST[:], 0.0), w=[tST])
            S.op("pool", lambda: POOL.memset(STB[:], 0.0), w=[tSTB])
            S.op("pool", lambda: POOL.memset(P[:, :, 0:1], 0.0), w=[tP])
            S.op("pool", lambda: POOL.memset(UH[:, :, 0:30], 0.0), w=[tUH])
            KONLY = os.environ.get("KONLY", "")
            if KONLY != "128":
                phaseA_tile(l, 0, 16, False)
            front(l, 16, 128)
            for i in range(16):
                mid(l, 16 + i * 128, 128, True, side=(front_units(l, 16 + (i + 1) * 128, 128) if i < 15 else None))
                back(l, 16 + i * 128, 128)
            if STAGE >= 10:
                emit_states(l, wkvp[l], shp[l], cvp[l], 128)
            for j in range(NS if KONLY == "" else 0):
                load_sample_state(l, j)
                phaseA_tile(l, NMETA + NPT + j * 16, 16, True)
                if STAGE >= 10:
                    emit_states(l, wkvs[l, j], shs[l, j], cvs[l, j], 16)
            barrier_all()
            if STAGE >= 11:
                SQ2 = WORK[:, 26240:26240 + 1024].rearrange("p (a b) -> p a b", a=8)
                RS2 = WORK[:, 26240 + 1024:26240 + 1024 + 256].bitcast(F32)
                tSQ2, tRS2 = Tk(), Tk()
                R32b = WORK[:, 26240:26240 + 1024].bitcast(F32)
                assert 26240 + 1024 + 256 <= WORK_E
                c = 0
                kalt = 0
                while c < TT:
                    n = min(128, TT - c)
                    if kalt % 2 == 0:
                        rmsnorm_tile(c, n, lambda cc: VEC[:, V_NF + cc:V_NF + cc + 1], lambda cc, c=c, n=n: HN[:, cc, c:c + n], tHN)
                    else:
                        rmsnorm_tile(c, n, lambda cc: VEC[:, V_NF + cc:V_NF + cc + 1], lambda cc, c=c, n=n: HN[:, cc, c:c + n], tHN,
                                     SQ2, tSQ2, RS2, tRS2, PS7, tPSB)
                    kalt += 1
                    c += n
                def load_slot(e8_):
                    sl_ = e8_ % 2
                    for kc in range(8):
                        stream_cast(RINGB, WU[sl_][:, kc, :], w_up[l, kc * 128:(kc + 1) * 128, e8_ * 512:(e8_ + 1) * 512], tWU[sl_])
                    for fc in range(4):
                        r0 = e8_ * 512 + fc * 128
                        stream_cast(RINGB, WDf[sl_][fc], w_down[l, r0:r0 + 128, :], tWD[sl_])

                tiles_ = [(0, 432), (432, 432), (864, 432), (1296, 432), (1728, TT - 1728)]
                assert sum(N for _, N in tiles_) == TT and all(N <= 512 and t0 % 16 == 0 for t0, N in tiles_)
                items = [(e8, t0, N) for e8 in range(8) for (t0, N) in tiles_]
                UPB = [0, 1, 2, 3, 6, 7]

                def bk(i):
                    return (PS[i], tPS[i]) if i < 7 else (PS7, tPSB)

                def emit_up(idx):
                    e8, t0, N = items[idx]
                    sl = e8 % 2
                    for fc in range(4):
                        pb_, tpb_ = bk(UPB[(4 * idx + fc) % 6])
                        for kc in range(8):
                            S.op("pe", lambda fc=fc, kc=kc, sl=sl, t0=t0, N=N, pb_=pb_: PE.matmul(pb_[:, 0:N], lhsT=WU[sl][:, kc, fc * 128:(fc + 1) * 128], rhs=HN[:, kc, t0:t0 + N],
                                                                                                  start=(kc == 0), stop=(kc == 7)), r=[tWU[sl], tHN], w=[tpb_], sig=(kc == 7))

                def emit_act(idx):
                    e8, t0, N = items[idx]
                    for fc in range(4):
                        pb_, tpb_ = bk(UPB[(4 * idx + fc) % 6])
                        rb_, trb_ = (R32, tR32) if fc % 2 == 0 else (R32b, tSQ2)
                        S.op("act", lambda N=N, pb_=pb_, rb_=rb_: ACT.activation(out=rb_[:, 0:N], in_=pb_[:, 0:N], func=AF.Relu), r=[tpb_], w=[trb_])
                        S.op("dve", lambda fc=fc, N=N, pb_=pb_, rb_=rb_: V.tensor_tensor(out=A2B[:, fc, 0:N], in0=rb_[:, 0:N], in1=pb_[:, 0:N], op=ALU.mult), r=[trb_, tpb_], w=[tA2B])

                def emit_down(idx):
                    e8, t0, N = items[idx]
                    sl = e8 % 2
                    for m in range(8):
                        pb = 4 + (m % 2)
                        for fc in range(4):
                            S.op("pe", lambda m=m, fc=fc, sl=sl, pb=pb, N=N: PE.matmul(PS[pb][:, 0:N], lhsT=WDf[sl][fc][:, m * 128:(m + 1) * 128], rhs=A2B[:, fc, 0:N],
                                                                                     start=(fc == 0), stop=(fc == 3)), r=[tWD[sl], tA2B], w=[tPS[pb]], sig=(fc == 3))
                        hv = H[:, m, t0:t0 + N]
                        S.op("dve", lambda hv=hv, pb=pb, N=N: V.tensor_tensor(out=hv, in0=hv, in1=PS[pb][:, 0:N], op=ALU.add),
                             r=htks(t0, N) + [tPS[pb]], w=htks(t0, N))

                load_slot(0)
                load_slot(1)
                pf_pieces = layer_weight_pieces(l + 1) if l + 1 < DEPTH else []
                pf_pieces.reverse()
                emit_up(0)
                emit_act(0)
                for idx in range(len(items)):
                    if idx + 1 < len(items):
                        emit_up(idx + 1)
                    emit_down(idx)
                    if idx + 1 < len(items):
                        emit_act(idx + 1)
                    e8 = items[idx][0]
                    if idx + 1 == len(items) or items[idx + 1][0] != e8:
                        if e8 + 2 < 8:
                            load_slot(e8 + 2)
                        for _ in range(3):
                            if pf_pieces:
                                pf_pieces.pop()()
                while pf_pieces:
                    pf_pieces.pop()()
        barrier_all()
        XFs = [REG[:, k * 2048:(k + 1) * 2048].bitcast(F32).rearrange("p (k n) -> p k n", k=8) for k in range(2)]
        tXFs = [Tk(), Tk()]
        IOTs = [REG[:, 4096 + k * 2048:4096 + (k + 1) * 2048].bitcast(F32) for k in range(2)]
        tIOTs = [Tk(), Tk()]
        fcount = [0]

        def final_tile(c0, n, dst):
            k_ = fcount[0] % 2
            fcount[0] += 1
            XF, tXF, IOT, tIOT = XFs[k_], tXFs[k_], IOTs[k_], tIOTs[k_]
            rmsnorm_tile(c0, n, lambda c: NFIN[:, c:c + 1], lambda c: XF[:, c, 0:n], tXF)
            for half in range(2):
                for c in range(4):
                    cc = half * 4 + c
                    S.op("pe", lambda half=half, c=c, cc=cc: PE.transpose(out=PS[half][0:n, c * 128:(c + 1) * 128], in_=XF[:, cc, 0:n], identity=IDF[:]), r=[tXF, tC], w=[tPS[half]], sig=(c == 3))
                S.op("act", lambda half=half: ACT.copy(out=IOT[0:n, half * 512:(half + 1) * 512], in_=PS[half][0:n, :]), r=[tPS[half]], w=[tIOT])
            out_toks.append(S.dma("sp", lambda e: e.dma_start(out=dst, in_=IOT[0:n, :]), r=[tIOT]))

        for i in range(16):
            final_tile(16 + i * 128, 128, yp[i * 128:(i + 1) * 128, :])
        for j in range(NS):
            final_tile(NMETA + NPT + j * 16, 16, ys[j * 16:(j + 1) * 16, :])
        S.final_wait(out_toks)
    return nc


_NC_CACHE = {}


def _pack_vecs(inp):
    f = np.float32
    out = np.zeros((DEPTH, 128, NV), f)

    def fm(v, ncol):
        return v.reshape(DEPTH, ncol, 128).transpose(0, 2, 1)

    out[:, :, V_NM:V_NM + 8] = fm(inp["norm_mix"], 8)
    mu = np.zeros((DEPTH, 15 * 128), f)
    mu[:, :PR] = inp["mu_shift"]
    out[:, :, V_MU:V_MU + 15] = fm(mu, 15)
    out[:, :, V_W0:V_W0 + 4] = fm(inp["w0"], 4)
    out[:, :, V_A0:V_A0 + 4] = fm(inp["a0"], 4)
    out[:, :, V_KK:V_KK + 4] = fm(inp["k_k"], 4)
    out[:, :, V_KA:V_KA + 4] = fm(inp["k_a"], 4)
    out[:, :, V_RK:V_RK + 4] = fm(inp["r_k"].reshape(DEPTH, 512), 4)
    cw = inp["conv_w"].reshape(DEPTH, CW, 4, 128).transpose(0, 3, 2, 1)
    out[:, :, V_CW:V_CW + 4 * CW] = cw.reshape(DEPTH, 128, 4 * CW)
    out[:, :, V_CB:V_CB + 4] = fm(inp["conv_b"], 4)
    out[:, :, V_CG:V_CG + 4] = fm(inp["cln_g"], 4)
    out[:, :, V_CBB:V_CBB + 4] = fm(inp["cln_b"], 4)
    out[:, :, V_NF:V_NF + 8] = fm(inp["norm_ffn"], 8)
    return np.ascontiguousarray(out)


def kernel(**inputs):
    inp = {k: np.asarray(v) for k, v in inputs.items()}
    if "nc" not in _NC_CACHE:
        _NC_CACHE["nc"] = build_program()
    nc = _NC_CACHE["nc"]
    vecs = _pack_vecs(inp)
    nfin = np.ascontiguousarray(inp["norm_final"].reshape(8, 128).T.astype(np.float32))
    shared = {
        "meta": np.ascontiguousarray(inp["meta_tokens"]), "vecs": vecs, "nfin": nfin,
        "w_in": inp["w_in"], "lora_w": inp["lora_w"], "lora_a": inp["lora_a"], "lora_g": inp["lora_g"],
        "gn_g": inp["gn_g"], "gn_b": inp["gn_b"], "w_out": inp["w_out"], "w_up": inp["w_up"], "w_down": inp["w_down"],
    }
    in_maps = []
    for c in range(8):
        m = dict(shared)
        m["xp"] = np.ascontiguousarray(inp["x_prompt"][c])
        m["xs"] = np.ascontiguousarray(inp["x_sample"][4 * c:4 * c + 4].reshape(NS * TS, D))
        m["swkv"] = np.ascontiguousarray(inp["state_wkv"][:, 4 * c:4 * c + 4])
        m["sshift"] = np.ascontiguousarray(inp["state_shift"][:, 4 * c:4 * c + 4])
        m["sconv"] = np.ascontiguousarray(inp["cache_conv"][:, 4 * c:4 * c + 4])
        in_maps.append(m)
    res = run_bass_kernel_spmd(nc, in_maps, core_ids=list(range(8)))
    rs = res.results
    y_prompt = np.stack([r["yp"] for r in rs], 0).astype(np.float32)
    y_sample = np.concatenate([r["ys"].reshape(NS, TS, D) for r in rs], 0).astype(np.float32)
    wkv_p = np.stack([r["wkvp"] for r in rs], 1).astype(np.float32)
    sh_p = np.stack([r["shp"] for r in rs], 1).astype(np.float32)
    cv_p = np.stack([r["cvp"] for r in rs], 1).astype(np.float32)
    wkv_s = np.concatenate([r["wkvs"] for r in rs], 1).astype(np.float32)
    sh_s = np.concatenate([r["shs"] for r in rs], 1).astype(np.float32)
    cv_s = np.concatenate([r["cvs"] for r in rs], 1).astype(np.float32)
    return (y_prompt, y_sample, wkv_p, sh_p, cv_p, wkv_s, sh_s, cv_s)
```
